# Optimizing a Trainium2 kernel written in Bass

```python
import math
import jax, jax.numpy as jnp
from jax import lax
import numpy as np

D_MODEL = 4096
BATCH = 4
SEQ = 4096
DEPTH = 2

GRID_W = 64
HEAD_DIM = 128
ROPE_THETA = 10000.0
Q_BLOCK = 128
LN_EPS = 1e-5
RMS_EPS = 1e-6

NA_HEADS = 8
NA_WIN_ROWS = 8
NA_WIN_COLS = 16
WIDTH_A = NA_HEADS * HEAD_DIM

MLA_HEADS = 8
MLA_Q_RANK = 1536
MLA_KV_RANK = 512
MLA_NOPE = 128
MLA_ROPE = 64
MLA_V = 128
WIDTH_B = MLA_HEADS * MLA_V

DIFF_HEADS = 8
DIFF_QK = 64
DIFF_V = 2 * DIFF_QK
WIDTH_C = DIFF_HEADS * DIFF_V

N_BRANCH = 3
DEEPNORM_ALPHA = (2.0 * DEPTH) ** 0.25
DEEPNORM_BETA = (8.0 * DEPTH) ** -0.25

IN_SPLITS = (
    WIDTH_A, WIDTH_A, WIDTH_A, WIDTH_A,
    MLA_Q_RANK, MLA_KV_RANK, MLA_ROPE, WIDTH_B,
    DIFF_HEADS * 2 * DIFF_QK, DIFF_HEADS * 2 * DIFF_QK, WIDTH_C, WIDTH_C,
    N_BRANCH * D_MODEL,
)
IN_WIDTH = sum(IN_SPLITS)

kernel_name = "hybrid_natten_mla_diffattn_encoder"


def _layer_norm(x, g, b):
    xf = x.astype(jnp.float32)
    mu = jnp.mean(xf, axis=-1, keepdims=True)
    var = jnp.mean(jnp.square(xf - mu), axis=-1, keepdims=True)
    return ((xf - mu) * lax.rsqrt(var + LN_EPS) * g + b).astype(x.dtype)


def _rms_norm(x, g):
    xf = x.astype(jnp.float32)
    return (xf * lax.rsqrt(jnp.mean(xf * xf, axis=-1, keepdims=True) + RMS_EPS) * g).astype(x.dtype)


def _rope(x, pos):
    d = x.shape[-1]
    half = d // 2
    inv_freq = ROPE_THETA ** (-jnp.arange(half, dtype=jnp.float32) * 2.0 / d)
    ang = pos.astype(jnp.float32)[:, None] * inv_freq[None, :]
    cos = jnp.cos(ang)[None, :, None, :]
    sin = jnp.sin(ang)[None, :, None, :]
    x1 = x[..., :half].astype(jnp.float32)
    x2 = x[..., half:].astype(jnp.float32)
    return jnp.concatenate([x1 * cos - x2 * sin, x2 * cos + x1 * sin], axis=-1).astype(x.dtype)


def _split_cols(h):
    points = np.cumsum(np.array(IN_SPLITS))[:-1].tolist()
    return jnp.split(h, points, axis=-1)


def _to_blocks(t):
    b, s = t.shape[0], t.shape[1]
    t = t.reshape((b, s // Q_BLOCK, Q_BLOCK) + t.shape[2:])
    return jnp.moveaxis(t, 1, 0)


def _from_blocks(t):
    t = jnp.moveaxis(t, 0, 1)
    return t.reshape((t.shape[0], t.shape[1] * t.shape[2]) + t.shape[3:])


def _dense_attention(q, k, v, scale):
    def one_block(qb):
        sc = jnp.einsum('bqhd,bkhd->bhqk', qb, k, preferred_element_type=jnp.float32) * scale
        p = jax.nn.softmax(sc, axis=-1)
        return jnp.einsum('bhqk,bkhd->bqhd', p.astype(v.dtype), v)
    return _from_blocks(lax.map(one_block, _to_blocks(q)))


def _diff_attention(q1, k1, q2, k2, v, lam, scale):
    def one_block(qs):
        q1b, q2b = qs
        p1 = jax.nn.softmax(jnp.einsum('bqhd,bkhd->bhqk', q1b, k1, preferred_element_type=jnp.float32) * scale, axis=-1)
        p2 = jax.nn.softmax(jnp.einsum('bqhd,bkhd->bhqk', q2b, k2, preferred_element_type=jnp.float32) * scale, axis=-1)
        p = (p1 - lam * p2).astype(v.dtype)
        return jnp.einsum('bhqk,bkhd->bqhd', p, v)
    return _from_blocks(lax.map(one_block, (_to_blocks(q1), _to_blocks(q2))))


def _neighbourhood_attention(q, k, v, rpb):
    b, s, h, d = q.shape
    rows = s // GRID_W
    kh = min(NA_WIN_ROWS, rows)
    kw = NA_WIN_COLS
    qg = q.reshape(b, rows, GRID_W, h, d)
    kg = k.reshape(b, rows, GRID_W, h, d)
    vg = v.reshape(b, rows, GRID_W, h, d)
    r = jnp.arange(rows)
    row_start = jnp.clip(r - kh // 2, 0, rows - kh)
    row_idx = row_start[:, None] + jnp.arange(kh)[None, :]
    k_rows = kg[:, row_idx]
    v_rows = vg[:, row_idx]
    c = jnp.arange(GRID_W)
    col_start = jnp.clip(c - kw // 2, 0, GRID_W - kw)
    col_in = (c[None, :] >= col_start[:, None]) & (c[None, :] < col_start[:, None] + kw)
    dr = row_idx - r[:, None] + (NA_WIN_ROWS - 1)
    dc = jnp.clip(c[None, :] - c[:, None] + (kw - 1), 0, 2 * kw - 2)
    bias = rpb[:, dr[:, None, :, None], dc[None, :, None, :]]
    sc = jnp.einsum('brqhd,brikhd->bhrqik', qg, k_rows, preferred_element_type=jnp.float32) * (d ** -0.5)
    sc = sc + bias[None].astype(jnp.float32)
    sc = jnp.where(col_in[:, None, :], sc, -jnp.inf)
    p = jax.nn.softmax(sc.reshape(b, h, rows, GRID_W, kh * GRID_W), axis=-1)
    p = p.reshape(b, h, rows, GRID_W, kh, GRID_W).astype(v.dtype)
    out = jnp.einsum('bhrqik,brikhd->brqhd', p, v_rows)
    return out.reshape(b, s, h, d)


def _hybrid_layer(x, layer_idx, w_in, w_uq, q_norm, w_ukv, kv_norm, na_rpb,
                  lam_q1, lam_k1, lam_q2, lam_k2, diff_subln,
                  w_o_a, w_o_b, w_o_c, b_merge, w_out, ln_g, ln_b):
    b, s, _ = x.shape
    pos = jnp.arange(s)
    h = jnp.einsum('bsd,dn->bsn', x, w_in)
    (a_q, a_k, a_v, a_gate,
     b_cq, b_ckv, b_krope, b_gate,
     c_q, c_k, c_v, c_gate,
     merge_logits) = _split_cols(h)

    ya = _neighbourhood_attention(a_q.reshape(b, s, NA_HEADS, HEAD_DIM),
                                  a_k.reshape(b, s, NA_HEADS, HEAD_DIM),
                                  a_v.reshape(b, s, NA_HEADS, HEAD_DIM), na_rpb)
    ya = ya.reshape(b, s, WIDTH_A) * jax.nn.silu(a_gate)

    cq = _rms_norm(b_cq, q_norm)
    qb = jnp.einsum('bsr,rn->bsn', cq, w_uq).reshape(b, s, MLA_HEADS, MLA_NOPE + MLA_ROPE)
    qb = jnp.concatenate([qb[..., :MLA_NOPE], _rope(qb[..., MLA_NOPE:], pos)], axis=-1)
    ckv = _rms_norm(b_ckv, kv_norm)
    kv = jnp.einsum('bsr,rn->bsn', ckv, w_ukv).reshape(b, s, MLA_HEADS, MLA_NOPE + MLA_V)
    k_nope, vb = kv[..., :MLA_NOPE], kv[..., MLA_NOPE:]
    k_rope = _rope(b_krope.reshape(b, s, 1, MLA_ROPE), pos)
    kb = jnp.concatenate([k_nope, jnp.broadcast_to(k_rope, (b, s, MLA_HEADS, MLA_ROPE))], axis=-1)
    yb = _dense_attention(qb, kb, vb, (MLA_NOPE + MLA_ROPE) ** -0.5)
    yb = yb.reshape(b, s, WIDTH_B) * jax.nn.silu(b_gate)

    qc = _rope(c_q.reshape(b, s, DIFF_HEADS * 2, DIFF_QK), pos).reshape(b, s, DIFF_HEADS, 2, DIFF_QK)
    kc = _rope(c_k.reshape(b, s, DIFF_HEADS * 2, DIFF_QK), pos).reshape(b, s, DIFF_HEADS, 2, DIFF_QK)
    vc = c_v.reshape(b, s, DIFF_HEADS, DIFF_V)
    lam_init = 0.8 - 0.6 * math.exp(-0.3 * layer_idx)
    lam = (jnp.exp(jnp.sum(lam_q1.astype(jnp.float32) * lam_k1.astype(jnp.float32)))
           - jnp.exp(jnp.sum(lam_q2.astype(jnp.float32) * lam_k2.astype(jnp.float32))) + lam_init)
    yc = _diff_attention(qc[:, :, :, 0], kc[:, :, :, 0], qc[:, :, :, 1], kc[:, :, :, 1], vc, lam, DIFF_QK ** -0.5)
    yc = _rms_norm(yc, diff_subln) * (1.0 - lam_init)
    yc = yc.reshape(b, s, WIDTH_C) * jax.nn.silu(c_gate)

    g = jax.nn.sigmoid(merge_logits + b_merge).reshape(b, s, N_BRANCH, D_MODEL)
    merged = (g[:, :, 0] * jnp.einsum('bsw,wd->bsd', ya, w_o_a)
              + g[:, :, 1] * jnp.einsum('bsw,wd->bsd', yb, w_o_b)
              + g[:, :, 2] * jnp.einsum('bsw,wd->bsd', yc, w_o_c))
    y = jnp.einsum('bsd,de->bse', merged, w_out)

    return _layer_norm(DEEPNORM_ALPHA * x + y, ln_g, ln_b)


def setup_inputs(seed: int = 0) -> dict:
    key = jax.random.key(seed)
    ks = jax.random.split(key, 20)
    f32 = jnp.float32

    def nrm(k, shape, scale):
        return jax.random.normal(k, shape, f32) * scale

    return {
        "x": nrm(ks[0], (BATCH, SEQ, D_MODEL), 1.0),
        "w_in": nrm(ks[1], (DEPTH, D_MODEL, IN_WIDTH), D_MODEL ** -0.5),
        "w_uq": nrm(ks[2], (DEPTH, MLA_Q_RANK, MLA_HEADS * (MLA_NOPE + MLA_ROPE)), MLA_Q_RANK ** -0.5),
        "q_norm": 1.0 + nrm(ks[3], (DEPTH, MLA_Q_RANK), 0.02),
        "w_ukv": nrm(ks[4], (DEPTH, MLA_KV_RANK, MLA_HEADS * (MLA_NOPE + MLA_V)), MLA_KV_RANK ** -0.5),
        "kv_norm": 1.0 + nrm(ks[5], (DEPTH, MLA_KV_RANK), 0.02),
        "na_rpb": nrm(ks[6], (DEPTH, NA_HEADS, 2 * NA_WIN_ROWS - 1, 2 * NA_WIN_COLS - 1), 0.05),
        "lam_q1": nrm(ks[7], (DEPTH, DIFF_QK), 0.1),
        "lam_k1": nrm(ks[8], (DEPTH, DIFF_QK), 0.1),
        "lam_q2": nrm(ks[9], (DEPTH, DIFF_QK), 0.1),
        "lam_k2": nrm(ks[10], (DEPTH, DIFF_QK), 0.1),
        "diff_subln": 1.0 + nrm(ks[11], (DEPTH, DIFF_V), 0.02),
        "w_o_a": nrm(ks[12], (DEPTH, WIDTH_A, D_MODEL), WIDTH_A ** -0.5 * DEEPNORM_BETA),
        "w_o_b": nrm(ks[13], (DEPTH, WIDTH_B, D_MODEL), WIDTH_B ** -0.5 * DEEPNORM_BETA),
        "w_o_c": nrm(ks[14], (DEPTH, WIDTH_C, D_MODEL), WIDTH_C ** -0.5 * DEEPNORM_BETA),
        "b_merge": nrm(ks[15], (DEPTH, N_BRANCH * D_MODEL), 0.02),
        "w_out": nrm(ks[16], (DEPTH, D_MODEL, D_MODEL), D_MODEL ** -0.5 * DEEPNORM_BETA),
        "ln_g": 1.0 + nrm(ks[17], (DEPTH, D_MODEL), 0.02),
        "ln_b": nrm(ks[18], (DEPTH, D_MODEL), 0.02),
    }


def reference(x, w_in, w_uq, q_norm, w_ukv, kv_norm, na_rpb, lam_q1, lam_k1, lam_q2, lam_k2,
              diff_subln, w_o_a, w_o_b, w_o_c, b_merge, w_out, ln_g, ln_b):
    for l in range(DEPTH):
        x = _hybrid_layer(x, l, w_in[l], w_uq[l], q_norm[l], w_ukv[l], kv_norm[l], na_rpb[l],
                          lam_q1[l], lam_k1[l], lam_q2[l], lam_k2[l], diff_subln[l],
                          w_o_a[l], w_o_b[l], w_o_c[l], b_merge[l], w_out[l], ln_g[l], ln_b[l])
    return x
```

```python
import math
from contextlib import ExitStack
import numpy as np
import concourse.bass as bass
import concourse.mybir as mybir
from concourse.bass_utils import run_bass_kernel_spmd

F32 = mybir.dt.float32
BF16 = mybir.dt.bfloat16
AF = mybir.ActivationFunctionType
ALU = mybir.AluOpType

D = 4096
T = 2048
SEQ = 4096
L = 2
INW = 23616
NEG = -30000.0
LN_EPS = 1e-5
RMS_EPS = 1e-6
ALPHA = (2.0 * L) ** 0.25
PAIRS = [[0, 1], [2, 3], [4, 5], [6, 7]]

SEGS = [
    ("qA", 0, 1024, "fm", "copy", 128 ** -0.5),
    ("kA", 1024, 1024, "fm", "copy", 1.0),
    ("vA", 2048, 1024, "tm", None, 1.0),
    ("gA", 3072, 1024, "fm", "silu", 1.0),
    ("cq", 4096, 1536, "fm", "copy", 1.0),
    ("ckv", 5632, 512, "fm", "copy", 1.0),
    ("kr", 6144, 64, "rope", None, 1.0),
    ("gB", 6208, 1024, "fm", "silu", 1.0),
    ("qC", 7232, 1024, "rope", None, 0.125),
    ("kC", 8256, 1024, "rope", None, 1.0),
    ("vC", 9280, 1024, "tm", None, 1.0),
    ("gC", 10304, 1024, "fm", "silu", 1.0),
    ("gm", 11328, 12288, "fm", "sigmoid", 1.0),
]


class Sem:
    __slots__ = ("h", "v")

    def __init__(self, h):
        self.h = h
        self.v = 0


class Buf:
    __slots__ = ("t", "wr", "rd")

    def __init__(self, t):
        self.t = t
        self.wr = None
        self.rd = []

    def ww(self):
        if self.rd:
            return list(self.rd)
        return [self.wr] if self.wr is not None else []

    def wrote(self, ev):
        self.wr = ev
        self.rd = []

    def rw(self):
        return [self.wr] if self.wr is not None else []

    def read(self, ev):
        self.rd.append(ev)


class State:
    pass


ENGS = ("sync", "act", "dve", "pool", "pe")
ENGMAP = {"sync": "sync", "act": "scalar", "dve": "vector", "pool": "gpsimd", "pe": "tensor"}


class Prog:
    def __init__(self, nc, S):
        self.nc = nc
        self.S = S
        self.q = {e: [] for e in ENGS}

    def add(self, eng, fn, waits=(), inc=None, amt=1):
        ws = []
        seen = self.S.seen[eng]
        for w in waits:
            if w is None:
                continue
            s, v = w
            if seen.get(s, 0) >= v:
                continue
            seen[s] = v
            ws.append((s, v))
        ev = None
        if inc is not None:
            inc.v += amt
            ev = (inc, inc.v)
        self.q[eng].append((fn, ws, inc, amt))
        return ev

    def op(self, eng, fn, waits=(), sig=True):
        return self.add(eng, fn, waits, self.S.esem[eng] if sig else None, 1)

    def dma(self, eng, out, in_, waits, sem):
        return self.add(eng, lambda e: e.dma_start(out=out, in_=in_), waits, sem, 16)

    def join(self):
        allsems = [s for s in self.S.allsems if s.v > 0]
        for e in ENGS:
            self.add(e, None, [(s, s.v) for s in allsems])

    def emit(self):
        self.join()
        with self.nc.Block() as block:
            for e in ENGS:
                items = self.q[e]

                def body(eng, items=items):
                    for fn, ws, inc, amt in items:
                        for s, v in ws:
                            eng.wait_ge(s.h, v)
                        if fn is None:
                            continue
                        ins = fn(eng)
                        if inc is not None:
                            ins.then_inc(inc.h, amt)

                getattr(block, ENGMAP[e])(body)


def newsem(S, es, name):
    s = Sem(es.enter_context(S.nc.semaphore(name)))
    S.allsems.append(s)
    return s


def sb(S, es, shape, dt, name):
    S.uid += 1
    return es.enter_context(S.nc.sbuf_tensor(f"{name}_{S.uid}", shape, dt))


def psum_bufs(S, es, n=8):
    out = []
    for i in range(n):
        S.uid += 1
        out.append(Buf(es.enter_context(S.nc.psum_tensor(f"ps_{S.uid}", [128, 512], F32))))
    return out


def inproj_groups():
    groups = []
    for name, col0, width, kind, func, scale in SEGS:
        for c in range(0, width, 512):
            w = min(512, width - c)
            if kind == "tm":
                jobs = [dict(kind="tm", dest=name, dcol0=c, n=w)]
            elif kind == "rope":
                jobs = [dict(kind="rope", wcol=j, n=min(128, w - j), scale=scale, dest=name, row0=c + j)
                        for j in range(0, w, 128)]
            else:
                jobs = [dict(kind="fm", wcol=j, n=128, func=func, scale=scale, dest=name,
                             row0=c + j) for j in range(0, w, 128)]
            groups.append(dict(col0=col0 + c, w=w, jobs=jobs))
    return groups


def phase_inproj(S, l, tb, xsrc):
    nc = S.nc
    P = Prog(nc, S)
    w_in = S.w_in[l]
    with ExitStack() as es:
        xT = sb(S, es, [128, 32, 1024], BF16, "xT")
        xs = [Buf(sb(S, es, [128, 2048], F32, "xs")) for _ in range(4)]
        wb = [Buf(sb(S, es, [128, 32, 512], BF16, "wb")) for _ in range(2)]
        stg = [Buf(sb(S, es, [128, 1024], BF16, "stg")) for _ in range(4)]
        rt = [sb(S, es, [128, 512], F32, "rt") for _ in range(2)]
        rtA = [Buf(sb(S, es, [128, 512], F32, "rtA")) for _ in range(3)]
        ps = psum_bufs(S, es)
        psi = [0]
        pending = []
        rai = [0]
        rope_dve = [None]

        def nextps():
            b = ps[psi[0] % 8]
            psi[0] += 1
            return b

        def flush():
            while pending:
                job, tc, ta, st, si_ = pending.pop(0)
                n = job["n"]
                sc = job["scale"]
                t0c = tb * 1024 + tc * 512
                pb2 = nextps()
                lp2 = P.op("pe", lambda e, o=pb2.t[0:n, :], r=ta.t[0:n, :], n=n: e.matmul(o, S.pswap[0:n, 0:n], r, start=True, stop=True),
                           waits=pb2.ww() + ta.rw())
                pb2.wrote(lp2)
                e1 = P.op("dve", lambda e, o=rt[0][0:n, :], a=ta.t[0:n, :], c=S.cosT[0:n, t0c:t0c + 512], sc=sc:
                          e.scalar_tensor_tensor(o, a, sc, c, ALU.mult, ALU.mult),
                          waits=ta.rw() + ([rope_dve[0]] if rope_dve[0] else []))
                e2 = P.op("dve", lambda e, o=rt[1][0:n, :], a=pb2.t[0:n, :], c=S.sinT[0:n, t0c:t0c + 512], sc=sc:
                          e.scalar_tensor_tensor(o, a, sc, c, ALU.mult, ALU.mult), waits=pb2.rw())
                pb2.read(e2)
                ta.read(lp2)
                ta.read(e1)
                e3 = P.op("dve", lambda e, o=st.t[0:n, tc * 512:(tc + 1) * 512], a=rt[0][0:n, :], c=rt[1][0:n, :]:
                          e.tensor_tensor(o, a, c, ALU.add), waits=[e1, e2] + (st.ww() if tc == 0 else []))
                rope_dve[0] = e3
                if tc == 1:
                    st.wrote(e3)
                    dest = S.dram[job["dest"]]
                    ev = P.dma("sync", dest[job["row0"]:job["row0"] + n, tb * 1024:(tb + 1) * 1024],
                               st.t[0:n, :], st.rw(), S.sem_stg[si_ % 4])
                    st.read(ev)

        xT_events = []
        k = 0
        for tt in range(8):
            r0 = tb * 1024 + tt * 128
            for hf in range(2):
                xb = xs[k % 4]
                k += 1
                ev = P.dma("sync" if k % 2 else "act", xb.t[:], xsrc[r0:r0 + 128, hf * 2048:(hf + 1) * 2048], xb.ww(),
                           S.sem_xs[(k - 1) % 4])
                xb.wrote(ev)
                for g in range(4):
                    pb = nextps()
                    for i in range(4):
                        lastev = P.op("pe", (lambda e, o=pb.t[:, i * 128:(i + 1) * 128],
                                             a=xb.t[:, (g * 4 + i) * 128:(g * 4 + i + 1) * 128]:
                                             e.transpose(o, a, S.ident[:])),
                                      waits=(pb.ww() + xb.rw()) if i == 0 else (), sig=(i == 3))
                    pb.wrote(lastev)
                    kc0 = hf * 16 + g * 4
                    o = xT[:, kc0:kc0 + 4, tt * 128:(tt + 1) * 128]
                    a = pb.t[:, :].rearrange("p (a b) -> p a b", a=4)
                    eng = "act" if (g % 2 == 0) else "dve"
                    if eng == "act":
                        ev2 = P.op("act", lambda e, o=o, a=a: e.copy(o, a), waits=pb.rw())
                    else:
                        ev2 = P.op("dve", lambda e, o=o, a=a: e.tensor_copy(o, a), waits=pb.rw())
                    pb.read(ev2)
                    xT_events.append(ev2)
                xb.read(lastev)
        xT_ready = []
        for en in ("act", "dve"):
            evs = [e for e in xT_events if e[0] is S.esem[en]]
            xT_ready.append(max(evs, key=lambda t: t[1]))

        groups = inproj_groups()

        def load_group(gi):
            g = groups[gi]
            b = wb[gi % 2]
            src = w_in[:, g["col0"]:g["col0"] + g["w"]].rearrange("(kc p) n -> p kc n", p=128)
            ww = b.ww()
            ev = None
            for i in range(4):
                ev = P.dma("pool", b.t[:, 8 * i:8 * i + 8, 0:g["w"]], src[:, 8 * i:8 * i + 8, :], ww,
                           S.sem_wb[gi % 2])
            b.wrote(ev)

        load_group(0)
        load_group(1)
        si = 0
        for gi, g in enumerate(groups):
            b = wb[gi % 2]
            lastpe = None
            for job in g["jobs"]:
                dest = S.dram[job["dest"]]
                if job["kind"] == "tm":
                    n = job["n"]
                    for tt in range(8):
                        pb = nextps()
                        for kc in range(32):
                            lastpe = P.op("pe", (lambda e, o=pb.t[:, 0:n], a=xT[:, kc, tt * 128:(tt + 1) * 128],
                                                 r=b.t[:, kc, 0:n], kc=kc:
                                                 e.matmul(o, a, r, start=(kc == 0), stop=(kc == 31))),
                                          waits=(pb.ww() + b.rw() + xT_ready) if kc == 0 else (),
                                          sig=(kc == 31))
                        pb.wrote(lastpe)
                        flush()
                        st = stg[si % 4]
                        si += 1
                        ev = P.op("dve", lambda e, o=st.t[:, 0:n], a=pb.t[:, 0:n]: e.tensor_copy(o, a),
                                  waits=pb.rw() + st.ww())
                        pb.read(ev)
                        st.wrote(ev)
                        r0 = tb * 1024 + tt * 128
                        ev = P.dma("sync", dest[r0:r0 + 128, job["dcol0"]:job["dcol0"] + n], st.t[:, 0:n],
                                   st.rw(), S.sem_stg[(si - 1) % 4])
                        st.read(ev)
                elif job["kind"] == "fm":
                    n = job["n"]
                    st = stg[si % 4]
                    si += 1
                    evs = []
                    for tc in range(2):
                        pb = nextps()
                        for kc in range(32):
                            lastpe = P.op("pe", (lambda e, o=pb.t[0:n, :], a=b.t[:, kc, job["wcol"]:job["wcol"] + n],
                                                 r=xT[:, kc, tc * 512:(tc + 1) * 512], kc=kc:
                                                 e.matmul(o, a, r, start=(kc == 0), stop=(kc == 31))),
                                          waits=(pb.ww() + b.rw() + xT_ready) if kc == 0 else (),
                                          sig=(kc == 31))
                        pb.wrote(lastpe)
                        flush()
                        o = st.t[0:n, tc * 512:(tc + 1) * 512]
                        a = pb.t[0:n, :]
                        f = job["func"]
                        if f == "copy":
                            if job["scale"] == 1.0:
                                fn = lambda e, o=o, a=a: e.copy(o, a)
                            else:
                                fn = lambda e, o=o, a=a, s=job["scale"]: e.mul(o, a, s)
                        elif f == "silu":
                            fn = lambda e, o=o, a=a: e.activation(out=o, in_=a, func=AF.Silu)
                        else:
                            bidx = job["row0"] // 128
                            fn = lambda e, o=o, a=a, bi=bidx: e.activation(
                                out=o, in_=a, func=AF.Sigmoid, bias=S.bm[l][:, bi:bi + 1], scale=1.0)
                        ev = P.op("act", fn, waits=pb.rw() + (st.ww() if tc == 0 else []))
                        pb.read(ev)
                        evs.append(ev)
                    st.wrote(evs[-1])
                    ev = P.dma("sync", dest[job["row0"]:job["row0"] + n, tb * 1024:(tb + 1) * 1024],
                               st.t[0:n, :], st.rw(), S.sem_stg[(si - 1) % 4])
                    st.read(ev)
                else:
                    n = job["n"]
                    st = stg[si % 4]
                    si += 1
                    for tc in range(2):
                        pb = nextps()
                        for kc in range(32):
                            lastpe = P.op("pe", (lambda e, o=pb.t[0:n, :], a=b.t[:, kc, job["wcol"]:job["wcol"] + n],
                                                 r=xT[:, kc, tc * 512:(tc + 1) * 512], kc=kc:
                                                 e.matmul(o, a, r, start=(kc == 0), stop=(kc == 31))),
                                          waits=(pb.ww() + b.rw() + xT_ready) if kc == 0 else (),
                                          sig=(kc == 31))
                        pb.wrote(lastpe)
                        flush()
                        ta = rtA[rai[0] % 3]
                        rai[0] += 1
                        ev = P.op("act", lambda e, o=ta.t[0:n, :], a=pb.t[0:n, :]: e.copy(o, a), waits=pb.rw() + ta.ww())
                        pb.read(ev)
                        ta.wrote(ev)
                        pending.append((job, tc, ta, st, si - 1))
            b.read(lastpe)
            if gi + 2 < len(groups):
                load_group(gi + 2)
        flush()
        P.emit()


def rstd_from_ps(P, S, psb, n_inv, eps, tmp, out, extra_waits=()):
    e1 = P.op("dve", lambda e: e.tensor_scalar(tmp, psb.t[:, :], n_inv, eps, ALU.mult, ALU.add),
              waits=psb.rw() + list(extra_waits))
    psb.read(e1)
    e2 = P.op("act", lambda e: e.sqrt(tmp, tmp), waits=[e1])
    e3 = P.op("dve", lambda e: e.reciprocal(out, tmp), waits=[e2])
    return e3


def phase_mla_proj(S, l):
    nc = S.nc
    P = Prog(nc, S)
    with ExitStack() as es:
        wuq = sb(S, es, [128, 12, 1536], BF16, "wuq")
        wuqs = sb(S, es, [128, 12, 512], BF16, "wuqs")
        wukv = sb(S, es, [128, 4, 2048], BF16, "wukv")
        cqc = [Buf(sb(S, es, [128, 12, 512], BF16, "cqc")) for _ in range(2)]
        ckc = [Buf(sb(S, es, [128, 4, 512], BF16, "ckc")) for _ in range(2)]
        sq = sb(S, es, [128, 16, 512], BF16, "sq")
        cqn = sb(S, es, [128, 12, 512], BF16, "cqn")
        ckn = sb(S, es, [128, 4, 512], BF16, "ckn")
        tmpq = sb(S, es, [128, 512], F32, "tmpq")
        tmpk = sb(S, es, [128, 512], F32, "tmpk")
        rq = sb(S, es, [128, 512], F32, "rq")
        rk = sb(S, es, [128, 512], F32, "rk")
        stg = [Buf(sb(S, es, [128, 512], BF16, "stg2")) for _ in range(4)]
        rt = [sb(S, es, [128, 512], F32, "rt2") for _ in range(2)]
        ps = psum_bufs(S, es)
        psi = [0]

        def nextps():
            b = ps[psi[0] % 8]
            psi[0] += 1
            return b

        wev = None
        srcq = S.w_uq[l].rearrange("(rc p) n -> p rc n", p=128)
        for i in range(3):
            wev = P.dma("pool", wuq[:, 4 * i:4 * i + 4, :], srcq[:, 4 * i:4 * i + 4, :], (), S.sem_w2)
        srck = S.w_ukv[l].rearrange("(rc p) n -> p rc n", p=128)
        wev = P.dma("pool", wukv[:, :, :], srck, (), S.sem_w2)
        sview = wuq[:, :, :].rearrange("p k (h c) -> p k h c", h=8)
        dview = wuqs[:, :, :].rearrange("p k (h c) -> p k h c", h=8)
        P.op("pool", lambda e: e.tensor_copy(dview[:, :, :, 0:32], sview[:, :, :, 160:192]), waits=[wev])
        wsw = P.op("pool", lambda e: e.tensor_copy(dview[:, :, :, 32:64], sview[:, :, :, 128:160]))
        wready = [wev, wsw]
        exchange_a(P, S)

        si = 0
        prev_norm_reads = []
        for tc in range(4):
            t0 = tc * 512
            cb = cqc[tc % 2]
            kb = ckc[tc % 2]
            ev = P.dma("sync", cb.t[:], S.dram["cq"][:, t0:t0 + 512].rearrange("(rc p) t -> p rc t", p=128),
                       cb.ww(), S.sem_ld2[tc % 2])
            ev = P.dma("sync", kb.t[:], S.dram["ckv"][:, t0:t0 + 512].rearrange("(rc p) t -> p rc t", p=128),
                       kb.ww(), S.sem_ld2[tc % 2])
            cb.wrote(ev)
            kb.wrote(ev)
            e_sq1 = P.op("act", lambda e, a=cb.t[:]: e.square(sq[:, 0:12, :], a), waits=cb.rw() + prev_norm_reads)
            e_sq2 = P.op("act", lambda e, a=kb.t[:]: e.square(sq[:, 12:16, :], a), waits=kb.rw())
            pa = nextps()
            for i in range(12):
                lp = P.op("pe", lambda e, o=pa.t[:, :], r=sq[:, i, :], i=i: e.matmul(o, S.onesb[:], r, start=(i == 0), stop=(i == 11)),
                          waits=(pa.ww() + [e_sq1, S.const_ready]) if i == 0 else (), sig=(i == 11))
            pa.wrote(lp)
            pk = nextps()
            for i in range(4):
                lp = P.op("pe", lambda e, o=pk.t[:, :], r=sq[:, 12 + i, :], i=i: e.matmul(o, S.onesb[:], r, start=(i == 0), stop=(i == 3)),
                          waits=(pk.ww() + [e_sq2]) if i == 0 else (), sig=(i == 3))
            pk.wrote(lp)
            sq_read = lp
            e_rq = rstd_from_ps(P, S, pa, 1.0 / 1536, RMS_EPS, tmpq[:], rq[:], extra_waits=prev_norm_reads)
            e_rk = rstd_from_ps(P, S, pk, 1.0 / 512, RMS_EPS, tmpk[:], rk[:])
            evn = []
            for i in range(12):
                eng = "dve"
                evn.append(P.op(eng, lambda e, o=cqn[:, i, :], a=cb.t[:, i, :], g=S.gq[l][:, i:i + 1]:
                                e.scalar_tensor_tensor(o, a, g, rq[:], ALU.mult, ALU.mult),
                                waits=[e_rq] + cb.rw() + prev_norm_reads))
            for i in range(4):
                eng = "dve"
                evn.append(P.op(eng, lambda e, o=ckn[:, i, :], a=kb.t[:, i, :], g=S.gkv[l][:, i:i + 1]:
                                e.scalar_tensor_tensor(o, a, g, rk[:], ALU.mult, ALU.mult),
                                waits=[e_rk] + kb.rw() + prev_norm_reads))
            nready = [evn[-1], evn[-2], evn[11], evn[10]]
            cb.read(evn[11]); cb.read(evn[10]); kb.read(evn[-1]); kb.read(evn[-2])
            lastpe = None
            for h in range(8):
                pb = nextps()
                for rc in range(12):
                    lastpe = P.op("pe", lambda e, o=pb.t[:, :], a=wuq[:, rc, h * 192:h * 192 + 128], r=cqn[:, rc, :], rc=rc:
                                  e.matmul(o, a, r, start=(rc == 0), stop=(rc == 11)),
                                  waits=(pb.ww() + nready + wready) if rc == 0 else (), sig=(rc == 11))
                pb.wrote(lastpe)
                st = stg[si % 4]; si += 1
                ev = P.op("act", lambda e, o=st.t[:, :], a=pb.t[:, :]: e.mul(o, a, 192 ** -0.5), waits=pb.rw() + st.ww())
                pb.read(ev); st.wrote(ev)
                ev = P.dma("sync", S.dram["qBn"][h * 128:(h + 1) * 128, t0:t0 + 512], st.t[:, :], st.rw(), S.sem_stg[(si - 1) % 4])
                st.read(ev)
                pbs = []
                for which in range(2):
                    pb = nextps()
                    for rc in range(12):
                        a = wuq[:, rc, h * 192 + 128:h * 192 + 192] if which == 0 else wuqs[:, rc, h * 64:(h + 1) * 64]
                        lastpe = P.op("pe", lambda e, o=pb.t[0:64, :], a=a, r=cqn[:, rc, :], rc=rc:
                                      e.matmul(o, a, r, start=(rc == 0), stop=(rc == 11)),
                                      waits=(pb.ww() + nready + wready) if rc == 0 else (), sig=(rc == 11))
                    pb.wrote(lastpe)
                    pbs.append(pb)
                sc = 192 ** -0.5
                tg = T0 = t0
                e1 = P.op("dve", lambda e, a=pbs[0].t[0:64, :], c=S.cosT[0:64, tg:tg + 512]:
                          e.scalar_tensor_tensor(rt[0][0:64, :], a, sc, c, ALU.mult, ALU.mult), waits=pbs[0].rw())
                pbs[0].read(e1)
                e2 = P.op("dve", lambda e, a=pbs[1].t[0:64, :], c=S.sinT[0:64, tg:tg + 512]:
                          e.scalar_tensor_tensor(rt[1][0:64, :], a, sc, c, ALU.mult, ALU.mult), waits=pbs[1].rw())
                pbs[1].read(e2)
                st = stg[si % 4]; si += 1
                ev = P.op("dve", lambda e, o=st.t[0:64, :]: e.tensor_tensor(o, rt[0][0:64, :], rt[1][0:64, :], ALU.add),
                          waits=[e1, e2] + st.ww())
                st.wrote(ev)
                ev = P.dma("sync", S.dram["qBr"][h * 64:(h + 1) * 64, t0:t0 + 512], st.t[0:64, :], st.rw(), S.sem_stg[(si - 1) % 4])
                st.read(ev)
            for h in range(8):
                pb = nextps()
                for rc in range(4):
                    lastpe = P.op("pe", lambda e, o=pb.t[:, :], a=wukv[:, rc, h * 256:h * 256 + 128], r=ckn[:, rc, :], rc=rc:
                                  e.matmul(o, a, r, start=(rc == 0), stop=(rc == 3)),
                                  waits=(pb.ww() + nready + wready) if rc == 0 else (), sig=(rc == 3))
                pb.wrote(lastpe)
                st = stg[si % 4]; si += 1
                ev = P.op("act", lambda e, o=st.t[:, :], a=pb.t[:, :]: e.copy(o, a), waits=pb.rw() + st.ww())
                pb.read(ev); st.wrote(ev)
                ev = P.dma("sync", S.dram["kBn"][h * 128:(h + 1) * 128, t0:t0 + 512], st.t[:, :], st.rw(), S.sem_stg[(si - 1) % 4])
                st.read(ev)
            wv = wukv[:, :, :].rearrange("p k (h c) -> p k h c", h=8)
            for tt in range(4):
                for half in range(2):
                    pb = nextps()
                    for rc in range(4):
                        lastpe = P.op("pe", lambda e, o=pb.t[:, :].rearrange("p (h c) -> p h c", h=4),
                                      a=ckn[:, rc, tt * 128:(tt + 1) * 128], r=wv[:, rc, half * 4:(half + 1) * 4, 128:256], rc=rc:
                                      e.matmul(o, a, r, start=(rc == 0), stop=(rc == 3)),
                                      waits=(pb.ww() + nready + wready) if rc == 0 else (), sig=(rc == 3))
                    pb.wrote(lastpe)
                    st = stg[si % 4]; si += 1
                    ev = P.op("dve", lambda e, o=st.t[:, :], a=pb.t[:, :]: e.tensor_copy(o, a), waits=pb.rw() + st.ww())
                    pb.read(ev); st.wrote(ev)
                    r0 = t0 + tt * 128
                    ev = P.dma("sync", S.dram["vB"][r0:r0 + 128, half * 512:(half + 1) * 512], st.t[:, :], st.rw(), S.sem_stg[(si - 1) % 4])
                    st.read(ev)
            prev_norm_reads = [lastpe, sq_read]
        P.emit()


def emit_collectives(P, S, pairs, waits):
    first = True
    for a, o in pairs:
        P.add("pool", lambda e, a=a, o=o: e.collective_compute(
            "AllGather", ALU.bypass, replica_groups=PAIRS, ins=[a.opt()], outs=[o.opt()]),
            waits=waits if first else (), inc=S.sem_cc, amt=1)
        first = False


def exchange_a(P, S):
    d = S.dram
    e1 = P.dma("sync", d["kAh"][:, 0:384], d["kA"][:, 0:384], (), S.sem_ex)
    e1 = P.dma("sync", d["kAh"][:, 384:768], d["kA"][:, T - 384:T], (), S.sem_ex)
    e1 = P.dma("sync", d["vAh"][0:384, :], d["vA"][0:384, :], (), S.sem_ex)
    e1 = P.dma("sync", d["vAh"][384:768, :], d["vA"][T - 384:T, :], (), S.sem_ex)
    pairs = [(d["kAh"], d["kAh_g"]), (d["vAh"], d["vAh_g"]), (d["kr"], d["kr_g"])]
    for i in range(2):
        pairs.append((d["kC"][i * 512:(i + 1) * 512, :], d[f"kC_g{i}"]))
        pairs.append((d["vC"][i * 1024:(i + 1) * 1024, :], d[f"vC_g{i}"]))
    emit_collectives(P, S, pairs, [e1])


def exchange_b(P, S):
    d = S.dram
    pairs = []
    for i in range(2):
        pairs.append((d["kBn"][i * 512:(i + 1) * 512, :], d[f"kBn_g{i}"]))
        pairs.append((d["vB"][i * 1024:(i + 1) * 1024, :], d[f"vB_g{i}"]))
    emit_collectives(P, S, pairs, [])


def na_type(j):
    return 0 if j == 0 else 1 if j == 1 else 3 if j == 14 else 4 if j == 15 else 2


def phase_na(S, l):
    nc = S.nc
    P = Prog(nc, S)
    d = S.dram
    with ExitStack() as es:
        KTs = [sb(S, es, [128, 22 * 128], BF16, "naK") for _ in range(2)]
        Vs = [sb(S, es, [128, 22, 128], BF16, "naV") for _ in range(2)]
        QTs = [sb(S, es, [128, T], BF16, "naQ") for _ in range(2)]
        GTs = [sb(S, es, [128, T], BF16, "naG") for _ in range(2)]
        BT = sb(S, es, [128, 8, 35, 128], BF16, "naB")
        PT = [Buf(sb(S, es, [128, 7 * 128], BF16, "naP")) for _ in range(2)]
        rec = sb(S, es, [128, 512], F32, "narec")
        tmp = sb(S, es, [128, 512], F32, "natmp")
        stg = [Buf(sb(S, es, [128, 512], BF16, "nastg")) for _ in range(2)]
        ps = psum_bufs(S, es)
        psS = [ps[0], ps[1], ps[2], ps[3]]
        psO = [ps[4], ps[5]]
        psR = [ps[6], ps[7]]
        evbs = []
        for h in range(8):
            evbs.append(P.dma("pool", BT[:, h, :, :], S.nabias[l][h], (), S.sem_nab[h]))
        exchange_b(P, S)
        head_done = {}
        loaded = {}

        def load(h):
            KT, V, QT, GT = KTs[h % 2], Vs[h % 2], QTs[h % 2], GTs[h % 2]
            ww = head_done.get(h - 2, [])
            sem = S.sem_ln[h % 2]
            r = slice(h * 128, (h + 1) * 128)
            ev = P.dma("sync", KT[:, 0:384], d["kAh_g"][h * 128:(h + 1) * 128, 384:768], ww, sem)
            ev = P.dma("sync", KT[:, 384:384 + T], d["kA"][r, :], ww, sem)
            ev = P.dma("sync", KT[:, 384 + T:768 + T], d["kAh_g"][1024 + h * 128:1024 + (h + 1) * 128, 0:384], ww, sem)
            ev = P.dma("sync", V[:, 0:3, :], d["vAh_g"][384:768, r].rearrange("(i p) c -> p i c", p=128), ww, sem)
            vav = d["vA"][:, r].rearrange("(i p) c -> p i c", p=128)
            ev = P.dma("sync", V[:, 3:11, :], vav[:, 0:8, :], ww, sem)
            ev = P.dma("sync", V[:, 11:19, :], vav[:, 8:16, :], ww, sem)
            ev = P.dma("sync", V[:, 19:22, :], d["vAh_g"][768:768 + 384, r].rearrange("(i p) c -> p i c", p=128), ww, sem)
            ev = P.dma("sync", QT[:, :], d["qA"][r, :], ww, sem)
            ev = P.dma("sync", GT[:, :], d["gA"][r, :], ww, sem)
            loaded[h] = ev

        load(0)
        load(1)
        last_evac = None
        pti = 0
        sti = 0
        gi = 0
        for h in range(8):
            KT, V, QT, GT = KTs[h % 2], Vs[h % 2], QTs[h % 2], GTs[h % 2]
            r = slice(h * 128, (h + 1) * 128)
            ready = [loaded[h], evbs[h], S.const_ready]
            lp = None
            for qg in range(4):
                pO = psO[gi % 2]
                pR = psR[gi % 2]
                gi += 1
                for jj in range(4):
                    j = qg * 4 + jj
                    ty = na_type(j)
                    slots = list(range(7)) if ty != 2 else [1, 2, 3, 4, 5]
                    ns = len(slots)
                    pt = PT[pti % 2]
                    p1 = psS[(pti % 2) * 2]
                    p2 = psS[(pti % 2) * 2 + 1]
                    pti += 1
                    for idx, s in enumerate(slots):
                        pb = p1 if idx < 4 else p2
                        c0 = (idx % 4) * 128
                        P.op("pe", lambda e, o=pb.t[:, c0:c0 + 128], a=KT[:, (j + s) * 128:(j + s + 1) * 128],
                             q=QT[:, j * 128:(j + 1) * 128]: e.matmul(o, a, q, start=True, stop=False),
                             waits=(pb.ww() + ready) if idx in (0, 4) else (), sig=False)
                        lp = P.op("pe", lambda e, o=pb.t[:, c0:c0 + 128], b=BT[:, h, ty * 7 + s, :]:
                                  e.matmul(o, S.identb[:], b, start=False, stop=True), sig=(idx in (3, ns - 1)))
                        if idx == 3:
                            p1.wrote(lp)
                        if idx == ns - 1 and idx != 3:
                            p2.wrote(lp)
                    e1 = P.op("act", lambda e, o=pt.t[:, 0:512], a=p1.t[:, :]: e.activation(out=o, in_=a, func=AF.Exp),
                              waits=p1.rw() + pt.ww())
                    p1.read(e1)
                    n2 = (ns - 4) * 128
                    e2 = P.op("act", lambda e, o=pt.t[:, 512:512 + n2], a=p2.t[:, 0:n2]: e.activation(out=o, in_=a, func=AF.Exp),
                              waits=p2.rw())
                    p2.read(e2)
                    pt.wrote(e2)
                    for idx, s in enumerate(slots):
                        P.op("pe", lambda e, o=pO.t[:, jj * 128:(jj + 1) * 128], a=V[:, j + s, :], p=pt.t[:, idx * 128:(idx + 1) * 128], idx=idx, ns=ns:
                             e.matmul(o, a, p, start=(idx == 0), stop=(idx == ns - 1)),
                             waits=(pt.rw() + pO.ww() + pR.ww()) if idx == 0 else (), sig=False)
                        lp = P.op("pe", lambda e, o=pR.t[:, jj * 128:(jj + 1) * 128], p=pt.t[:, idx * 128:(idx + 1) * 128], idx=idx, ns=ns:
                                  e.matmul(o, S.onesb[:], p, start=(idx == 0), stop=(idx == ns - 1)), sig=(idx == ns - 1))
                    pt.read(lp)
                pO.wrote(lp)
                pR.wrote(lp)
                ea = P.op("dve", lambda e, a=pR.t[:, :]: e.reciprocal(rec[:], a), waits=pR.rw() + ([last_evac] if last_evac else []))
                pR.read(ea)
                eb = P.op("dve", lambda e, a=pO.t[:, :]: e.tensor_tensor(tmp[:], a, rec[:], ALU.mult), waits=pO.rw() + [ea])
                pO.read(eb)
                st = stg[sti % 2]; sti += 1
                ec = P.op("dve", lambda e, o=st.t[:, :], g=GT[:, qg * 512:(qg + 1) * 512]: e.tensor_tensor(o, tmp[:], g, ALU.mult),
                          waits=[eb] + st.ww())
                st.wrote(ec)
                last_evac = ec
                ev = P.dma("sync", d["yA"][r, qg * 512:(qg + 1) * 512], st.t[:, :], st.rw(), S.sem_stg[(sti - 1) % 2])
                st.read(ev)
            head_done[h] = [lp, last_evac]
            if h + 2 < 8:
                load(h + 2)
        P.emit()


def phase_dense(S, l, kind):
    nc = S.nc
    P = Prog(nc, S)
    d = S.dram
    mla = (kind == "mla")
    nm = 1 if mla else 2
    with ExitStack() as es:
        KTs = [sb(S, es, [128, SEQ], BF16, "dK") for _ in range(2)]
        Vs = [sb(S, es, [128, 32, 128], BF16, "dV") for _ in range(2)]
        GTs = [sb(S, es, [128, T], BF16, "dG") for _ in range(2)]
        if mla:
            KR = sb(S, es, [128, SEQ], BF16, "dKr")
            QTs = [sb(S, es, [128, T], BF16, "dQ") for _ in range(2)]
            QRs = [sb(S, es, [128, T], BF16, "dQr") for _ in range(2)]
        else:
            QAs = [sb(S, es, [128, T], BF16, "dQa") for _ in range(2)]
            QBs = [sb(S, es, [128, T], BF16, "dQb") for _ in range(2)]
        NPT = 12
        PT = [Buf(sb(S, es, [128, 512], BF16, "dP")) for _ in range(NPT)]
        rsum = [sb(S, es, [128, 512], F32, "drsum") for _ in range(nm)]
        oraw = [sb(S, es, [128, 512], F32, "doraw") for _ in range(nm)]
        rec = sb(S, es, [128, 512], F32, "drec")
        yy = sb(S, es, [128, 512], F32, "dyy")
        sqy = sb(S, es, [128, 512], F32, "dsq")
        rs = sb(S, es, [128, 512], F32, "drs")
        stg = [Buf(sb(S, es, [128, 512], BF16, "dstg")) for _ in range(2)]
        ps = psum_bufs(S, es)
        psS = ps[0:4]
        psO = ps[4:6]
        psR = ps[6:8]
        zev = None
        kr_ev = None
        if mla:
            zev = P.op("dve", lambda e: e.memset(KR[:, :], 0.0))
            for i in range(2):
                zev = P.op("dve", lambda e, q=QRs[i]: e.memset(q[:, :], 0.0))
            kr_ev = P.dma("sync", KR[0:64, 0:T], d["kr_g"][0:64, :], [zev], S.sem_ld3)
            kr_ev = P.dma("sync", KR[0:64, T:SEQ], d["kr_g"][64:128, :], [zev], S.sem_ld3)
        else:
            for i in range(2):
                P.op("dve", lambda e, q=QAs[i]: e.memset(q[:, :], 0.0))
                zev = P.op("dve", lambda e, q=QBs[i]: e.memset(q[:, :], 0.0))
        kgs = [d["kBn_g0"], d["kBn_g1"]] if mla else [d["kC_g0"], d["kC_g1"]]
        vgs = [d["vB_g0"], d["vB_g1"]] if mla else [d["vC_g0"], d["vC_g1"]]
        qsrc = d["qBn"] if mla else d["qC"]
        gsrc = d["gB"] if mla else d["gC"]
        ydst = d["yB"] if mla else d["yC"]
        head_done = {}
        loaded = {}

        def load(h):
            b = h % 2
            r = slice(h * 128, (h + 1) * 128)
            ww = head_done.get(h - 2, []) + [zev]
            sem = S.sem_ln[b]
            kg = kgs[h // 4]
            hr = (h % 4) * 128
            ev = P.dma("sync", KTs[b][:, 0:T], kg[hr:hr + 128, :], ww, sem)
            ev = P.dma("sync", KTs[b][:, T:SEQ], kg[512 + hr:512 + hr + 128, :], ww, sem)
            for rk_ in range(2):
                for th in range(2):
                    vi = rk_ * 2 + th
                    ev = P.dma("sync", Vs[b][:, vi * 8:(vi + 1) * 8, :],
                               vgs[th][rk_ * 1024:(rk_ + 1) * 1024, r].rearrange("(i p) c -> p i c", p=128), ww, sem)
            ev = P.dma("sync", GTs[b][:, :], gsrc[r, :], ww, sem)
            if mla:
                ev = P.dma("sync", QTs[b][:, :], qsrc[r, :], ww, sem)
                ev = P.dma("sync", QRs[b][0:64, :], d["qBr"][h * 64:(h + 1) * 64, :], ww, sem)
            else:
                ev = P.dma("sync", QAs[b][0:64, :], qsrc[h * 128:h * 128 + 64, :], ww, sem)
                ev = P.dma("sync", QBs[b][64:128, :], qsrc[h * 128 + 64:h * 128 + 128, :], ww, sem)
            loaded[h] = ev

        load(0)
        load(1)
        epi_done = [None]
        deferred = []
        si = 0
        pi = 0
        sti = 0
        for h in range(8):
            b = h % 2
            KT, V, GT = KTs[b], Vs[b], GTs[b]
            r = slice(h * 128, (h + 1) * 128)
            ready = [loaded[h], S.const_ready] + ([kr_ev] if mla else [])
            lp = None
            for qc in range(4):
                q0 = qc * 512
                pend = []
                units = [(kt, m) for kt in range(32) for m in range(nm)]

                def issue_s(u):
                    nonlocal si, pi
                    kt, m = u
                    pb = psS[si % 4]; si += 1
                    if mla:
                        P.op("pe", lambda e, o=pb.t[:, :], a=KT[:, kt * 128:(kt + 1) * 128], q=QTs[b][:, q0:q0 + 512]:
                             e.matmul(o, a, q, start=True, stop=False), waits=pb.ww() + ready, sig=False)
                        lps = P.op("pe", lambda e, o=pb.t[:, :], a=KR[:, kt * 128:(kt + 1) * 128], q=QRs[b][:, q0:q0 + 512]:
                                   e.matmul(o, a, q, start=False, stop=True))
                    else:
                        lps = P.op("pe", lambda e, o=pb.t[:, :], a=KT[:, kt * 128:(kt + 1) * 128],
                                   q=(QAs[b] if m == 0 else QBs[b])[:, q0:q0 + 512]: e.matmul(o, a, q, start=True, stop=True),
                                   waits=pb.ww() + ready)
                    pb.wrote(lps)
                    pt = PT[pi % NPT]; pi += 1
                    ee = P.op("act", lambda e, o=pt.t[:, :], a=pb.t[:, :]: e.activation(out=o, in_=a, func=AF.Exp),
                              waits=pb.rw() + pt.ww())
                    pb.read(ee)
                    pt.wrote(ee)
                    pend.append(pt)

                LA = 3
                for u in units[:LA]:
                    issue_s(u)
                G = 4 * nm
                for gidx, g0 in enumerate(range(0, len(units), G)):
                    if gidx in (2, 4) and deferred:
                        deferred.pop(0)()
                    grp = list(range(g0, g0 + G))
                    for ui in grp:
                        kt, m = units[ui]
                        pt = pend[ui]
                        pO = psO[m]
                        first = (kt == 0)
                        last = (kt == 31)
                        P.op("pe", lambda e, o=pO.t[:, :], a=V[:, kt, :], p=pt.t[:, :], first=first, last=last:
                             e.matmul(o, a, p, start=first, stop=last),
                             waits=pt.rw() + ((pO.ww() + psR[m].ww()) if first else []), sig=False)
                        if ui + LA < len(units):
                            issue_s(units[ui + LA])
                    for m in range(nm):
                        for ui in grp:
                            kt, mm = units[ui]
                            if mm != m:
                                continue
                            pt = pend[ui]
                            pR = psR[m]
                            j = kt % 4
                            lp = P.op("pe", lambda e, o=pR.t[32 * j:32 * j + 32, :], p=pt.t[:, :], kt=kt, j=j:
                                      e.matmul(o, S.onesb[:, 0:32], p, start=(kt < 4), stop=(kt >= 28), tile_position=(0, 32 * j)),
                                      sig=(j == 3))
                        for ui in grp:
                            if units[ui][1] == m:
                                pend[ui].read(lp)
                        if units[grp[-1]][0] == 31:
                            psO[m].wrote(lp)
                            psR[m].wrote(lp)
                pw = [epi_done[0]] if epi_done[0] else []
                cps = []
                for m in range(nm):
                    c1 = P.op("dve", lambda e, o=rsum[m][:], a=psR[m].t[:, :]: e.tensor_copy(o, a),
                              waits=psR[m].rw() + pw + ([lastpe_n[1]] if lastpe_n[1] else []))
                    psR[m].read(c1)
                    c2 = P.op("dve", lambda e, o=oraw[m][:], a=psO[m].t[:, :]: e.tensor_copy(o, a), waits=psO[m].rw() + pw)
                    psO[m].read(c2)
                    cps.append((c1, c2))
                state = {}

                def stage_a(cps=cps, state=state):
                    nonlocal si
                    outs = []
                    for m in range(nm):
                        c1, c2 = cps[m]
                        pn2 = psS[si % 4]; si += 1
                        l2 = P.op("pe", lambda e, o=pn2.t[:, :], rr=rsum[m][:]: e.matmul(o, S.sel4[:], rr, start=True, stop=True),
                                  waits=pn2.ww() + [c1])
                        pn2.wrote(l2)
                        lastpe_n[1] = l2
                        c3 = P.op("dve", lambda e, a=pn2.t[:, :]: e.reciprocal(rec[:], a), waits=pn2.rw())
                        pn2.read(c3)
                        c4 = P.op("dve", lambda e, o=oraw[m][:]: e.tensor_tensor(o, o, rec[:], ALU.mult), waits=[c3, c2])
                        outs.append(c4)
                    if not mla:
                        e3 = P.op("dve", lambda e: e.scalar_tensor_tensor(yy[:], oraw[1][:], S.neglam[l][:, 0:1], oraw[0][:], ALU.mult, ALU.add),
                                  waits=[outs[1], S.lam_ready])
                        e4 = P.op("pool", lambda e: e.tensor_tensor(sqy[:], yy[:], yy[:], ALU.mult),
                                  waits=[e3] + ([lastpe_n[0]] if lastpe_n[0] else []))
                        state["e4"] = e4
                    state["outs"] = outs

                def stage_b(state=state, GT=GT, q0=q0, r=r, h=h, qc=qc, lp=lp):
                    nonlocal si, sti
                    st = stg[sti % 2]; sti += 1
                    if mla:
                        ec = P.op("dve", lambda e, o=st.t[:, :], g=GT[:, q0:q0 + 512]: e.tensor_tensor(o, oraw[0][:], g, ALU.mult),
                                  waits=[state["outs"][0]] + st.ww())
                    else:
                        pn = psS[si % 4]; si += 1
                        lpn = P.op("pe", lambda e, o=pn.t[:, :]: e.matmul(o, S.onesf[:], sqy[:], start=True, stop=True),
                                   waits=pn.ww() + [state["e4"]])
                        pn.wrote(lpn)
                        lastpe_n[0] = lpn
                        e5 = rstd_from_ps(P, S, pn, 1.0 / 128, RMS_EPS, rs[:], rs[:])
                        e6 = P.op("dve", lambda e: e.tensor_tensor(yy[:], yy[:], rs[:], ALU.mult), waits=[e5])
                        ec = P.op("dve", lambda e, o=st.t[:, :], g=GT[:, q0:q0 + 512]:
                                  e.scalar_tensor_tensor(o, yy[:], S.subc[l][:, 0:1], g, ALU.mult, ALU.mult),
                                  waits=[e6] + st.ww())
                    st.wrote(ec)
                    epi_done[0] = ec
                    ev2 = P.dma("sync", ydst[r, q0:q0 + 512], st.t[:, :], st.rw(), S.sem_stg[(sti - 1) % 2])
                    st.read(ev2)
                    if qc == 3:
                        head_done[h] = [lp, ec]
                        if h + 2 < 8:
                            load(h + 2)

                deferred.append(stage_a)
                deferred.append(stage_b)
        while deferred:
            deferred.pop(0)()
        P.emit()


lastpe_n = [None, None]


def phase_outproj(S, l, tb, xsrc):
    nc = S.nc
    d = S.dram
    tb0 = tb * 1024
    with ExitStack() as es0:
        mT = sb(S, es0, [128, 32, 1024], BF16, "mT")
        P = Prog(nc, S)
        with ExitStack() as es:
            yT = sb(S, es, [128, 3, 8, 1024], BF16, "yT")
            wo = [Buf(sb(S, es, [128, 3, 8, 512], BF16, "wo")) for _ in range(2)]
            gt = [Buf(sb(S, es, [128, 3, 512], BF16, "gt")) for _ in range(2)]
            tt_ = [[sb(S, es, [128, 512], F32, "t5") for _ in range(3)] for _ in range(2)]
            ps = psum_bufs(S, es)
            psi = 0
            yev = None
            for j, nmy in enumerate(("yA", "yB", "yC")):
                yev = P.dma("sync", yT[:, j, :, :], d[nmy][:, tb0:tb0 + 1024].rearrange("(wc p) t -> p wc t", p=128), (), S.sem_ld3)
            wsrc = [S.w_o[j][l] for j in range(3)]

            def load_wo(dg):
                b = wo[dg % 2]
                ww = b.ww()
                ev = None
                for j in range(3):
                    ev = P.dma("pool", b.t[:, j, :, :], wsrc[j][:, dg * 512:(dg + 1) * 512].rearrange("(wc p) n -> p wc n", p=128),
                               ww, S.sem_wb[dg % 2])
                b.wrote(ev)

            gview = d["gm"].rearrange("(j dc p) t -> dc p j t", j=3, p=128)
            load_wo(0)
            load_wo(1)
            gi = 0
            prev_tt = [None, None]
            mT_ev = []
            for dg in range(8):
                b = wo[dg % 2]
                lp = None
                for ds in range(4):
                    dc = dg * 4 + ds
                    for tc in range(2):
                        g = gt[gi % 2]
                        tset = tt_[gi % 2]
                        pv = prev_tt[gi % 2]
                        gi += 1
                        ev = P.dma("sync", g.t[:, :, :], gview[dc][:, :, tb0 + tc * 512:tb0 + (tc + 1) * 512], g.ww(), S.sem_ld2[(gi - 1) % 2])
                        g.wrote(ev)
                        pbs = []
                        for j in range(3):
                            pb = ps[psi % 8]; psi += 1
                            for wc in range(8):
                                lp = P.op("pe", lambda e, o=pb.t[:, :], a=b.t[:, j, wc, ds * 128:(ds + 1) * 128],
                                          r=yT[:, j, wc, tc * 512:(tc + 1) * 512], wc=wc:
                                          e.matmul(o, a, r, start=(wc == 0), stop=(wc == 7)),
                                          waits=(pb.ww() + b.rw() + [yev]) if wc == 0 else (), sig=(wc == 7))
                            pb.wrote(lp)
                            pbs.append(pb)
                        evs = []
                        for j in range(3):
                            e1 = P.op("dve", lambda e, o=tset[j][:], a=pbs[j].t[:, :], gg=g.t[:, j, :]: e.tensor_tensor(o, a, gg, ALU.mult),
                                      waits=pbs[j].rw() + g.rw() + ([pv] if pv else []))
                            pbs[j].read(e1)
                            evs.append(e1)
                        g.read(evs[-1])
                        e2 = P.op("pool", lambda e, a=tset[0][:], c=tset[1][:]: e.tensor_tensor(a, a, c, ALU.add), waits=evs)
                        e3 = P.op("pool", lambda e, o=mT[:, dc, tc * 512:(tc + 1) * 512], a=tset[0][:], c=tset[2][:]:
                                  e.tensor_tensor(o, a, c, ALU.add), waits=[e2])
                        prev_tt[(gi - 1) % 2] = e3
                        mT_ev = [e3]
                b.read(lp)
                if dg + 2 < 8:
                    load_wo(dg + 2)
            P.emit()
        P = Prog(nc, S)
        with ExitStack() as es:
            wob = [Buf(sb(S, es, [128, 32, 512], BF16, "wout")) for _ in range(2)]
            stf = [Buf(sb(S, es, [128, 512], F32, "stf")) for _ in range(4)]
            xcs = [Buf(sb(S, es, [128, 512], F32, "xc")) for _ in range(4)]
            ps = psum_bufs(S, es)
            psi = 0
            sti = 0
            wsrc = S.w_out[l]

            def load_w(eg):
                b = wob[eg % 2]
                ww = b.ww()
                src = wsrc[:, eg * 512:(eg + 1) * 512].rearrange("(dc p) n -> p dc n", p=128)
                ev = None
                for i in range(4):
                    ev = P.dma("pool", b.t[:, 8 * i:8 * i + 8, :], src[:, 8 * i:8 * i + 8, :], ww, S.sem_wb[eg % 2])
                b.wrote(ev)

            load_w(0)
            load_w(1)
            zrs = P.op("dve", lambda e: e.memset(S.rowsum[:, tb * 8:(tb + 1) * 8, :], 0.0))
            for eg in range(8):
                b = wob[eg % 2]
                lp = None
                for tt in range(8):
                    pb = ps[psi % 8]; psi += 1
                    for dc in range(32):
                        lp = P.op("pe", lambda e, o=pb.t[:, :], a=mT[:, dc, tt * 128:(tt + 1) * 128], r=b.t[:, dc, :], dc=dc:
                                  e.matmul(o, a, r, start=(dc == 0), stop=(dc == 31)),
                                  waits=(pb.ww() + b.rw()) if dc == 0 else (), sig=(dc == 31))
                    pb.wrote(lp)
                    st = stf[sti % 4]
                    xc = xcs[sti % 4]
                    sti += 1
                    r0 = tb0 + tt * 128
                    evx = P.dma("act", xc.t[:, :], xsrc[r0:r0 + 128, eg * 512:(eg + 1) * 512], xc.ww(), S.sem_ln[(sti - 1) % 4])
                    xc.wrote(evx)
                    ev = P.op("dve", lambda e, o=st.t[:, :], a=pb.t[:, :], x=xc.t[:, :], acc=S.rowsum[:, tb * 8 + tt, eg:eg + 1]:
                              e.scalar_tensor_tensor(o, x, ALPHA, a, ALU.mult, ALU.add, accum_out=acc),
                              waits=pb.rw() + st.ww() + xc.rw() + [zrs])
                    pb.read(ev); st.wrote(ev); xc.read(ev)
                    ev = P.dma("sync", d["yout"][r0:r0 + 128, eg * 512:(eg + 1) * 512], st.t[:, :], st.rw(), S.sem_stg[(sti - 1) % 4])
                    st.read(ev)
                b.read(lp)
                if eg + 2 < 8:
                    load_w(eg + 2)
            P.emit()


def phase_ln(S, l, xsrc, xdst):
    nc = S.nc
    P = Prog(nc, S)
    d = S.dram
    NT = T // 128
    with ExitStack() as es:
        lng = sb(S, es, [128, D], F32, "lng")
        lnb = sb(S, es, [128, D], F32, "lnb")
        yt = [Buf(sb(S, es, [128, D], F32, "lny")) for _ in range(4)]
        xt = [Buf(sb(S, es, [128, D], F32, "lnx")) for _ in range(4)]
        sts = [sb(S, es, [128, 8], F32, "lnst") for _ in range(4)]
        cev = P.dma("sync", lng[:], S.lngD[l], (), S.sem_c)
        cev = P.dma("sync", lnb[:], S.lnbD[l], (), S.sem_c)
        evR, evQ, evT, evN = {}, {}, {}, {}

        def load(i):
            yb = yt[i % 4]
            r0 = i * 128
            ev = P.dma("sync", yb.t[:], d["yout"][r0:r0 + 128, :], yb.ww(), S.sem_ln[i % 4])
            yb.wrote(ev)

        def st_R(i):
            yb, st1 = yt[i % 4], sts[i % 4]
            e0 = P.op("dve", lambda e, st1=st1: e.memset(st1[:, :], 0.0), waits=[evN[i - 4]] if (i - 4) in evN else [])
            e1 = P.op("dve", lambda e, rsrc=S.rowsum[:, i, :], st1=st1: e.reduce_sum(st1[:, 0:1], rsrc, axis=mybir.AxisListType.X),
                      waits=[e0])
            evR[i] = P.op("dve", lambda e, st1=st1: e.tensor_scalar(st1[:, 1:2], st1[:, 0:1], -1.0 / D, None, ALU.mult), waits=[e1])

        def st_Q(i):
            yb, xb, st1 = yt[i % 4], xt[i % 4], sts[i % 4]
            evQ[i] = P.op("act", lambda e, y=yb.t[:], x=xb.t[:], st1=st1:
                          e.activation(out=x, in_=y, func=AF.Square, bias=st1[:, 1:2], scale=1.0, accum_out=st1[:, 2:3]),
                          waits=[evR[i]] + xb.ww() + yb.rw())

        def st_T(i):
            st1 = sts[i % 4]
            e7 = P.op("dve", lambda e, st1=st1: e.tensor_scalar(st1[:, 3:4], st1[:, 2:3], 1.0 / D, LN_EPS, ALU.mult, ALU.add), waits=[evQ[i]])
            e8 = P.op("act", lambda e, st1=st1: e.sqrt(st1[:, 4:5], st1[:, 3:4]), waits=[e7])
            e9 = P.op("dve", lambda e, st1=st1: e.reciprocal(st1[:, 5:6], st1[:, 4:5]), waits=[e8])
            evT[i] = P.op("dve", lambda e, st1=st1: e.tensor_tensor(st1[:, 6:7], st1[:, 1:2], st1[:, 5:6], ALU.mult), waits=[e9])

        def st_N(i):
            yb, st1 = yt[i % 4], sts[i % 4]
            evN[i] = P.op("act", lambda e, y=yb.t[:], st1=st1:
                          e.activation(out=y, in_=y, func=AF.Identity, bias=st1[:, 6:7], scale=st1[:, 5:6]), waits=[evT[i]])

        evM = {}

        def st_M(i):
            yb, xb = yt[i % 4], xt[i % 4]
            e10 = P.op("dve", lambda e, y=yb.t[:], x=xb.t[:]: e.tensor_tensor(x, y, lng[:], ALU.mult), waits=[evN[i], cev])
            yb.read(e10)
            evM[i] = e10

        def st_H(i):
            yb, xb = yt[i % 4], xt[i % 4]
            r0 = i * 128
            e10 = evM[i]
            e11 = P.op("dve", lambda e, x=xb.t[:]: e.tensor_tensor(x, x, lnb[:], ALU.add), waits=[e10])
            xb.wrote(e11)
            ev = P.dma("pool", xdst[r0:r0 + 128, :], xb.t[:], [e11], S.sem_stg[i % 4])
            xb.read(ev)

        for i in range(4):
            load(i)
        st_R(0)
        st_Q(0)
        st_T(0)
        st_R(1)
        for i in range(NT):
            st_N(i)
            st_M(i)
            if i + 1 < NT:
                st_Q(i + 1)
                st_T(i + 1)
            st_H(i)
            if i + 2 < NT:
                st_R(i + 2)
            if i + 4 < NT:
                load(i + 4)
        P.emit()


def build(stop_after=None, debug_out=(), nlayers=L):
    nc = bass.Bass("TRN2", target_bir_lowering=False)
    S = State()
    S.nc = nc
    S.uid = 0
    S.allsems = []
    S.seen = {e: {} for e in ENGS}
    lastpe_n[0] = None
    lastpe_n[1] = None

    def din(name, shape, dt=F32):
        return nc.dram_tensor(name, shape, dt, kind="ExternalInput").ap()

    S.x = din("x", [T, D])
    w_in_all = din("w_in", [L, D, INW])
    S.w_in = [w_in_all[l] for l in range(L)]
    w_uq = din("w_uq", [L, 1536, 1536]); S.w_uq = [w_uq[l] for l in range(L)]
    w_ukv = din("w_ukv", [L, 512, 2048]); S.w_ukv = [w_ukv[l] for l in range(L)]
    S.w_o = []
    for nm in ("w_o_a", "w_o_b", "w_o_c"):
        t = din(nm, [L, 1024, D])
        S.w_o.append([t[l] for l in range(L)])
    w_out = din("w_out", [L, D, D]); S.w_out = [w_out[l] for l in range(L)]
    S.identD = din("ident", [128, 128])
    S.pswapD = din("pswap", [128, 128])
    S.sel4D = din("sel4", [128, 128])
    S.cosD = din("cosT", [128, T])
    S.sinD = din("sinT", [128, T])
    S.bmD = din("bm", [L, 128, 96])
    S.gqD = din("gq", [L, 128, 12])
    S.gkvD = din("gkv", [L, 128, 4])
    S.sublnD = din("subln", [L, 128, 1])
    S.lamD = din("lamrep", [L, 128, 4, 64])
    lng = din("lng", [L, 128, D]); S.lngD = [lng[l] for l in range(L)]
    lnb = din("lnb", [L, 128, D]); S.lnbD = [lnb[l] for l in range(L)]
    nab = din("nabias", [L, 8, 128, 35, 128])
    S.nabias = [[nab[l][h] for h in range(8)] for l in range(L)]
    S.out = nc.dram_tensor("out", [T, D], F32, kind="ExternalOutput").ap()

    S.dram = {}

    def scr(name, shape, dt=BF16):
        if name in debug_out:
            S.dram[name] = nc.dram_tensor(name, shape, dt, kind="ExternalOutput").ap()
        else:
            S.dram[name] = nc.dram_tensor(name, shape, dt).ap()

    for nm in ("qA", "kA", "gA", "gB", "qC", "kC", "gC", "qBn", "kBn", "yA", "yB", "yC"):
        scr(nm, [1024, T])
    for nm in ("vA", "vC", "vB"):
        scr(nm, [T, 1024])
    scr("cq", [1536, T]); scr("ckv", [512, T]); scr("kr", [64, T]); scr("gm", [12288, T])
    scr("qBr", [512, T])
    scr("kAh", [1024, 768]); scr("kAh_g", [2048, 768])
    scr("vAh", [768, 1024]); scr("vAh_g", [1536, 1024])
    scr("kr_g", [128, T])
    for i in range(2):
        scr(f"kBn_g{i}", [1024, T]); scr(f"kC_g{i}", [1024, T])
        scr(f"vB_g{i}", [2048, 1024]); scr(f"vC_g{i}", [2048, 1024])
    scr("yout", [T, D], F32)
    scr("x1", [T, D], F32)

    with ExitStack() as es:
        S.esem = {e: newsem(S, es, f"e_{e}") for e in ("act", "dve", "pool", "pe")}
        S.sem_xs = [newsem(S, es, f"xs{i}") for i in range(4)]
        S.sem_nab = [newsem(S, es, f"nab{i}") for i in range(8)]
        S.sem_wb = [newsem(S, es, f"wb{i}") for i in range(2)]
        S.sem_stg = [newsem(S, es, f"stg{i}") for i in range(4)]
        S.sem_ld2 = [newsem(S, es, f"ld2{i}") for i in range(2)]
        S.sem_c = newsem(S, es, "const")
        S.sem_out = newsem(S, es, "outs")
        S.sem_w2 = newsem(S, es, "w2")
        S.sem_ex = newsem(S, es, "ex")
        S.sem_cc = newsem(S, es, "cc")
        S.sem_ld3 = newsem(S, es, "ld3")
        S.sem_ld4 = newsem(S, es, "ld4")
        S.sem_ln = [newsem(S, es, f"ln{i}") for i in range(6)]
        S.ident = sb(S, es, [128, 128], F32, "ident")
        S.identb = sb(S, es, [128, 128], BF16, "identb")
        S.pswap = sb(S, es, [128, 128], F32, "pswap")
        S.sel4 = sb(S, es, [128, 128], F32, "sel4")
        S.onesb = sb(S, es, [128, 128], BF16, "onesb")
        S.onesf = sb(S, es, [128, 128], F32, "onesf")
        S.cosT = sb(S, es, [128, T], F32, "cosT")
        S.sinT = sb(S, es, [128, T], F32, "sinT")
        S.bm = [sb(S, es, [128, 96], F32, "bm") for _ in range(L)]
        S.gq = [sb(S, es, [128, 12], F32, "gq") for _ in range(L)]
        S.gkv = [sb(S, es, [128, 4], F32, "gkv") for _ in range(L)]
        S.subc = [sb(S, es, [128, 1], F32, "subc") for _ in range(L)]
        S.neglam = [sb(S, es, [128, 1], F32, "neglam") for _ in range(L)]
        S.rowsum = sb(S, es, [128, T // 128, 8], F32, "rowsum")
        lamt = sb(S, es, [128, 4, 64], F32, "lamt")
        lamw = sb(S, es, [128, 8], F32, "lamw")
        P = Prog(nc, S)
        P.dma("sync", S.ident[:], S.identD[:, :], (), S.sem_c)
        P.dma("sync", S.pswap[:], S.pswapD[:, :], (), S.sem_c)
        P.dma("sync", S.sel4[:], S.sel4D[:, :], (), S.sem_c)
        P.dma("sync", S.cosT[:], S.cosD[:, :], (), S.sem_c)
        P.dma("sync", S.sinT[:], S.sinD[:, :], (), S.sem_c)
        cev = None
        for l in range(L):
            P.dma("sync", S.bm[l][:], S.bmD[l], (), S.sem_c)
            P.dma("sync", S.gq[l][:], S.gqD[l], (), S.sem_c)
            P.dma("sync", S.gkv[l][:], S.gkvD[l], (), S.sem_c)
            cev = P.dma("sync", S.subc[l][:], S.sublnD[l], (), S.sem_c)
        e0 = P.op("dve", lambda e: e.tensor_copy(S.identb[:], S.ident[:]), waits=[cev])
        P.op("dve", lambda e: e.memset(S.onesb[:], 1.0))
        e1 = P.op("dve", lambda e: e.memset(S.onesf[:], 1.0))
        S.const_ready = e1
        ev = e1
        for l in range(L):
            lam_init = 0.8 - 0.6 * math.exp(-0.3 * l)
            lev = P.dma("sync", lamt[:], S.lamD[l], [ev], S.sem_c)
            a = P.op("dve", lambda e: e.tensor_tensor(lamt[:, 0, :], lamt[:, 0, :], lamt[:, 1, :], ALU.mult), waits=[lev])
            a = P.op("dve", lambda e: e.tensor_tensor(lamt[:, 2, :], lamt[:, 2, :], lamt[:, 3, :], ALU.mult), waits=[a])
            a = P.op("dve", lambda e: e.reduce_sum(lamw[:, 0:1], lamt[:, 0, :], axis=mybir.AxisListType.X), waits=[a])
            a = P.op("dve", lambda e: e.reduce_sum(lamw[:, 1:2], lamt[:, 2, :], axis=mybir.AxisListType.X), waits=[a])
            b = P.op("act", lambda e: e.activation(out=lamw[:, 2:4], in_=lamw[:, 0:2], func=AF.Exp), waits=[a])
            a = P.op("dve", lambda e: e.tensor_tensor(lamw[:, 4:5], lamw[:, 3:4], lamw[:, 2:3], ALU.subtract), waits=[b])
            a = P.op("dve", lambda e, l=l, li=lam_init: e.tensor_scalar(S.neglam[l][:], lamw[:, 4:5], -li, None, ALU.add), waits=[a])
            a = P.op("dve", lambda e, l=l, li=lam_init: e.tensor_scalar(S.subc[l][:], S.subc[l][:], 1.0 - li, None, ALU.mult), waits=[a])
            ev = a
        S.lam_ready = ev
        P.emit()

        for l in range(nlayers):
            xsrc = S.x if l == 0 else S.dram["x1"]
            xdst = S.dram["x1"] if l < L - 1 else S.out
            for tb in range(2):
                phase_inproj(S, l, tb, xsrc)
            if stop_after == "inproj":
                break
            phase_mla_proj(S, l)
            if stop_after == "mla_proj":
                break
            phase_na(S, l)
            if stop_after == "na":
                break
            phase_dense(S, l, "mla")
            if stop_after == "mla":
                break
            phase_dense(S, l, "diff")
            if stop_after == "diff":
                break
            for tb in range(2):
                phase_outproj(S, l, tb, xsrc)
            if stop_after == "outproj":
                break
            phase_ln(S, l, xsrc, xdst)

        if stop_after is not None or nlayers < L:
            P = Prog(nc, S)
            with ExitStack() as es2:
                o = sb(S, es2, [128, 512], F32, "dummy")
                ev = P.op("dve", lambda e: e.memset(o[:], 0.0))
                ev = P.dma("sync", S.out[0:128, 0:512], o[:], [ev], S.sem_out)
                P.emit()
    return nc


def rope_tables(hf):
    half = 32
    inv = (10000.0 ** (-np.arange(half, dtype=np.float32) * 2.0 / 64)).astype(np.float32)
    pos = (np.arange(T) + hf * T).astype(np.float32)
    ang = pos[:, None] * inv[None, :]
    c = np.cos(ang).T.astype(np.float32)
    s = np.sin(ang).T.astype(np.float32)
    cosT = np.concatenate([c, c, c, c], axis=0)
    sinT = np.concatenate([-s, s, -s, s], axis=0)
    return np.ascontiguousarray(cosT), np.ascontiguousarray(sinT)


def na_bias_table(rpb, hf):
    out = np.full((5, 7, 8, 128, 128), NEG, np.float32)
    reps = [0, 1, 5, 14, 15]
    kl = np.arange(128)
    krl, wk = kl // 64, kl % 64
    ql = np.arange(128)
    qrl, wq = ql // 64, ql % 64
    for t, j in enumerate(reps):
        gj = hf * 16 + j
        for s in range(7):
            gk = gj + s - 3
            krow = 2 * gk + krl
            qrow = 2 * gj + qrl
            start = np.clip(qrow - 4, 0, 56)
            cstart = np.clip(wq - 8, 0, 48)
            valid = ((krow[:, None] >= 0) & (krow[:, None] < 64)
                     & (krow[:, None] >= start[None, :]) & (krow[:, None] < start[None, :] + 8)
                     & (wk[:, None] >= cstart[None, :]) & (wk[:, None] < cstart[None, :] + 16))
            dr = np.clip(krow[:, None] - qrow[None, :] + 7, 0, 14)
            dc = np.clip(wk[:, None] - wq[None, :] + 15, 0, 30)
            vals = rpb[:, dr, dc]
            out[t, s] = np.where(valid[None], vals, np.float32(NEG))
    return np.ascontiguousarray(out.reshape(35, 8, 128, 128).transpose(1, 2, 0, 3))


def make_in_maps(inputs):
    f = lambda k: np.ascontiguousarray(np.asarray(inputs[k], dtype=np.float32))
    x = f("x")
    shared = {
        "w_in": f("w_in"), "w_uq": f("w_uq"), "w_ukv": f("w_ukv"),
        "w_o_a": f("w_o_a"), "w_o_b": f("w_o_b"), "w_o_c": f("w_o_c"), "w_out": f("w_out"),
        "ident": np.eye(128, dtype=np.float32),
        "pswap": np.ascontiguousarray(np.eye(128, dtype=np.float32)[np.arange(128) ^ 32]),
        "sel4": np.ascontiguousarray(np.broadcast_to((np.arange(128) % 32 == 0).astype(np.float32)[:, None], (128, 128))),
        "bm": np.ascontiguousarray(f("b_merge").reshape(L, 96, 128).transpose(0, 2, 1)),
        "gq": np.ascontiguousarray(f("q_norm").reshape(L, 12, 128).transpose(0, 2, 1)),
        "gkv": np.ascontiguousarray(f("kv_norm").reshape(L, 4, 128).transpose(0, 2, 1)),
        "subln": np.ascontiguousarray(f("diff_subln").reshape(L, 128, 1)),
        "lamrep": np.ascontiguousarray(np.broadcast_to(
            np.stack([f("lam_q1"), f("lam_k1"), f("lam_q2"), f("lam_k2")], axis=1)[:, None], (L, 128, 4, 64))),
        "lng": np.ascontiguousarray(np.broadcast_to(f("ln_g")[:, None, :], (L, 128, D))),
        "lnb": np.ascontiguousarray(np.broadcast_to(f("ln_b")[:, None, :], (L, 128, D))),
    }
    rpb = f("na_rpb")
    nab = [np.stack([na_bias_table(rpb[l], hf) for l in range(L)]) for hf in range(2)]
    tabs = [rope_tables(hf) for hf in range(2)]
    maps = []
    for c in range(8):
        b, hf = c // 2, c % 2
        m = dict(shared)
        m["x"] = np.ascontiguousarray(x[b, hf * T:(hf + 1) * T, :])
        m["cosT"], m["sinT"] = tabs[hf]
        m["nabias"] = nab[hf]
        maps.append(m)
    return maps


def kernel(**inputs):
    nc = build()
    res = run_bass_kernel_spmd(nc, make_in_maps(inputs), core_ids=list(range(8)))
    full = np.zeros((4, SEQ, D), np.float32)
    for c in range(8):
        full[c // 2, (c % 2) * T:(c % 2 + 1) * T, :] = res.results[c]["out"]
    return full
```

```python
import math
from contextlib import ExitStack
import numpy as np
import concourse.bass as bass
import concourse.mybir as mybir
from concourse.bass_utils import run_bass_kernel_spmd

F32 = mybir.dt.float32
BF16 = mybir.dt.bfloat16
AF = mybir.ActivationFunctionType
ALU = mybir.AluOpType

D = 4096
T = 2048
SEQ = 4096
L = 2
INW = 23616
NEG = -30000.0
LN_EPS = 1e-5
RMS_EPS = 1e-6
ALPHA = (2.0 * L) ** 0.25
PAIRS = [[0, 1], [2, 3], [4, 5], [6, 7]]

SEGS = [
    ("qA", 0, 1024, "fm", "copy", 128 ** -0.5),
    ("kA", 1024, 1024, "fm", "copy", 1.0),
    ("vA", 2048, 1024, "tm", None, 1.0),
    ("gA", 3072, 1024, "fm", "silu", 1.0),
    ("cq", 4096, 1536, "fm", "copy", 1.0),
    ("ckv", 5632, 512, "fm", "copy", 1.0),
    ("kr", 6144, 64, "rope", None, 1.0),
    ("gB", 6208, 1024, "fm", "silu", 1.0),
    ("qC", 7232, 1024, "rope", None, 0.125),
    ("kC", 8256, 1024, "rope", None, 1.0),
    ("vC", 9280, 1024, "tm", None, 1.0),
    ("gC", 10304, 1024, "fm", "silu", 1.0),
    ("gm", 11328, 12288, "fm", "sigmoid", 1.0),
]


class Sem:
    __slots__ = ("h", "v")

    def __init__(self, h):
        self.h = h
        self.v = 0


class Buf:
    __slots__ = ("t", "wr", "rd")

    def __init__(self, t):
        self.t = t
        self.wr = None
        self.rd = []

    def ww(self):
        if self.rd:
            return list(self.rd)
        return [self.wr] if self.wr is not None else []

    def wrote(self, ev):
        self.wr = ev
        self.rd = []

    def rw(self):
        return [self.wr] if self.wr is not None else []

    def read(self, ev):
        self.rd.append(ev)


class State:
    pass


ENGS = ("sync", "act", "dve", "pool", "pe")
ENGMAP = {"sync": "sync", "act": "scalar", "dve": "vector", "pool": "gpsimd", "pe": "tensor"}


class Prog:
    def __init__(self, nc, S):
        self.nc = nc
        self.S = S
        self.q = {e: [] for e in ENGS}

    def add(self, eng, fn, waits=(), inc=None, amt=1):
        ws = []
        seen = self.S.seen[eng]
        for w in waits:
            if w is None:
                continue
            s, v = w
            if seen.get(s, 0) >= v:
                continue
            seen[s] = v
            ws.append((s, v))
        ev = None
        if inc is not None:
            inc.v += amt
            ev = (inc, inc.v)
        self.q[eng].append((fn, ws, inc, amt))
        return ev

    def op(self, eng, fn, waits=(), sig=True):
        return self.add(eng, fn, waits, self.S.esem[eng] if sig else None, 1)

    def dma(self, eng, out, in_, waits, sem):
        return self.add(eng, lambda e: e.dma_start(out=out, in_=in_), waits, sem, 16)

    def join(self):
        allsems = [s for s in self.S.allsems if s.v > 0]
        for e in ENGS:
            self.add(e, None, [(s, s.v) for s in allsems])

    def emit(self):
        self.join()
        with self.nc.Block() as block:
            for e in ENGS:
                items = self.q[e]

                def body(eng, items=items):
                    for fn, ws, inc, amt in items:
                        for s, v in ws:
                            eng.wait_ge(s.h, v)
                        if fn is None:
                            continue
                        ins = fn(eng)
                        if inc is not None:
                            ins.then_inc(inc.h, amt)

                getattr(block, ENGMAP[e])(body)


def newsem(S, es, name):
    s = Sem(es.enter_context(S.nc.semaphore(name)))
    S.allsems.append(s)
    return s


def sb(S, es, shape, dt, name):
    S.uid += 1
    return es.enter_context(S.nc.sbuf_tensor(f"{name}_{S.uid}", shape, dt))


def psum_bufs(S, es, n=8):
    out = []
    for i in range(n):
        S.uid += 1
        out.append(Buf(es.enter_context(S.nc.psum_tensor(f"ps_{S.uid}", [128, 512], F32))))
    return out


def inproj_groups():
    groups = []
    for name, col0, width, kind, func, scale in SEGS:
        for c in range(0, width, 512):
            w = min(512, width - c)
            if kind == "tm":
                jobs = [dict(kind="tm", dest=name, dcol0=c, n=w)]
            elif kind == "rope":
                jobs = [dict(kind="rope", wcol=j, n=min(128, w - j), scale=scale, dest=name, row0=c + j)
                        for j in range(0, w, 128)]
            else:
                jobs = [dict(kind="fm", wcol=j, n=128, func=func, scale=scale, dest=name,
                             row0=c + j) for j in range(0, w, 128)]
            groups.append(dict(col0=col0 + c, w=w, jobs=jobs))
    return groups


def phase_inproj(S, l, tb, xsrc):
    nc = S.nc
    P = Prog(nc, S)
    w_in = S.w_in[l]
    with ExitStack() as es:
        xT = sb(S, es, [128, 32, 1024], BF16, "xT")
        xs = [Buf(sb(S, es, [128, 2048], F32, "xs")) for _ in range(4)]
        wb = [Buf(sb(S, es, [128, 32, 512], BF16, "wb")) for _ in range(2)]
        stg = [Buf(sb(S, es, [128, 1024], BF16, "stg")) for _ in range(4)]
        rt = [sb(S, es, [128, 512], F32, "rt") for _ in range(2)]
        rtA = [Buf(sb(S, es, [128, 512], F32, "rtA")) for _ in range(3)]
        ps = psum_bufs(S, es)
        psi = [0]
        pending = []
        rai = [0]
        rope_dve = [None]

        def nextps():
            b = ps[psi[0] % 8]
            psi[0] += 1
            return b

        def flush():
            while pending:
                job, tc, ta, st, si_ = pending.pop(0)
                n = job["n"]
                sc = job["scale"]
                t0c = tb * 1024 + tc * 512
                pb2 = nextps()
                lp2 = P.op("pe", lambda e, o=pb2.t[0:n, :], r=ta.t[0:n, :], n=n: e.matmul(o, S.pswap[0:n, 0:n], r, start=True, stop=True),
                           waits=pb2.ww() + ta.rw())
                pb2.wrote(lp2)
                e1 = P.op("dve", lambda e, o=rt[0][0:n, :], a=ta.t[0:n, :], c=S.cosT[0:n, t0c:t0c + 512], sc=sc:
                          e.scalar_tensor_tensor(o, a, sc, c, ALU.mult, ALU.mult),
                          waits=ta.rw() + ([rope_dve[0]] if rope_dve[0] else []))
                e2 = P.op("dve", lambda e, o=rt[1][0:n, :], a=pb2.t[0:n, :], c=S.sinT[0:n, t0c:t0c + 512], sc=sc:
                          e.scalar_tensor_tensor(o, a, sc, c, ALU.mult, ALU.mult), waits=pb2.rw())
                pb2.read(e2)
                ta.read(lp2)
                ta.read(e1)
                e3 = P.op("dve", lambda e, o=st.t[0:n, tc * 512:(tc + 1) * 512], a=rt[0][0:n, :], c=rt[1][0:n, :]:
                          e.tensor_tensor(o, a, c, ALU.add), waits=[e1, e2] + (st.ww() if tc == 0 else []))
                rope_dve[0] = e3
                if tc == 1:
                    st.wrote(e3)
                    dest = S.dram[job["dest"]]
                    ev = P.dma("sync", dest[job["row0"]:job["row0"] + n, tb * 1024:(tb + 1) * 1024],
                               st.t[0:n, :], st.rw(), S.sem_stg[si_ % 4])
                    st.read(ev)

        xT_events = []
        k = 0
        for tt in range(8):
            r0 = tb * 1024 + tt * 128
            for hf in range(2):
                xb = xs[k % 4]
                k += 1
                ev = P.dma("sync" if k % 2 else "act", xb.t[:], xsrc[r0:r0 + 128, hf * 2048:(hf + 1) * 2048], xb.ww(),
                           S.sem_xs[(k - 1) % 4])
                xb.wrote(ev)
                for g in range(4):
                    pb = nextps()
                    for i in range(4):
                        lastev = P.op("pe", (lambda e, o=pb.t[:, i * 128:(i + 1) * 128],
                                             a=xb.t[:, (g * 4 + i) * 128:(g * 4 + i + 1) * 128]:
                                             e.transpose(o, a, S.ident[:])),
                                      waits=(pb.ww() + xb.rw()) if i == 0 else (), sig=(i == 3))
                    pb.wrote(lastev)
                    kc0 = hf * 16 + g * 4
                    o = xT[:, kc0:kc0 + 4, tt * 128:(tt + 1) * 128]
                    a = pb.t[:, :].rearrange("p (a b) -> p a b", a=4)
                    eng = "act" if (g % 2 == 0) else "dve"
                    if eng == "act":
                        ev2 = P.op("act", lambda e, o=o, a=a: e.copy(o, a), waits=pb.rw())
                    else:
                        ev2 = P.op("dve", lambda e, o=o, a=a: e.tensor_copy(o, a), waits=pb.rw())
                    pb.read(ev2)
                    xT_events.append(ev2)
                xb.read(lastev)
        xT_ready = []
        for en in ("act", "dve"):
            evs = [e for e in xT_events if e[0] is S.esem[en]]
            xT_ready.append(max(evs, key=lambda t: t[1]))

        groups = inproj_groups()

        def load_group(gi):
            g = groups[gi]
            b = wb[gi % 2]
            src = w_in[:, g["col0"]:g["col0"] + g["w"]].rearrange("(kc p) n -> p kc n", p=128)
            ww = b.ww()
            ev = None
            for i in range(4):
                ev = P.dma("pool", b.t[:, 8 * i:8 * i + 8, 0:g["w"]], src[:, 8 * i:8 * i + 8, :], ww,
                           S.sem_wb[gi % 2])
            b.wrote(ev)

        load_group(0)
        load_group(1)
        si = 0
        for gi, g in enumerate(groups):
            b = wb[gi % 2]
            lastpe = None
            for job in g["jobs"]:
                dest = S.dram[job["dest"]]
                if job["kind"] == "tm":
                    n = job["n"]
                    for tt in range(8):
                        pb = nextps()
                        for kc in range(32):
                            lastpe = P.op("pe", (lambda e, o=pb.t[:, 0:n], a=xT[:, kc, tt * 128:(tt + 1) * 128],
                                                 r=b.t[:, kc, 0:n], kc=kc:
                                                 e.matmul(o, a, r, start=(kc == 0), stop=(kc == 31))),
                                          waits=(pb.ww() + b.rw() + xT_ready) if kc == 0 else (),
                                          sig=(kc == 31))
                        pb.wrote(lastpe)
                        flush()
                        st = stg[si % 4]
                        si += 1
                        ev = P.op("dve", lambda e, o=st.t[:, 0:n], a=pb.t[:, 0:n]: e.tensor_copy(o, a),
                                  waits=pb.rw() + st.ww())
                        pb.read(ev)
                        st.wrote(ev)
                        r0 = tb * 1024 + tt * 128
                        ev = P.dma("sync", dest[r0:r0 + 128, job["dcol0"]:job["dcol0"] + n], st.t[:, 0:n],
                                   st.rw(), S.sem_stg[(si - 1) % 4])
                        st.read(ev)
                elif job["kind"] == "fm":
                    n = job["n"]
                    st = stg[si % 4]
                    si += 1
                    evs = []
                    for tc in range(2):
                        pb = nextps()
                        for kc in range(32):
                            lastpe = P.op("pe", (lambda e, o=pb.t[0:n, :], a=b.t[:, kc, job["wcol"]:job["wcol"] + n],
                                                 r=xT[:, kc, tc * 512:(tc + 1) * 512], kc=kc:
                                                 e.matmul(o, a, r, start=(kc == 0), stop=(kc == 31))),
                                          waits=(pb.ww() + b.rw() + xT_ready) if kc == 0 else (),
                                          sig=(kc == 31))
                        pb.wrote(lastpe)
                        flush()
                        o = st.t[0:n, tc * 512:(tc + 1) * 512]
                        a = pb.t[0:n, :]
                        f = job["func"]
                        if f == "copy":
                            if job["scale"] == 1.0:
                                fn = lambda e, o=o, a=a: e.copy(o, a)
                            else:
                                fn = lambda e, o=o, a=a, s=job["scale"]: e.mul(o, a, s)
                        elif f == "silu":
                            fn = lambda e, o=o, a=a: e.activation(out=o, in_=a, func=AF.Silu)
                        else:
                            bidx = job["row0"] // 128
                            fn = lambda e, o=o, a=a, bi=bidx: e.activation(
                                out=o, in_=a, func=AF.Sigmoid, bias=S.bm[l][:, bi:bi + 1], scale=1.0)
                        ev = P.op("act", fn, waits=pb.rw() + (st.ww() if tc == 0 else []))
                        pb.read(ev)
                        evs.append(ev)
                    st.wrote(evs[-1])
                    ev = P.dma("sync", dest[job["row0"]:job["row0"] + n, tb * 1024:(tb + 1) * 1024],
                               st.t[0:n, :], st.rw(), S.sem_stg[(si - 1) % 4])
                    st.read(ev)
                else:
                    n = job["n"]
                    st = stg[si % 4]
                    si += 1
                    for tc in range(2):
                        pb = nextps()
                        for kc in range(32):
                            lastpe = P.op("pe", (lambda e, o=pb.t[0:n, :], a=b.t[:, kc, job["wcol"]:job["wcol"] + n],
                                                 r=xT[:, kc, tc * 512:(tc + 1) * 512], kc=kc:
                                                 e.matmul(o, a, r, start=(kc == 0), stop=(kc == 31))),
                                          waits=(pb.ww() + b.rw() + xT_ready) if kc == 0 else (),
                                          sig=(kc == 31))
                        pb.wrote(lastpe)
                        flush()
                        ta = rtA[rai[0] % 3]
                        rai[0] += 1
                        ev = P.op("act", lambda e, o=ta.t[0:n, :], a=pb.t[0:n, :]: e.copy(o, a), waits=pb.rw() + ta.ww())
                        pb.read(ev)
                        ta.wrote(ev)
                        pending.append((job, tc, ta, st, si - 1))
            b.read(lastpe)
            if gi + 2 < len(groups):
                load_group(gi + 2)
        flush()
        P.emit()


def rstd_from_ps(P, S, psb, n_inv, eps, tmp, out, extra_waits=()):
    e1 = P.op("dve", lambda e: e.tensor_scalar(tmp, psb.t[:, :], n_inv, eps, ALU.mult, ALU.add),
              waits=psb.rw() + list(extra_waits))
    psb.read(e1)
    e2 = P.op("act", lambda e: e.sqrt(tmp, tmp), waits=[e1])
    e3 = P.op("dve", lambda e: e.reciprocal(out, tmp), waits=[e2])
    return e3


def phase_mla_proj(S, l):
    nc = S.nc
    P = Prog(nc, S)
    with ExitStack() as es:
        wuq = sb(S, es, [128, 12, 1536], BF16, "wuq")
        wuqs = sb(S, es, [128, 12, 512], BF16, "wuqs")
        wukv = sb(S, es, [128, 4, 2048], BF16, "wukv")
        cqc = [Buf(sb(S, es, [128, 12, 512], BF16, "cqc")) for _ in range(2)]
        ckc = [Buf(sb(S, es, [128, 4, 512], BF16, "ckc")) for _ in range(2)]
        sq = sb(S, es, [128, 16, 512], BF16, "sq")
        cqn = sb(S, es, [128, 12, 512], BF16, "cqn")
        ckn = sb(S, es, [128, 4, 512], BF16, "ckn")
        tmpq = sb(S, es, [128, 512], F32, "tmpq")
        tmpk = sb(S, es, [128, 512], F32, "tmpk")
        rq = sb(S, es, [128, 512], F32, "rq")
        rk = sb(S, es, [128, 512], F32, "rk")
        stg = [Buf(sb(S, es, [128, 512], BF16, "stg2")) for _ in range(4)]
        rt = [sb(S, es, [128, 512], F32, "rt2") for _ in range(2)]
        ps = psum_bufs(S, es)
        psi = [0]

        def nextps():
            b = ps[psi[0] % 8]
            psi[0] += 1
            return b

        wev = None
        srcq = S.w_uq[l].rearrange("(rc p) n -> p rc n", p=128)
        for i in range(3):
            wev = P.dma("pool", wuq[:, 4 * i:4 * i + 4, :], srcq[:, 4 * i:4 * i + 4, :], (), S.sem_w2)
        srck = S.w_ukv[l].rearrange("(rc p) n -> p rc n", p=128)
        wev = P.dma("pool", wukv[:, :, :], srck, (), S.sem_w2)
        sview = wuq[:, :, :].rearrange("p k (h c) -> p k h c", h=8)
        dview = wuqs[:, :, :].rearrange("p k (h c) -> p k h c", h=8)
        P.op("pool", lambda e: e.tensor_copy(dview[:, :, :, 0:32], sview[:, :, :, 160:192]), waits=[wev])
        wsw = P.op("pool", lambda e: e.tensor_copy(dview[:, :, :, 32:64], sview[:, :, :, 128:160]))
        wready = [wev, wsw]
        exchange_a(P, S)

        si = 0
        prev_norm_reads = []
        for tc in range(4):
            t0 = tc * 512
            cb = cqc[tc % 2]
            kb = ckc[tc % 2]
            ev = P.dma("sync", cb.t[:], S.dram["cq"][:, t0:t0 + 512].rearrange("(rc p) t -> p rc t", p=128),
                       cb.ww(), S.sem_ld2[tc % 2])
            ev = P.dma("sync", kb.t[:], S.dram["ckv"][:, t0:t0 + 512].rearrange("(rc p) t -> p rc t", p=128),
                       kb.ww(), S.sem_ld2[tc % 2])
            cb.wrote(ev)
            kb.wrote(ev)
            e_sq1 = P.op("act", lambda e, a=cb.t[:]: e.square(sq[:, 0:12, :], a), waits=cb.rw() + prev_norm_reads)
            e_sq2 = P.op("act", lambda e, a=kb.t[:]: e.square(sq[:, 12:16, :], a), waits=kb.rw())
            pa = nextps()
            for i in range(12):
                lp = P.op("pe", lambda e, o=pa.t[:, :], r=sq[:, i, :], i=i: e.matmul(o, S.onesb[:], r, start=(i == 0), stop=(i == 11)),
                          waits=(pa.ww() + [e_sq1, S.const_ready]) if i == 0 else (), sig=(i == 11))
            pa.wrote(lp)
            pk = nextps()
            for i in range(4):
                lp = P.op("pe", lambda e, o=pk.t[:, :], r=sq[:, 12 + i, :], i=i: e.matmul(o, S.onesb[:], r, start=(i == 0), stop=(i == 3)),
                          waits=(pk.ww() + [e_sq2]) if i == 0 else (), sig=(i == 3))
            pk.wrote(lp)
            sq_read = lp
            e_rq = rstd_from_ps(P, S, pa, 1.0 / 1536, RMS_EPS, tmpq[:], rq[:], extra_waits=prev_norm_reads)
            e_rk = rstd_from_ps(P, S, pk, 1.0 / 512, RMS_EPS, tmpk[:], rk[:])
            evn = []
            for i in range(12):
                eng = "dve"
                evn.append(P.op(eng, lambda e, o=cqn[:, i, :], a=cb.t[:, i, :], g=S.gq[l][:, i:i + 1]:
                                e.scalar_tensor_tensor(o, a, g, rq[:], ALU.mult, ALU.mult),
                                waits=[e_rq] + cb.rw() + prev_norm_reads))
            for i in range(4):
                eng = "dve"
                evn.append(P.op(eng, lambda e, o=ckn[:, i, :], a=kb.t[:, i, :], g=S.gkv[l][:, i:i + 1]:
                                e.scalar_tensor_tensor(o, a, g, rk[:], ALU.mult, ALU.mult),
                                waits=[e_rk] + kb.rw() + prev_norm_reads))
            nready = [evn[-1], evn[-2], evn[11], evn[10]]
            cb.read(evn[11]); cb.read(evn[10]); kb.read(evn[-1]); kb.read(evn[-2])
            lastpe = None
            for h in range(8):
                pb = nextps()
                for rc in range(12):
                    lastpe = P.op("pe", lambda e, o=pb.t[:, :], a=wuq[:, rc, h * 192:h * 192 + 128], r=cqn[:, rc, :], rc=rc:
                                  e.matmul(o, a, r, start=(rc == 0), stop=(rc == 11)),
                                  waits=(pb.ww() + nready + wready) if rc == 0 else (), sig=(rc == 11))
                pb.wrote(lastpe)
                st = stg[si % 4]; si += 1
                ev = P.op("act", lambda e, o=st.t[:, :], a=pb.t[:, :]: e.mul(o, a, 192 ** -0.5), waits=pb.rw() + st.ww())
                pb.read(ev); st.wrote(ev)
                ev = P.dma("sync", S.dram["qBn"][h * 128:(h + 1) * 128, t0:t0 + 512], st.t[:, :], st.rw(), S.sem_stg[(si - 1) % 4])
                st.read(ev)
                pbs = []
                for which in range(2):
                    pb = nextps()
                    for rc in range(12):
                        a = wuq[:, rc, h * 192 + 128:h * 192 + 192] if which == 0 else wuqs[:, rc, h * 64:(h + 1) * 64]
                        lastpe = P.op("pe", lambda e, o=pb.t[0:64, :], a=a, r=cqn[:, rc, :], rc=rc:
                                      e.matmul(o, a, r, start=(rc == 0), stop=(rc == 11)),
                                      waits=(pb.ww() + nready + wready) if rc == 0 else (), sig=(rc == 11))
                    pb.wrote(lastpe)
                    pbs.append(pb)
                sc = 192 ** -0.5
                tg = T0 = t0
                e1 = P.op("dve", lambda e, a=pbs[0].t[0:64, :], c=S.cosT[0:64, tg:tg + 512]:
                          e.scalar_tensor_tensor(rt[0][0:64, :], a, sc, c, ALU.mult, ALU.mult), waits=pbs[0].rw())
                pbs[0].read(e1)
                e2 = P.op("dve", lambda e, a=pbs[1].t[0:64, :], c=S.sinT[0:64, tg:tg + 512]:
                          e.scalar_tensor_tensor(rt[1][0:64, :], a, sc, c, ALU.mult, ALU.mult), waits=pbs[1].rw())
                pbs[1].read(e2)
                st = stg[si % 4]; si += 1
                ev = P.op("dve", lambda e, o=st.t[0:64, :]: e.tensor_tensor(o, rt[0][0:64, :], rt[1][0:64, :], ALU.add),
                          waits=[e1, e2] + st.ww())
                st.wrote(ev)
                ev = P.dma("sync", S.dram["qBr"][h * 64:(h + 1) * 64, t0:t0 + 512], st.t[0:64, :], st.rw(), S.sem_stg[(si - 1) % 4])
                st.read(ev)
            for h in range(8):
                pb = nextps()
                for rc in range(4):
                    lastpe = P.op("pe", lambda e, o=pb.t[:, :], a=wukv[:, rc, h * 256:h * 256 + 128], r=ckn[:, rc, :], rc=rc:
                                  e.matmul(o, a, r, start=(rc == 0), stop=(rc == 3)),
                                  waits=(pb.ww() + nready + wready) if rc == 0 else (), sig=(rc == 3))
                pb.wrote(lastpe)
                st = stg[si % 4]; si += 1
                ev = P.op("act", lambda e, o=st.t[:, :], a=pb.t[:, :]: e.copy(o, a), waits=pb.rw() + st.ww())
                pb.read(ev); st.wrote(ev)
                ev = P.dma("sync", S.dram["kBn"][h * 128:(h + 1) * 128, t0:t0 + 512], st.t[:, :], st.rw(), S.sem_stg[(si - 1) % 4])
                st.read(ev)
            wv = wukv[:, :, :].rearrange("p k (h c) -> p k h c", h=8)
            for tt in range(4):
                for half in range(2):
                    pb = nextps()
                    for rc in range(4):
                        lastpe = P.op("pe", lambda e, o=pb.t[:, :].rearrange("p (h c) -> p h c", h=4),
                                      a=ckn[:, rc, tt * 128:(tt + 1) * 128], r=wv[:, rc, half * 4:(half + 1) * 4, 128:256], rc=rc:
                                      e.matmul(o, a, r, start=(rc == 0), stop=(rc == 3)),
                                      waits=(pb.ww() + nready + wready) if rc == 0 else (), sig=(rc == 3))
                    pb.wrote(lastpe)
                    st = stg[si % 4]; si += 1
                    ev = P.op("dve", lambda e, o=st.t[:, :], a=pb.t[:, :]: e.tensor_copy(o, a), waits=pb.rw() + st.ww())
                    pb.read(ev); st.wrote(ev)
                    r0 = t0 + tt * 128
                    ev = P.dma("sync", S.dram["vB"][r0:r0 + 128, half * 512:(half + 1) * 512], st.t[:, :], st.rw(), S.sem_stg[(si - 1) % 4])
                    st.read(ev)
            prev_norm_reads = [lastpe, sq_read]
        P.emit()


def emit_collectives(P, S, pairs, waits):
    first = True
    for a, o in pairs:
        P.add("pool", lambda e, a=a, o=o: e.collective_compute(
            "AllGather", ALU.bypass, replica_groups=PAIRS, ins=[a.opt()], outs=[o.opt()]),
            waits=waits if first else (), inc=S.sem_cc, amt=1)
        first = False


def exchange_a(P, S):
    d = S.dram
    e1 = P.dma("sync", d["kAh"][:, 0:384], d["kA"][:, 0:384], (), S.sem_ex)
    e1 = P.dma("sync", d["kAh"][:, 384:768], d["kA"][:, T - 384:T], (), S.sem_ex)
    e1 = P.dma("sync", d["vAh"][0:384, :], d["vA"][0:384, :], (), S.sem_ex)
    e1 = P.dma("sync", d["vAh"][384:768, :], d["vA"][T - 384:T, :], (), S.sem_ex)
    pairs = [(d["kAh"], d["kAh_g"]), (d["vAh"], d["vAh_g"]), (d["kr"], d["kr_g"])]
    for i in range(2):
        pairs.append((d["kC"][i * 512:(i + 1) * 512, :], d[f"kC_g{i}"]))
        pairs.append((d["vC"][i * 1024:(i + 1) * 1024, :], d[f"vC_g{i}"]))
    emit_collectives(P, S, pairs, [e1])


def exchange_b(P, S):
    d = S.dram
    pairs = []
    for i in range(2):
        pairs.append((d["kBn"][i * 512:(i + 1) * 512, :], d[f"kBn_g{i}"]))
        pairs.append((d["vB"][i * 1024:(i + 1) * 1024, :], d[f"vB_g{i}"]))
    emit_collectives(P, S, pairs, [])


def na_type(j):
    return 0 if j == 0 else 1 if j == 1 else 3 if j == 14 else 4 if j == 15 else 2


def phase_na(S, l):
    nc = S.nc
    P = Prog(nc, S)
    d = S.dram
    with ExitStack() as es:
        KTs = [sb(S, es, [128, 22 * 128], BF16, "naK") for _ in range(2)]
        Vs = [sb(S, es, [128, 22, 128], BF16, "naV") for _ in range(2)]
        QTs = [sb(S, es, [128, T], BF16, "naQ") for _ in range(2)]
        GTs = [sb(S, es, [128, T], BF16, "naG") for _ in range(2)]
        BT = sb(S, es, [128, 8, 35, 128], BF16, "naB")
        PT = [Buf(sb(S, es, [128, 7 * 128], BF16, "naP")) for _ in range(2)]
        rec = sb(S, es, [128, 512], F32, "narec")
        tmp = sb(S, es, [128, 512], F32, "natmp")
        stg = [Buf(sb(S, es, [128, 512], BF16, "nastg")) for _ in range(2)]
        ps = psum_bufs(S, es)
        psS = [ps[0], ps[1], ps[2], ps[3]]
        psO = [ps[4], ps[5]]
        psR = [ps[6], ps[7]]
        evbs = []
        for h in range(8):
            evbs.append(P.dma("pool", BT[:, h, :, :], S.nabias[l][h], (), S.sem_nab[h]))
        exchange_b(P, S)
        head_done = {}
        loaded = {}

        def load(h):
            KT, V, QT, GT = KTs[h % 2], Vs[h % 2], QTs[h % 2], GTs[h % 2]
            ww = head_done.get(h - 2, [])
            sem = S.sem_ln[h % 2]
            r = slice(h * 128, (h + 1) * 128)
            ev = P.dma("sync", KT[:, 0:384], d["kAh_g"][h * 128:(h + 1) * 128, 384:768], ww, sem)
            ev = P.dma("sync", KT[:, 384:384 + T], d["kA"][r, :], ww, sem)
            ev = P.dma("sync", KT[:, 384 + T:768 + T], d["kAh_g"][1024 + h * 128:1024 + (h + 1) * 128, 0:384], ww, sem)
            ev = P.dma("sync", V[:, 0:3, :], d["vAh_g"][384:768, r].rearrange("(i p) c -> p i c", p=128), ww, sem)
            vav = d["vA"][:, r].rearrange("(i p) c -> p i c", p=128)
            ev = P.dma("sync", V[:, 3:11, :], vav[:, 0:8, :], ww, sem)
            ev = P.dma("sync", V[:, 11:19, :], vav[:, 8:16, :], ww, sem)
            ev = P.dma("sync", V[:, 19:22, :], d["vAh_g"][768:768 + 384, r].rearrange("(i p) c -> p i c", p=128), ww, sem)
            ev = P.dma("sync", QT[:, :], d["qA"][r, :], ww, sem)
            ev = P.dma("sync", GT[:, :], d["gA"][r, :], ww, sem)
            loaded[h] = ev

        load(0)
        load(1)
        last_evac = None
        pti = 0
        sti = 0
        gi = 0
        for h in range(8):
            KT, V, QT, GT = KTs[h % 2], Vs[h % 2], QTs[h % 2], GTs[h % 2]
            r = slice(h * 128, (h + 1) * 128)
            ready = [loaded[h], evbs[h], S.const_ready]
            lp = None
            for qg in range(4):
                pO = psO[gi % 2]
                pR = psR[gi % 2]
                gi += 1
                for jj in range(4):
                    j = qg * 4 + jj
                    ty = na_type(j)
                    slots = list(range(7)) if ty != 2 else [1, 2, 3, 4, 5]
                    ns = len(slots)
                    pt = PT[pti % 2]
                    p1 = psS[(pti % 2) * 2]
                    p2 = psS[(pti % 2) * 2 + 1]
                    pti += 1
                    for idx, s in enumerate(slots):
                        pb = p1 if idx < 4 else p2
                        c0 = (idx % 4) * 128
                        P.op("pe", lambda e, o=pb.t[:, c0:c0 + 128], a=KT[:, (j + s) * 128:(j + s + 1) * 128],
                             q=QT[:, j * 128:(j + 1) * 128]: e.matmul(o, a, q, start=True, stop=False),
                             waits=(pb.ww() + ready) if idx in (0, 4) else (), sig=False)
                        lp = P.op("pe", lambda e, o=pb.t[:, c0:c0 + 128], b=BT[:, h, ty * 7 + s, :]:
                                  e.matmul(o, S.identb[:], b, start=False, stop=True), sig=(idx in (3, ns - 1)))
                        if idx == 3:
                            p1.wrote(lp)
                        if idx == ns - 1 and idx != 3:
                            p2.wrote(lp)
                    e1 = P.op("act", lambda e, o=pt.t[:, 0:512], a=p1.t[:, :]: e.activation(out=o, in_=a, func=AF.Exp),
                              waits=p1.rw() + pt.ww())
                    p1.read(e1)
                    n2 = (ns - 4) * 128
                    e2 = P.op("act", lambda e, o=pt.t[:, 512:512 + n2], a=p2.t[:, 0:n2]: e.activation(out=o, in_=a, func=AF.Exp),
                              waits=p2.rw())
                    p2.read(e2)
                    pt.wrote(e2)
                    for idx, s in enumerate(slots):
                        P.op("pe", lambda e, o=pO.t[:, jj * 128:(jj + 1) * 128], a=V[:, j + s, :], p=pt.t[:, idx * 128:(idx + 1) * 128], idx=idx, ns=ns:
                             e.matmul(o, a, p, start=(idx == 0), stop=(idx == ns - 1)),
                             waits=(pt.rw() + pO.ww() + pR.ww()) if idx == 0 else (), sig=False)
                        lp = P.op("pe", lambda e, o=pR.t[:, jj * 128:(jj + 1) * 128], p=pt.t[:, idx * 128:(idx + 1) * 128], idx=idx, ns=ns:
                                  e.matmul(o, S.onesb[:], p, start=(idx == 0), stop=(idx == ns - 1)), sig=(idx == ns - 1))
                    pt.read(lp)
                pO.wrote(lp)
                pR.wrote(lp)
                ea = P.op("dve", lambda e, a=pR.t[:, :]: e.reciprocal(rec[:], a), waits=pR.rw() + ([last_evac] if last_evac else []))
                pR.read(ea)
                eb = P.op("dve", lambda e, a=pO.t[:, :]: e.tensor_tensor(tmp[:], a, rec[:], ALU.mult), waits=pO.rw() + [ea])
                pO.read(eb)
                st = stg[sti % 2]; sti += 1
                ec = P.op("dve", lambda e, o=st.t[:, :], g=GT[:, qg * 512:(qg + 1) * 512]: e.tensor_tensor(o, tmp[:], g, ALU.mult),
                          waits=[eb] + st.ww())
                st.wrote(ec)
                last_evac = ec
                ev = P.dma("sync", d["yA"][r, qg * 512:(qg + 1) * 512], st.t[:, :], st.rw(), S.sem_stg[(sti - 1) % 2])
                st.read(ev)
            head_done[h] = [lp, last_evac]
            if h + 2 < 8:
                load(h + 2)
        P.emit()


def phase_dense(S, l, kind):
    nc = S.nc
    P = Prog(nc, S)
    d = S.dram
    mla = (kind == "mla")
    nm = 1 if mla else 2
    with ExitStack() as es:
        KTs = [sb(S, es, [128, SEQ], BF16, "dK") for _ in range(2)]
        Vs = [sb(S, es, [128, 32, 128], BF16, "dV") for _ in range(2)]
        GTs = [sb(S, es, [128, T], BF16, "dG") for _ in range(2)]
        if mla:
            KR = sb(S, es, [128, SEQ], BF16, "dKr")
            QTs = [sb(S, es, [128, T], BF16, "dQ") for _ in range(2)]
            QRs = [sb(S, es, [128, T], BF16, "dQr") for _ in range(2)]
        else:
            QAs = [sb(S, es, [128, T], BF16, "dQa") for _ in range(2)]
            QBs = [sb(S, es, [128, T], BF16, "dQb") for _ in range(2)]
        NPT = 12
        PT = [Buf(sb(S, es, [128, 512], BF16, "dP")) for _ in range(NPT)]
        rsum = [sb(S, es, [128, 512], F32, "drsum") for _ in range(nm)]
        oraw = [sb(S, es, [128, 512], F32, "doraw") for _ in range(nm)]
        rec = sb(S, es, [128, 512], F32, "drec")
        yy = sb(S, es, [128, 512], F32, "dyy")
        sqy = sb(S, es, [128, 512], F32, "dsq")
        rs = sb(S, es, [128, 512], F32, "drs")
        stg = [Buf(sb(S, es, [128, 512], BF16, "dstg")) for _ in range(2)]
        ps = psum_bufs(S, es)
        psS = ps[0:4]
        psO = ps[4:6]
        psR = ps[6:8]
        zev = None
        kr_ev = None
        if mla:
            zev = P.op("dve", lambda e: e.memset(KR[:, :], 0.0))
            for i in range(2):
                zev = P.op("dve", lambda e, q=QRs[i]: e.memset(q[:, :], 0.0))
            kr_ev = P.dma("sync", KR[0:64, 0:T], d["kr_g"][0:64, :], [zev], S.sem_ld3)
            kr_ev = P.dma("sync", KR[0:64, T:SEQ], d["kr_g"][64:128, :], [zev], S.sem_ld3)
        else:
            for i in range(2):
                P.op("dve", lambda e, q=QAs[i]: e.memset(q[:, :], 0.0))
                zev = P.op("dve", lambda e, q=QBs[i]: e.memset(q[:, :], 0.0))
        kgs = [d["kBn_g0"], d["kBn_g1"]] if mla else [d["kC_g0"], d["kC_g1"]]
        vgs = [d["vB_g0"], d["vB_g1"]] if mla else [d["vC_g0"], d["vC_g1"]]
        qsrc = d["qBn"] if mla else d["qC"]
        gsrc = d["gB"] if mla else d["gC"]
        ydst = d["yB"] if mla else d["yC"]
        head_done = {}
        loaded = {}

        def load(h):
            b = h % 2
            r = slice(h * 128, (h + 1) * 128)
            ww = head_done.get(h - 2, []) + [zev]
            sem = S.sem_ln[b]
            kg = kgs[h // 4]
            hr = (h % 4) * 128
            ev = P.dma("sync", KTs[b][:, 0:T], kg[hr:hr + 128, :], ww, sem)
            ev = P.dma("sync", KTs[b][:, T:SEQ], kg[512 + hr:512 + hr + 128, :], ww, sem)
            for rk_ in range(2):
                for th in range(2):
                    vi = rk_ * 2 + th
                    ev = P.dma("sync", Vs[b][:, vi * 8:(vi + 1) * 8, :],
                               vgs[th][rk_ * 1024:(rk_ + 1) * 1024, r].rearrange("(i p) c -> p i c", p=128), ww, sem)
            ev = P.dma("sync", GTs[b][:, :], gsrc[r, :], ww, sem)
            if mla:
                ev = P.dma("sync", QTs[b][:, :], qsrc[r, :], ww, sem)
                ev = P.dma("sync", QRs[b][0:64, :], d["qBr"][h * 64:(h + 1) * 64, :], ww, sem)
            else:
                ev = P.dma("sync", QAs[b][0:64, :], qsrc[h * 128:h * 128 + 64, :], ww, sem)
                ev = P.dma("sync", QBs[b][64:128, :], qsrc[h * 128 + 64:h * 128 + 128, :], ww, sem)
            loaded[h] = ev

        load(0)
        load(1)
        epi_done = [None]
        deferred = []
        si = 0
        pi = 0
        sti = 0
        for h in range(8):
            b = h % 2
            KT, V, GT = KTs[b], Vs[b], GTs[b]
            r = slice(h * 128, (h + 1) * 128)
            ready = [loaded[h], S.const_ready] + ([kr_ev] if mla else [])
            lp = None
            for qc in range(4):
                q0 = qc * 512
                pend = []
                units = [(kt, m) for kt in range(32) for m in range(nm)]

                def issue_s(u):
                    nonlocal si, pi
                    kt, m = u
                    pb = psS[si % 4]; si += 1
                    if mla:
                        P.op("pe", lambda e, o=pb.t[:, :], a=KT[:, kt * 128:(kt + 1) * 128], q=QTs[b][:, q0:q0 + 512]:
                             e.matmul(o, a, q, start=True, stop=False), waits=pb.ww() + ready, sig=False)
                        lps = P.op("pe", lambda e, o=pb.t[:, :], a=KR[:, kt * 128:(kt + 1) * 128], q=QRs[b][:, q0:q0 + 512]:
                                   e.matmul(o, a, q, start=False, stop=True))
                    else:
                        lps = P.op("pe", lambda e, o=pb.t[:, :], a=KT[:, kt * 128:(kt + 1) * 128],
                                   q=(QAs[b] if m == 0 else QBs[b])[:, q0:q0 + 512]: e.matmul(o, a, q, start=True, stop=True),
                                   waits=pb.ww() + ready)
                    pb.wrote(lps)
                    pt = PT[pi % NPT]; pi += 1
                    ee = P.op("act", lambda e, o=pt.t[:, :], a=pb.t[:, :]: e.activation(out=o, in_=a, func=AF.Exp),
                              waits=pb.rw() + pt.ww())
                    pb.read(ee)
                    pt.wrote(ee)
                    pend.append(pt)

                LA = 3
                for u in units[:LA]:
                    issue_s(u)
                G = 4 * nm
                for gidx, g0 in enumerate(range(0, len(units), G)):
                    if gidx in ((2, 4) if mla else (1, 3, 5)) and deferred:
                        deferred.pop(0)()
                    grp = list(range(g0, g0 + G))
                    for ui in grp:
                        kt, m = units[ui]
                        pt = pend[ui]
                        pO = psO[m]
                        first = (kt == 0)
                        last = (kt == 31)
                        P.op("pe", lambda e, o=pO.t[:, :], a=V[:, kt, :], p=pt.t[:, :], first=first, last=last:
                             e.matmul(o, a, p, start=first, stop=last),
                             waits=pt.rw() + ((pO.ww() + psR[m].ww()) if first else []), sig=False)
                        if ui + LA < len(units):
                            issue_s(units[ui + LA])
                    for m in range(nm):
                        for ui in grp:
                            kt, mm = units[ui]
                            if mm != m:
                                continue
                            pt = pend[ui]
                            pR = psR[m]
                            j = kt % 4
                            lp = P.op("pe", lambda e, o=pR.t[32 * j:32 * j + 32, :], p=pt.t[:, :], kt=kt, j=j:
                                      e.matmul(o, S.onesb[:, 0:32], p, start=(kt < 4), stop=(kt >= 28), tile_position=(0, 32 * j)),
                                      sig=(j == 3))
                        for ui in grp:
                            if units[ui][1] == m:
                                pend[ui].read(lp)
                        if units[grp[-1]][0] == 31:
                            psO[m].wrote(lp)
                            psR[m].wrote(lp)
                pw = [epi_done[0]] if epi_done[0] else []
                cps = []
                for m in range(nm):
                    c1 = P.op("dve", lambda e, o=rsum[m][:], a=psR[m].t[:, :]: e.tensor_copy(o, a),
                              waits=psR[m].rw() + pw + ([lastpe_n[1]] if lastpe_n[1] else []))
                    psR[m].read(c1)
                    c2 = P.op("dve", lambda e, o=oraw[m][:], a=psO[m].t[:, :]: e.tensor_copy(o, a), waits=psO[m].rw() + pw)
                    psO[m].read(c2)
                    cps.append((c1, c2))
                state = {}

                def stage_a(m, cps=cps, state=state):
                    nonlocal si
                    outs = state.setdefault("outs", [])
                    c1, c2 = cps[m]
                    pn2 = psS[si % 4]; si += 1
                    l2 = P.op("pe", lambda e, o=pn2.t[:, :], rr=rsum[m][:]: e.matmul(o, S.sel4[:], rr, start=True, stop=True),
                              waits=pn2.ww() + [c1])
                    pn2.wrote(l2)
                    lastpe_n[1] = l2
                    c3 = P.op("dve", lambda e, a=pn2.t[:, :]: e.reciprocal(rec[:], a), waits=pn2.rw())
                    pn2.read(c3)
                    c4 = P.op("dve", lambda e, o=oraw[m][:]: e.tensor_tensor(o, o, rec[:], ALU.mult), waits=[c3, c2])
                    outs.append(c4)
                    if (not mla) and m == 1:
                        e3 = P.op("dve", lambda e: e.scalar_tensor_tensor(yy[:], oraw[1][:], S.neglam[l][:, 0:1], oraw[0][:], ALU.mult, ALU.add),
                                  waits=[outs[1], S.lam_ready])
                        e4 = P.op("pool", lambda e: e.tensor_tensor(sqy[:], yy[:], yy[:], ALU.mult),
                                  waits=[e3] + ([lastpe_n[0]] if lastpe_n[0] else []))
                        state["e4"] = e4

                def stage_b(state=state, GT=GT, q0=q0, r=r, h=h, qc=qc, lp=lp):
                    nonlocal si, sti
                    st = stg[sti % 2]; sti += 1
                    if mla:
                        ec = P.op("dve", lambda e, o=st.t[:, :], g=GT[:, q0:q0 + 512]: e.tensor_tensor(o, oraw[0][:], g, ALU.mult),
                                  waits=[state["outs"][0]] + st.ww())
                    else:
                        pn = psS[si % 4]; si += 1
                        lpn = P.op("pe", lambda e, o=pn.t[:, :]: e.matmul(o, S.onesf[:], sqy[:], start=True, stop=True),
                                   waits=pn.ww() + [state["e4"]])
                        pn.wrote(lpn)
                        lastpe_n[0] = lpn
                        e5 = rstd_from_ps(P, S, pn, 1.0 / 128, RMS_EPS, rs[:], rs[:])
                        e6 = P.op("dve", lambda e: e.tensor_tensor(yy[:], yy[:], rs[:], ALU.mult), waits=[e5])
                        ec = P.op("dve", lambda e, o=st.t[:, :], g=GT[:, q0:q0 + 512]:
                                  e.scalar_tensor_tensor(o, yy[:], S.subc[l][:, 0:1], g, ALU.mult, ALU.mult),
                                  waits=[e6] + st.ww())
                    st.wrote(ec)
                    epi_done[0] = ec
                    ev2 = P.dma("sync", ydst[r, q0:q0 + 512], st.t[:, :], st.rw(), S.sem_stg[(sti - 1) % 2])
                    st.read(ev2)
                    if qc == 3:
                        head_done[h] = [lp, ec]
                        if h + 2 < 8:
                            load(h + 2)

                for m in range(nm):
                    deferred.append(lambda m=m, f=stage_a: f(m))
                deferred.append(stage_b)
        while deferred:
            deferred.pop(0)()
        P.emit()


lastpe_n = [None, None]


def phase_outproj(S, l, tb, xsrc):
    nc = S.nc
    d = S.dram
    tb0 = tb * 1024
    with ExitStack() as es0:
        mT = sb(S, es0, [128, 32, 1024], BF16, "mT")
        P = Prog(nc, S)
        with ExitStack() as es:
            yT = sb(S, es, [128, 3, 8, 1024], BF16, "yT")
            wo = [Buf(sb(S, es, [128, 3, 8, 512], BF16, "wo")) for _ in range(2)]
            gt = [Buf(sb(S, es, [128, 3, 512], BF16, "gt")) for _ in range(2)]
            tt_ = [[sb(S, es, [128, 512], F32, "t5") for _ in range(3)] for _ in range(2)]
            ps = psum_bufs(S, es)
            psi = 0
            yev = None
            for j, nmy in enumerate(("yA", "yB", "yC")):
                yev = P.dma("sync", yT[:, j, :, :], d[nmy][:, tb0:tb0 + 1024].rearrange("(wc p) t -> p wc t", p=128), (), S.sem_ld3)
            wsrc = [S.w_o[j][l] for j in range(3)]

            def load_wo(dg):
                b = wo[dg % 2]
                ww = b.ww()
                ev = None
                for j in range(3):
                    ev = P.dma("pool", b.t[:, j, :, :], wsrc[j][:, dg * 512:(dg + 1) * 512].rearrange("(wc p) n -> p wc n", p=128),
                               ww, S.sem_wb[dg % 2])
                b.wrote(ev)

            gview = d["gm"].rearrange("(j dc p) t -> dc p j t", j=3, p=128)
            load_wo(0)
            load_wo(1)
            gi = 0
            prev_tt = [None, None]
            mT_ev = []
            for dg in range(8):
                b = wo[dg % 2]
                lp = None
                for ds in range(4):
                    dc = dg * 4 + ds
                    for tc in range(2):
                        g = gt[gi % 2]
                        tset = tt_[gi % 2]
                        pv = prev_tt[gi % 2]
                        gi += 1
                        ev = P.dma("sync", g.t[:, :, :], gview[dc][:, :, tb0 + tc * 512:tb0 + (tc + 1) * 512], g.ww(), S.sem_ld2[(gi - 1) % 2])
                        g.wrote(ev)
                        pbs = []
                        for j in range(3):
                            pb = ps[psi % 8]; psi += 1
                            for wc in range(8):
                                lp = P.op("pe", lambda e, o=pb.t[:, :], a=b.t[:, j, wc, ds * 128:(ds + 1) * 128],
                                          r=yT[:, j, wc, tc * 512:(tc + 1) * 512], wc=wc:
                                          e.matmul(o, a, r, start=(wc == 0), stop=(wc == 7)),
                                          waits=(pb.ww() + b.rw() + [yev]) if wc == 0 else (), sig=(wc == 7))
                            pb.wrote(lp)
                            pbs.append(pb)
                        evs = []
                        for j in range(3):
                            e1 = P.op("dve", lambda e, o=tset[j][:], a=pbs[j].t[:, :], gg=g.t[:, j, :]: e.tensor_tensor(o, a, gg, ALU.mult),
                                      waits=pbs[j].rw() + g.rw() + ([pv] if pv else []))
                            pbs[j].read(e1)
                            evs.append(e1)
                        g.read(evs[-1])
                        e2 = P.op("pool", lambda e, a=tset[0][:], c=tset[1][:]: e.tensor_tensor(a, a, c, ALU.add), waits=evs)
                        e3 = P.op("pool", lambda e, o=mT[:, dc, tc * 512:(tc + 1) * 512], a=tset[0][:], c=tset[2][:]:
                                  e.tensor_tensor(o, a, c, ALU.add), waits=[e2])
                        prev_tt[(gi - 1) % 2] = e3
                        mT_ev = [e3]
                b.read(lp)
                if dg + 2 < 8:
                    load_wo(dg + 2)
            P.emit()
        P = Prog(nc, S)
        with ExitStack() as es:
            wob = [Buf(sb(S, es, [128, 32, 512], BF16, "wout")) for _ in range(2)]
            stf = [Buf(sb(S, es, [128, 512], F32, "stf")) for _ in range(4)]
            xcs = [Buf(sb(S, es, [128, 512], F32, "xc")) for _ in range(4)]
            ps = psum_bufs(S, es)
            psi = 0
            sti = 0
            wsrc = S.w_out[l]

            def load_w(eg):
                b = wob[eg % 2]
                ww = b.ww()
                src = wsrc[:, eg * 512:(eg + 1) * 512].rearrange("(dc p) n -> p dc n", p=128)
                ev = None
                for i in range(4):
                    ev = P.dma("pool", b.t[:, 8 * i:8 * i + 8, :], src[:, 8 * i:8 * i + 8, :], ww, S.sem_wb[eg % 2])
                b.wrote(ev)

            load_w(0)
            load_w(1)
            zrs = P.op("dve", lambda e: e.memset(S.rowsum[:, tb * 8:(tb + 1) * 8, :], 0.0))
            for eg in range(8):
                b = wob[eg % 2]
                lp = None
                for tt in range(8):
                    pb = ps[psi % 8]; psi += 1
                    for dc in range(32):
                        lp = P.op("pe", lambda e, o=pb.t[:, :], a=mT[:, dc, tt * 128:(tt + 1) * 128], r=b.t[:, dc, :], dc=dc:
                                  e.matmul(o, a, r, start=(dc == 0), stop=(dc == 31)),
                                  waits=(pb.ww() + b.rw()) if dc == 0 else (), sig=(dc == 31))
                    pb.wrote(lp)
                    st = stf[sti % 4]
                    xc = xcs[sti % 4]
                    sti += 1
                    r0 = tb0 + tt * 128
                    evx = P.dma("act", xc.t[:, :], xsrc[r0:r0 + 128, eg * 512:(eg + 1) * 512], xc.ww(), S.sem_ln[(sti - 1) % 4])
                    xc.wrote(evx)
                    ev = P.op("dve", lambda e, o=st.t[:, :], a=pb.t[:, :], x=xc.t[:, :], acc=S.rowsum[:, tb * 8 + tt, eg:eg + 1]:
                              e.scalar_tensor_tensor(o, x, ALPHA, a, ALU.mult, ALU.add, accum_out=acc),
                              waits=pb.rw() + st.ww() + xc.rw() + [zrs])
                    pb.read(ev); st.wrote(ev); xc.read(ev)
                    ev = P.dma("sync", d["yout"][r0:r0 + 128, eg * 512:(eg + 1) * 512], st.t[:, :], st.rw(), S.sem_stg[(sti - 1) % 4])
                    st.read(ev)
                b.read(lp)
                if eg + 2 < 8:
                    load_w(eg + 2)
            P.emit()


def phase_ln(S, l, xsrc, xdst):
    nc = S.nc
    P = Prog(nc, S)
    d = S.dram
    NT = T // 128
    with ExitStack() as es:
        lng = sb(S, es, [128, D], F32, "lng")
        lnb = sb(S, es, [128, D], F32, "lnb")
        yt = [Buf(sb(S, es, [128, D], F32, "lny")) for _ in range(4)]
        xt = [Buf(sb(S, es, [128, D], F32, "lnx")) for _ in range(4)]
        sts = [sb(S, es, [128, 8], F32, "lnst") for _ in range(4)]
        cev = P.dma("sync", lng[:], S.lngD[l], (), S.sem_c)
        cev = P.dma("sync", lnb[:], S.lnbD[l], (), S.sem_c)
        evR, evQ, evT, evN = {}, {}, {}, {}

        def load(i):
            yb = yt[i % 4]
            r0 = i * 128
            ev = P.dma("sync", yb.t[:], d["yout"][r0:r0 + 128, :], yb.ww(), S.sem_ln[i % 4])
            yb.wrote(ev)

        def st_R(i):
            yb, st1 = yt[i % 4], sts[i % 4]
            e0 = P.op("dve", lambda e, st1=st1: e.memset(st1[:, :], 0.0), waits=[evN[i - 4]] if (i - 4) in evN else [])
            e1 = P.op("dve", lambda e, rsrc=S.rowsum[:, i, :], st1=st1: e.reduce_sum(st1[:, 0:1], rsrc, axis=mybir.AxisListType.X),
                      waits=[e0])
            evR[i] = P.op("dve", lambda e, st1=st1: e.tensor_scalar(st1[:, 1:2], st1[:, 0:1], -1.0 / D, None, ALU.mult), waits=[e1])

        def st_Q(i):
            yb, xb, st1 = yt[i % 4], xt[i % 4], sts[i % 4]
            evQ[i] = P.op("act", lambda e, y=yb.t[:], x=xb.t[:], st1=st1:
                          e.activation(out=x, in_=y, func=AF.Square, bias=st1[:, 1:2], scale=1.0, accum_out=st1[:, 2:3]),
                          waits=[evR[i]] + xb.ww() + yb.rw())

        def st_T(i):
            st1 = sts[i % 4]
            e7 = P.op("dve", lambda e, st1=st1: e.tensor_scalar(st1[:, 3:4], st1[:, 2:3], 1.0 / D, LN_EPS, ALU.mult, ALU.add), waits=[evQ[i]])
            e8 = P.op("act", lambda e, st1=st1: e.sqrt(st1[:, 4:5], st1[:, 3:4]), waits=[e7])
            e9 = P.op("dve", lambda e, st1=st1: e.reciprocal(st1[:, 5:6], st1[:, 4:5]), waits=[e8])
            evT[i] = P.op("dve", lambda e, st1=st1: e.tensor_tensor(st1[:, 6:7], st1[:, 1:2], st1[:, 5:6], ALU.mult), waits=[e9])

        def st_N(i):
            yb, st1 = yt[i % 4], sts[i % 4]
            evN[i] = P.op("act", lambda e, y=yb.t[:], st1=st1:
                          e.activation(out=y, in_=y, func=AF.Identity, bias=st1[:, 6:7], scale=st1[:, 5:6]), waits=[evT[i]])

        evM = {}

        def st_M(i):
            yb, xb = yt[i % 4], xt[i % 4]
            e10 = P.op("dve", lambda e, y=yb.t[:], x=xb.t[:]: e.tensor_tensor(x, y, lng[:], ALU.mult), waits=[evN[i], cev])
            yb.read(e10)
            evM[i] = e10

        def st_H(i):
            yb, xb = yt[i % 4], xt[i % 4]
            r0 = i * 128
            e10 = evM[i]
            e11 = P.op("dve", lambda e, x=xb.t[:]: e.tensor_tensor(x, x, lnb[:], ALU.add), waits=[e10])
            xb.wrote(e11)
            ev = P.dma("pool", xdst[r0:r0 + 128, :], xb.t[:], [e11], S.sem_stg[i % 4])
            xb.read(ev)

        for i in range(4):
            load(i)
        st_R(0)
        st_Q(0)
        st_T(0)
        st_R(1)
        for i in range(NT):
            st_N(i)
            st_M(i)
            if i + 1 < NT:
                st_Q(i + 1)
                st_T(i + 1)
            st_H(i)
            if i + 2 < NT:
                st_R(i + 2)
            if i + 4 < NT:
                load(i + 4)
        P.emit()


def build(stop_after=None, debug_out=(), nlayers=L):
    nc = bass.Bass("TRN2", target_bir_lowering=False)
    S = State()
    S.nc = nc
    S.uid = 0
    S.allsems = []
    S.seen = {e: {} for e in ENGS}
    lastpe_n[0] = None
    lastpe_n[1] = None

    def din(name, shape, dt=F32):
        return nc.dram_tensor(name, shape, dt, kind="ExternalInput").ap()

    S.x = din("x", [T, D])
    w_in_all = din("w_in", [L, D, INW])
    S.w_in = [w_in_all[l] for l in range(L)]
    w_uq = din("w_uq", [L, 1536, 1536]); S.w_uq = [w_uq[l] for l in range(L)]
    w_ukv = din("w_ukv", [L, 512, 2048]); S.w_ukv = [w_ukv[l] for l in range(L)]
    S.w_o = []
    for nm in ("w_o_a", "w_o_b", "w_o_c"):
        t = din(nm, [L, 1024, D])
        S.w_o.append([t[l] for l in range(L)])
    w_out = din("w_out", [L, D, D]); S.w_out = [w_out[l] for l in range(L)]
    S.identD = din("ident", [128, 128])
    S.pswapD = din("pswap", [128, 128])
    S.sel4D = din("sel4", [128, 128])
    S.cosD = din("cosT", [128, T])
    S.sinD = din("sinT", [128, T])
    S.bmD = din("bm", [L, 128, 96])
    S.gqD = din("gq", [L, 128, 12])
    S.gkvD = din("gkv", [L, 128, 4])
    S.sublnD = din("subln", [L, 128, 1])
    S.lamD = din("lamrep", [L, 128, 4, 64])
    lng = din("lng", [L, 128, D]); S.lngD = [lng[l] for l in range(L)]
    lnb = din("lnb", [L, 128, D]); S.lnbD = [lnb[l] for l in range(L)]
    nab = din("nabias", [L, 8, 128, 35, 128])
    S.nabias = [[nab[l][h] for h in range(8)] for l in range(L)]
    S.out = nc.dram_tensor("out", [T, D], F32, kind="ExternalOutput").ap()

    S.dram = {}

    def scr(name, shape, dt=BF16):
        if name in debug_out:
            S.dram[name] = nc.dram_tensor(name, shape, dt, kind="ExternalOutput").ap()
        else:
            S.dram[name] = nc.dram_tensor(name, shape, dt).ap()

    for nm in ("qA", "kA", "gA", "gB", "qC", "kC", "gC", "qBn", "kBn", "yA", "yB", "yC"):
        scr(nm, [1024, T])
    for nm in ("vA", "vC", "vB"):
        scr(nm, [T, 1024])
    scr("cq", [1536, T]); scr("ckv", [512, T]); scr("kr", [64, T]); scr("gm", [12288, T])
    scr("qBr", [512, T])
    scr("kAh", [1024, 768]); scr("kAh_g", [2048, 768])
    scr("vAh", [768, 1024]); scr("vAh_g", [1536, 1024])
    scr("kr_g", [128, T])
    for i in range(2):
        scr(f"kBn_g{i}", [1024, T]); scr(f"kC_g{i}", [1024, T])
        scr(f"vB_g{i}", [2048, 1024]); scr(f"vC_g{i}", [2048, 1024])
    scr("yout", [T, D], F32)
    scr("x1", [T, D], F32)

    with ExitStack() as es:
        S.esem = {e: newsem(S, es, f"e_{e}") for e in ("act", "dve", "pool", "pe")}
        S.sem_xs = [newsem(S, es, f"xs{i}") for i in range(4)]
        S.sem_nab = [newsem(S, es, f"nab{i}") for i in range(8)]
        S.sem_wb = [newsem(S, es, f"wb{i}") for i in range(2)]
        S.sem_stg = [newsem(S, es, f"stg{i}") for i in range(4)]
        S.sem_ld2 = [newsem(S, es, f"ld2{i}") for i in range(2)]
        S.sem_c = newsem(S, es, "const")
        S.sem_out = newsem(S, es, "outs")
        S.sem_w2 = newsem(S, es, "w2")
        S.sem_ex = newsem(S, es, "ex")
        S.sem_cc = newsem(S, es, "cc")
        S.sem_ld3 = newsem(S, es, "ld3")
        S.sem_ld4 = newsem(S, es, "ld4")
        S.sem_ln = [newsem(S, es, f"ln{i}") for i in range(6)]
        S.ident = sb(S, es, [128, 128], F32, "ident")
        S.identb = sb(S, es, [128, 128], BF16, "identb")
        S.pswap = sb(S, es, [128, 128], F32, "pswap")
        S.sel4 = sb(S, es, [128, 128], F32, "sel4")
        S.onesb = sb(S, es, [128, 128], BF16, "onesb")
        S.onesf = sb(S, es, [128, 128], F32, "onesf")
        S.cosT = sb(S, es, [128, T], F32, "cosT")
        S.sinT = sb(S, es, [128, T], F32, "sinT")
        S.bm = [sb(S, es, [128, 96], F32, "bm") for _ in range(L)]
        S.gq = [sb(S, es, [128, 12], F32, "gq") for _ in range(L)]
        S.gkv = [sb(S, es, [128, 4], F32, "gkv") for _ in range(L)]
        S.subc = [sb(S, es, [128, 1], F32, "subc") for _ in range(L)]
        S.neglam = [sb(S, es, [128, 1], F32, "neglam") for _ in range(L)]
        S.rowsum = sb(S, es, [128, T // 128, 8], F32, "rowsum")
        lamt = sb(S, es, [128, 4, 64], F32, "lamt")
        lamw = sb(S, es, [128, 8], F32, "lamw")
        P = Prog(nc, S)
        P.dma("sync", S.ident[:], S.identD[:, :], (), S.sem_c)
        P.dma("sync", S.pswap[:], S.pswapD[:, :], (), S.sem_c)
        P.dma("sync", S.sel4[:], S.sel4D[:, :], (), S.sem_c)
        P.dma("sync", S.cosT[:], S.cosD[:, :], (), S.sem_c)
        P.dma("sync", S.sinT[:], S.sinD[:, :], (), S.sem_c)
        cev = None
        for l in range(L):
            P.dma("sync", S.bm[l][:], S.bmD[l], (), S.sem_c)
            P.dma("sync", S.gq[l][:], S.gqD[l], (), S.sem_c)
            P.dma("sync", S.gkv[l][:], S.gkvD[l], (), S.sem_c)
            cev = P.dma("sync", S.subc[l][:], S.sublnD[l], (), S.sem_c)
        e0 = P.op("dve", lambda e: e.tensor_copy(S.identb[:], S.ident[:]), waits=[cev])
        P.op("dve", lambda e: e.memset(S.onesb[:], 1.0))
        e1 = P.op("dve", lambda e: e.memset(S.onesf[:], 1.0))
        S.const_ready = e1
        ev = e1
        for l in range(L):
            lam_init = 0.8 - 0.6 * math.exp(-0.3 * l)
            lev = P.dma("sync", lamt[:], S.lamD[l], [ev], S.sem_c)
            a = P.op("dve", lambda e: e.tensor_tensor(lamt[:, 0, :], lamt[:, 0, :], lamt[:, 1, :], ALU.mult), waits=[lev])
            a = P.op("dve", lambda e: e.tensor_tensor(lamt[:, 2, :], lamt[:, 2, :], lamt[:, 3, :], ALU.mult), waits=[a])
            a = P.op("dve", lambda e: e.reduce_sum(lamw[:, 0:1], lamt[:, 0, :], axis=mybir.AxisListType.X), waits=[a])
            a = P.op("dve", lambda e: e.reduce_sum(lamw[:, 1:2], lamt[:, 2, :], axis=mybir.AxisListType.X), waits=[a])
            b = P.op("act", lambda e: e.activation(out=lamw[:, 2:4], in_=lamw[:, 0:2], func=AF.Exp), waits=[a])
            a = P.op("dve", lambda e: e.tensor_tensor(lamw[:, 4:5], lamw[:, 3:4], lamw[:, 2:3], ALU.subtract), waits=[b])
            a = P.op("dve", lambda e, l=l, li=lam_init: e.tensor_scalar(S.neglam[l][:], lamw[:, 4:5], -li, None, ALU.add), waits=[a])
            a = P.op("dve", lambda e, l=l, li=lam_init: e.tensor_scalar(S.subc[l][:], S.subc[l][:], 1.0 - li, None, ALU.mult), waits=[a])
            ev = a
        S.lam_ready = ev
        P.emit()

        for l in range(nlayers):
            xsrc = S.x if l == 0 else S.dram["x1"]
            xdst = S.dram["x1"] if l < L - 1 else S.out
            for tb in range(2):
                phase_inproj(S, l, tb, xsrc)
            if stop_after == "inproj":
                break
            phase_mla_proj(S, l)
            if stop_after == "mla_proj":
                break
            phase_na(S, l)
            if stop_after == "na":
                break
            phase_dense(S, l, "mla")
            if stop_after == "mla":
                break
            phase_dense(S, l, "diff")
            if stop_after == "diff":
                break
            for tb in range(2):
                phase_outproj(S, l, tb, xsrc)
            if stop_after == "outproj":
                break
            phase_ln(S, l, xsrc, xdst)

        if stop_after is not None or nlayers < L:
            P = Prog(nc, S)
            with ExitStack() as es2:
                o = sb(S, es2, [128, 512], F32, "dummy")
                ev = P.op("dve", lambda e: e.memset(o[:], 0.0))
                ev = P.dma("sync", S.out[0:128, 0:512], o[:], [ev], S.sem_out)
                P.emit()
    return nc


def rope_tables(hf):
    half = 32
    inv = (10000.0 ** (-np.arange(half, dtype=np.float32) * 2.0 / 64)).astype(np.float32)
    pos = (np.arange(T) + hf * T).astype(np.float32)
    ang = pos[:, None] * inv[None, :]
    c = np.cos(ang).T.astype(np.float32)
    s = np.sin(ang).T.astype(np.float32)
    cosT = np.concatenate([c, c, c, c], axis=0)
    sinT = np.concatenate([-s, s, -s, s], axis=0)
    return np.ascontiguousarray(cosT), np.ascontiguousarray(sinT)


def na_bias_table(rpb, hf):
    out = np.full((5, 7, 8, 128, 128), NEG, np.float32)
    reps = [0, 1, 5, 14, 15]
    kl = np.arange(128)
    krl, wk = kl // 64, kl % 64
    ql = np.arange(128)
    qrl, wq = ql // 64, ql % 64
    for t, j in enumerate(reps):
        gj = hf * 16 + j
        for s in range(7):
            gk = gj + s - 3
            krow = 2 * gk + krl
            qrow = 2 * gj + qrl
            start = np.clip(qrow - 4, 0, 56)
            cstart = np.clip(wq - 8, 0, 48)
            valid = ((krow[:, None] >= 0) & (krow[:, None] < 64)
                     & (krow[:, None] >= start[None, :]) & (krow[:, None] < start[None, :] + 8)
                     & (wk[:, None] >= cstart[None, :]) & (wk[:, None] < cstart[None, :] + 16))
            dr = np.clip(krow[:, None] - qrow[None, :] + 7, 0, 14)
            dc = np.clip(wk[:, None] - wq[None, :] + 15, 0, 30)
            vals = rpb[:, dr, dc]
            out[t, s] = np.where(valid[None], vals, np.float32(NEG))
    return np.ascontiguousarray(out.reshape(35, 8, 128, 128).transpose(1, 2, 0, 3))


def make_in_maps(inputs):
    f = lambda k: np.ascontiguousarray(np.asarray(inputs[k], dtype=np.float32))
    x = f("x")
    shared = {
        "w_in": f("w_in"), "w_uq": f("w_uq"), "w_ukv": f("w_ukv"),
        "w_o_a": f("w_o_a"), "w_o_b": f("w_o_b"), "w_o_c": f("w_o_c"), "w_out": f("w_out"),
        "ident": np.eye(128, dtype=np.float32),
        "pswap": np.ascontiguousarray(np.eye(128, dtype=np.float32)[np.arange(128) ^ 32]),
        "sel4": np.ascontiguousarray(np.broadcast_to((np.arange(128) % 32 == 0).astype(np.float32)[:, None], (128, 128))),
        "bm": np.ascontiguousarray(f("b_merge").reshape(L, 96, 128).transpose(0, 2, 1)),
        "gq": np.ascontiguousarray(f("q_norm").reshape(L, 12, 128).transpose(0, 2, 1)),
        "gkv": np.ascontiguousarray(f("kv_norm").reshape(L, 4, 128).transpose(0, 2, 1)),
        "subln": np.ascontiguousarray(f("diff_subln").reshape(L, 128, 1)),
        "lamrep": np.ascontiguousarray(np.broadcast_to(
            np.stack([f("lam_q1"), f("lam_k1"), f("lam_q2"), f("lam_k2")], axis=1)[:, None], (L, 128, 4, 64))),
        "lng": np.ascontiguousarray(np.broadcast_to(f("ln_g")[:, None, :], (L, 128, D))),
        "lnb": np.ascontiguousarray(np.broadcast_to(f("ln_b")[:, None, :], (L, 128, D))),
    }
    rpb = f("na_rpb")
    nab = [np.stack([na_bias_table(rpb[l], hf) for l in range(L)]) for hf in range(2)]
    tabs = [rope_tables(hf) for hf in range(2)]
    maps = []
    for c in range(8):
        b, hf = c // 2, c % 2
        m = dict(shared)
        m["x"] = np.ascontiguousarray(x[b, hf * T:(hf + 1) * T, :])
        m["cosT"], m["sinT"] = tabs[hf]
        m["nabias"] = nab[hf]
        maps.append(m)
    return maps


def kernel(**inputs):
    nc = build()
    res = run_bass_kernel_spmd(nc, make_in_maps(inputs), core_ids=list(range(8)))
    full = np.zeros((4, SEQ, D), np.float32)
    for c in range(8):
        full[c // 2, (c % 2) * T:(c % 2 + 1) * T, :] = res.results[c]["out"]
    return full
```

```python
import math
from contextlib import ExitStack
import numpy as np
import concourse.bass as bass
import concourse.mybir as mybir
from concourse.bass_utils import run_bass_kernel_spmd

F32 = mybir.dt.float32
BF16 = mybir.dt.bfloat16
AF = mybir.ActivationFunctionType
ALU = mybir.AluOpType

D = 4096
T = 2048
SEQ = 4096
L = 2
INW = 23616
NEG = -30000.0
LN_EPS = 1e-5
RMS_EPS = 1e-6
ALPHA = (2.0 * L) ** 0.25
PAIRS = [[0, 1], [2, 3], [4, 5], [6, 7]]

SEGS = [
    ("qA", 0, 1024, "fm", "copy", 128 ** -0.5),
    ("kA", 1024, 1024, "fm", "copy", 1.0),
    ("vA", 2048, 1024, "tm", None, 1.0),
    ("gA", 3072, 1024, "fm", "silu", 1.0),
    ("cq", 4096, 1536, "fm", "copy", 1.0),
    ("ckv", 5632, 512, "fm", "copy", 1.0),
    ("kr", 6144, 64, "rope", None, 1.0),
    ("gB", 6208, 1024, "fm", "silu", 1.0),
    ("qC", 7232, 1024, "rope", None, 0.125),
    ("kC", 8256, 1024, "rope", None, 1.0),
    ("vC", 9280, 1024, "tm", None, 1.0),
    ("gC", 10304, 1024, "fm", "silu", 1.0),
    ("gm", 11328, 12288, "fm", "sigmoid", 1.0),
]


class Sem:
    __slots__ = ("h", "v")

    def __init__(self, h):
        self.h = h
        self.v = 0


class Buf:
    __slots__ = ("t", "wr", "rd")

    def __init__(self, t):
        self.t = t
        self.wr = None
        self.rd = []

    def ww(self):
        if self.rd:
            return list(self.rd)
        return [self.wr] if self.wr is not None else []

    def wrote(self, ev):
        self.wr = ev
        self.rd = []

    def rw(self):
        return [self.wr] if self.wr is not None else []

    def read(self, ev):
        self.rd.append(ev)


class State:
    pass


ENGS = ("sync", "act", "dve", "pool", "pe")
ENGMAP = {"sync": "sync", "act": "scalar", "dve": "vector", "pool": "gpsimd", "pe": "tensor"}


class Prog:
    def __init__(self, nc, S):
        self.nc = nc
        self.S = S
        self.q = {e: [] for e in ENGS}

    def add(self, eng, fn, waits=(), inc=None, amt=1):
        ws = []
        seen = self.S.seen[eng]
        for w in waits:
            if w is None:
                continue
            s, v = w
            if seen.get(s, 0) >= v:
                continue
            seen[s] = v
            ws.append((s, v))
        ev = None
        if inc is not None:
            inc.v += amt
            ev = (inc, inc.v)
        self.q[eng].append((fn, ws, inc, amt))
        return ev

    def op(self, eng, fn, waits=(), sig=True):
        return self.add(eng, fn, waits, self.S.esem[eng] if sig else None, 1)

    def dma(self, eng, out, in_, waits, sem):
        return self.add(eng, lambda e: e.dma_start(out=out, in_=in_), waits, sem, 16)

    def join(self):
        allsems = [s for s in self.S.allsems if s.v > 0]
        for e in ENGS:
            self.add(e, None, [(s, s.v) for s in allsems])

    def emit(self):
        self.join()
        with self.nc.Block() as block:
            for e in ENGS:
                items = self.q[e]

                def body(eng, items=items):
                    for fn, ws, inc, amt in items:
                        for s, v in ws:
                            eng.wait_ge(s.h, v)
                        if fn is None:
                            continue
                        ins = fn(eng)
                        if inc is not None:
                            ins.then_inc(inc.h, amt)

                getattr(block, ENGMAP[e])(body)


def newsem(S, es, name):
    s = Sem(es.enter_context(S.nc.semaphore(name)))
    S.allsems.append(s)
    return s


def sb(S, es, shape, dt, name):
    S.uid += 1
    return es.enter_context(S.nc.sbuf_tensor(f"{name}_{S.uid}", shape, dt))


def psum_bufs(S, es, n=8):
    out = []
    for i in range(n):
        S.uid += 1
        out.append(Buf(es.enter_context(S.nc.psum_tensor(f"ps_{S.uid}", [128, 512], F32))))
    return out


def inproj_groups():
    groups = []
    for name, col0, width, kind, func, scale in SEGS:
        for c in range(0, width, 512):
            w = min(512, width - c)
            if kind == "tm":
                jobs = [dict(kind="tm", dest=name, dcol0=c, n=w)]
            elif kind == "rope":
                jobs = [dict(kind="rope", wcol=j, n=min(128, w - j), scale=scale, dest=name, row0=c + j)
                        for j in range(0, w, 128)]
            else:
                jobs = [dict(kind="fm", wcol=j, n=128, func=func, scale=scale, dest=name,
                             row0=c + j) for j in range(0, w, 128)]
            groups.append(dict(col0=col0 + c, w=w, jobs=jobs))
    return groups


def phase_inproj(S, l, tb, xsrc):
    nc = S.nc
    P = Prog(nc, S)
    w_in = S.w_in[l]
    with ExitStack() as es:
        xT = sb(S, es, [128, 32, 1024], BF16, "xT")
        xs = [Buf(sb(S, es, [128, 2048], F32, "xs")) for _ in range(4)]
        wb = [Buf(sb(S, es, [128, 32, 512], BF16, "wb")) for _ in range(2)]
        stg = [Buf(sb(S, es, [128, 1024], BF16, "stg")) for _ in range(4)]
        rt = [sb(S, es, [128, 512], F32, "rt") for _ in range(2)]
        rtA = [Buf(sb(S, es, [128, 512], F32, "rtA")) for _ in range(3)]
        ps = psum_bufs(S, es)
        psi = [0]
        pending = []
        rai = [0]
        rope_dve = [None]

        def nextps():
            b = ps[psi[0] % 8]
            psi[0] += 1
            return b

        def flush():
            while pending:
                job, tc, ta, st, si_ = pending.pop(0)
                n = job["n"]
                sc = job["scale"]
                t0c = tb * 1024 + tc * 512
                pb2 = nextps()
                lp2 = P.op("pe", lambda e, o=pb2.t[0:n, :], r=ta.t[0:n, :], n=n: e.matmul(o, S.pswap[0:n, 0:n], r, start=True, stop=True),
                           waits=pb2.ww() + ta.rw())
                pb2.wrote(lp2)
                e1 = P.op("dve", lambda e, o=rt[0][0:n, :], a=ta.t[0:n, :], c=S.cosT[0:n, t0c:t0c + 512], sc=sc:
                          e.scalar_tensor_tensor(o, a, sc, c, ALU.mult, ALU.mult),
                          waits=ta.rw() + ([rope_dve[0]] if rope_dve[0] else []))
                e2 = P.op("dve", lambda e, o=rt[1][0:n, :], a=pb2.t[0:n, :], c=S.sinT[0:n, t0c:t0c + 512], sc=sc:
                          e.scalar_tensor_tensor(o, a, sc, c, ALU.mult, ALU.mult), waits=pb2.rw())
                pb2.read(e2)
                ta.read(lp2)
                ta.read(e1)
                e3 = P.op("dve", lambda e, o=st.t[0:n, tc * 512:(tc + 1) * 512], a=rt[0][0:n, :], c=rt[1][0:n, :]:
                          e.tensor_tensor(o, a, c, ALU.add), waits=[e1, e2] + (st.ww() if tc == 0 else []))
                rope_dve[0] = e3
                if tc == 1:
                    st.wrote(e3)
                    dest = S.dram[job["dest"]]
                    ev = P.dma("sync", dest[job["row0"]:job["row0"] + n, tb * 1024:(tb + 1) * 1024],
                               st.t[0:n, :], st.rw(), S.sem_stg[si_ % 4])
                    st.read(ev)

        xT_events = []
        k = 0
        for tt in range(8):
            r0 = tb * 1024 + tt * 128
            for hf in range(2):
                xb = xs[k % 4]
                k += 1
                ev = P.dma("sync" if k % 2 else "act", xb.t[:], xsrc[r0:r0 + 128, hf * 2048:(hf + 1) * 2048], xb.ww(),
                           S.sem_xs[(k - 1) % 4])
                xb.wrote(ev)
                for g in range(4):
                    pb = nextps()
                    for i in range(4):
                        lastev = P.op("pe", (lambda e, o=pb.t[:, i * 128:(i + 1) * 128],
                                             a=xb.t[:, (g * 4 + i) * 128:(g * 4 + i + 1) * 128]:
                                             e.transpose(o, a, S.ident[:])),
                                      waits=(pb.ww() + xb.rw()) if i == 0 else (), sig=(i == 3))
                    pb.wrote(lastev)
                    kc0 = hf * 16 + g * 4
                    o = xT[:, kc0:kc0 + 4, tt * 128:(tt + 1) * 128]
                    a = pb.t[:, :].rearrange("p (a b) -> p a b", a=4)
                    eng = "act" if (g % 2 == 0) else "dve"
                    if eng == "act":
                        ev2 = P.op("act", lambda e, o=o, a=a: e.copy(o, a), waits=pb.rw())
                    else:
                        ev2 = P.op("dve", lambda e, o=o, a=a: e.tensor_copy(o, a), waits=pb.rw())
                    pb.read(ev2)
                    xT_events.append(ev2)
                xb.read(lastev)
        xT_ready = []
        for en in ("act", "dve"):
            evs = [e for e in xT_events if e[0] is S.esem[en]]
            xT_ready.append(max(evs, key=lambda t: t[1]))

        groups = inproj_groups()

        def load_group(gi):
            g = groups[gi]
            b = wb[gi % 2]
            src = w_in[:, g["col0"]:g["col0"] + g["w"]].rearrange("(kc p) n -> p kc n", p=128)
            ww = b.ww()
            ev = None
            for i in range(4):
                ev = P.dma("pool", b.t[:, 8 * i:8 * i + 8, 0:g["w"]], src[:, 8 * i:8 * i + 8, :], ww,
                           S.sem_wb[gi % 2])
            b.wrote(ev)

        load_group(0)
        load_group(1)
        si = 0
        for gi, g in enumerate(groups):
            b = wb[gi % 2]
            lastpe = None
            for job in g["jobs"]:
                dest = S.dram[job["dest"]]
                if job["kind"] == "tm":
                    n = job["n"]
                    for tt in range(8):
                        pb = nextps()
                        for kc in range(32):
                            lastpe = P.op("pe", (lambda e, o=pb.t[:, 0:n], a=xT[:, kc, tt * 128:(tt + 1) * 128],
                                                 r=b.t[:, kc, 0:n], kc=kc:
                                                 e.matmul(o, a, r, start=(kc == 0), stop=(kc == 31))),
                                          waits=(pb.ww() + b.rw() + xT_ready) if kc == 0 else (),
                                          sig=(kc == 31))
                        pb.wrote(lastpe)
                        flush()
                        st = stg[si % 4]
                        si += 1
                        ev = P.op("dve", lambda e, o=st.t[:, 0:n], a=pb.t[:, 0:n]: e.tensor_copy(o, a),
                                  waits=pb.rw() + st.ww())
                        pb.read(ev)
                        st.wrote(ev)
                        r0 = tb * 1024 + tt * 128
                        ev = P.dma("sync", dest[r0:r0 + 128, job["dcol0"]:job["dcol0"] + n], st.t[:, 0:n],
                                   st.rw(), S.sem_stg[(si - 1) % 4])
                        st.read(ev)
                elif job["kind"] == "fm":
                    n = job["n"]
                    st = stg[si % 4]
                    si += 1
                    evs = []
                    for tc in range(2):
                        pb = nextps()
                        for kc in range(32):
                            lastpe = P.op("pe", (lambda e, o=pb.t[0:n, :], a=b.t[:, kc, job["wcol"]:job["wcol"] + n],
                                                 r=xT[:, kc, tc * 512:(tc + 1) * 512], kc=kc:
                                                 e.matmul(o, a, r, start=(kc == 0), stop=(kc == 31))),
                                          waits=(pb.ww() + b.rw() + xT_ready) if kc == 0 else (),
                                          sig=(kc == 31))
                        pb.wrote(lastpe)
                        flush()
                        o = st.t[0:n, tc * 512:(tc + 1) * 512]
                        a = pb.t[0:n, :]
                        f = job["func"]
                        if f == "copy":
                            if job["scale"] == 1.0:
                                fn = lambda e, o=o, a=a: e.copy(o, a)
                            else:
                                fn = lambda e, o=o, a=a, s=job["scale"]: e.mul(o, a, s)
                        elif f == "silu":
                            fn = lambda e, o=o, a=a: e.activation(out=o, in_=a, func=AF.Silu)
                        else:
                            bidx = job["row0"] // 128
                            fn = lambda e, o=o, a=a, bi=bidx: e.activation(
                                out=o, in_=a, func=AF.Sigmoid, bias=S.bm[l][:, bi:bi + 1], scale=1.0)
                        ev = P.op("act", fn, waits=pb.rw() + (st.ww() if tc == 0 else []))
                        pb.read(ev)
                        evs.append(ev)
                    st.wrote(evs[-1])
                    ev = P.dma("sync", dest[job["row0"]:job["row0"] + n, tb * 1024:(tb + 1) * 1024],
                               st.t[0:n, :], st.rw(), S.sem_stg[(si - 1) % 4])
                    st.read(ev)
                else:
                    n = job["n"]
                    st = stg[si % 4]
                    si += 1
                    for tc in range(2):
                        pb = nextps()
                        for kc in range(32):
                            lastpe = P.op("pe", (lambda e, o=pb.t[0:n, :], a=b.t[:, kc, job["wcol"]:job["wcol"] + n],
                                                 r=xT[:, kc, tc * 512:(tc + 1) * 512], kc=kc:
                                                 e.matmul(o, a, r, start=(kc == 0), stop=(kc == 31))),
                                          waits=(pb.ww() + b.rw() + xT_ready) if kc == 0 else (),
                                          sig=(kc == 31))
                        pb.wrote(lastpe)
                        flush()
                        ta = rtA[rai[0] % 3]
                        rai[0] += 1
                        ev = P.op("act", lambda e, o=ta.t[0:n, :], a=pb.t[0:n, :]: e.copy(o, a), waits=pb.rw() + ta.ww())
                        pb.read(ev)
                        ta.wrote(ev)
                        pending.append((job, tc, ta, st, si - 1))
            b.read(lastpe)
            if gi + 2 < len(groups):
                load_group(gi + 2)
        flush()
        P.emit()


def rstd_from_ps(P, S, psb, n_inv, eps, tmp, out, extra_waits=()):
    e1 = P.op("dve", lambda e: e.tensor_scalar(tmp, psb.t[:, :], n_inv, eps, ALU.mult, ALU.add),
              waits=psb.rw() + list(extra_waits))
    psb.read(e1)
    e2 = P.op("act", lambda e: e.sqrt(tmp, tmp), waits=[e1])
    e3 = P.op("dve", lambda e: e.reciprocal(out, tmp), waits=[e2])
    return e3


def phase_mla_proj(S, l):
    nc = S.nc
    P = Prog(nc, S)
    with ExitStack() as es:
        wuq = sb(S, es, [128, 12, 1536], BF16, "wuq")
        wuqs = sb(S, es, [128, 12, 512], BF16, "wuqs")
        wukv = sb(S, es, [128, 4, 2048], BF16, "wukv")
        cqc = [Buf(sb(S, es, [128, 12, 512], BF16, "cqc")) for _ in range(2)]
        ckc = [Buf(sb(S, es, [128, 4, 512], BF16, "ckc")) for _ in range(2)]
        sq = sb(S, es, [128, 16, 512], BF16, "sq")
        cqn = sb(S, es, [128, 12, 512], BF16, "cqn")
        ckn = sb(S, es, [128, 4, 512], BF16, "ckn")
        tmpq = sb(S, es, [128, 512], F32, "tmpq")
        tmpk = sb(S, es, [128, 512], F32, "tmpk")
        rq = sb(S, es, [128, 512], F32, "rq")
        rk = sb(S, es, [128, 512], F32, "rk")
        stg = [Buf(sb(S, es, [128, 512], BF16, "stg2")) for _ in range(4)]
        rt = [sb(S, es, [128, 512], F32, "rt2") for _ in range(2)]
        ps = psum_bufs(S, es)
        psi = [0]

        def nextps():
            b = ps[psi[0] % 8]
            psi[0] += 1
            return b

        wev = None
        srcq = S.w_uq[l].rearrange("(rc p) n -> p rc n", p=128)
        for i in range(3):
            wev = P.dma("pool", wuq[:, 4 * i:4 * i + 4, :], srcq[:, 4 * i:4 * i + 4, :], (), S.sem_w2)
        srck = S.w_ukv[l].rearrange("(rc p) n -> p rc n", p=128)
        wev = P.dma("pool", wukv[:, :, :], srck, (), S.sem_w2)
        sview = wuq[:, :, :].rearrange("p k (h c) -> p k h c", h=8)
        dview = wuqs[:, :, :].rearrange("p k (h c) -> p k h c", h=8)
        P.op("pool", lambda e: e.tensor_copy(dview[:, :, :, 0:32], sview[:, :, :, 160:192]), waits=[wev])
        wsw = P.op("pool", lambda e: e.tensor_copy(dview[:, :, :, 32:64], sview[:, :, :, 128:160]))
        wready = [wev, wsw]
        exchange_a(P, S)

        si = 0
        prev_norm_reads = []
        for tc in range(4):
            t0 = tc * 512
            cb = cqc[tc % 2]
            kb = ckc[tc % 2]
            ev = P.dma("sync", cb.t[:], S.dram["cq"][:, t0:t0 + 512].rearrange("(rc p) t -> p rc t", p=128),
                       cb.ww(), S.sem_ld2[tc % 2])
            ev = P.dma("sync", kb.t[:], S.dram["ckv"][:, t0:t0 + 512].rearrange("(rc p) t -> p rc t", p=128),
                       kb.ww(), S.sem_ld2[tc % 2])
            cb.wrote(ev)
            kb.wrote(ev)
            e_sq1 = P.op("act", lambda e, a=cb.t[:]: e.square(sq[:, 0:12, :], a), waits=cb.rw() + prev_norm_reads)
            e_sq2 = P.op("act", lambda e, a=kb.t[:]: e.square(sq[:, 12:16, :], a), waits=kb.rw())
            pa = nextps()
            for i in range(12):
                lp = P.op("pe", lambda e, o=pa.t[:, :], r=sq[:, i, :], i=i: e.matmul(o, S.onesb[:], r, start=(i == 0), stop=(i == 11)),
                          waits=(pa.ww() + [e_sq1, S.const_ready]) if i == 0 else (), sig=(i == 11))
            pa.wrote(lp)
            pk = nextps()
            for i in range(4):
                lp = P.op("pe", lambda e, o=pk.t[:, :], r=sq[:, 12 + i, :], i=i: e.matmul(o, S.onesb[:], r, start=(i == 0), stop=(i == 3)),
                          waits=(pk.ww() + [e_sq2]) if i == 0 else (), sig=(i == 3))
            pk.wrote(lp)
            sq_read = lp
            e_rq = rstd_from_ps(P, S, pa, 1.0 / 1536, RMS_EPS, tmpq[:], rq[:], extra_waits=prev_norm_reads)
            e_rk = rstd_from_ps(P, S, pk, 1.0 / 512, RMS_EPS, tmpk[:], rk[:])
            evn = []
            for i in range(12):
                eng = "dve"
                evn.append(P.op(eng, lambda e, o=cqn[:, i, :], a=cb.t[:, i, :], g=S.gq[l][:, i:i + 1]:
                                e.scalar_tensor_tensor(o, a, g, rq[:], ALU.mult, ALU.mult),
                                waits=[e_rq] + cb.rw() + prev_norm_reads))
            for i in range(4):
                eng = "dve"
                evn.append(P.op(eng, lambda e, o=ckn[:, i, :], a=kb.t[:, i, :], g=S.gkv[l][:, i:i + 1]:
                                e.scalar_tensor_tensor(o, a, g, rk[:], ALU.mult, ALU.mult),
                                waits=[e_rk] + kb.rw() + prev_norm_reads))
            nready = [evn[-1], evn[-2], evn[11], evn[10]]
            cb.read(evn[11]); cb.read(evn[10]); kb.read(evn[-1]); kb.read(evn[-2])
            lastpe = None
            for h in range(8):
                pb = nextps()
                for rc in range(12):
                    lastpe = P.op("pe", lambda e, o=pb.t[:, :], a=wuq[:, rc, h * 192:h * 192 + 128], r=cqn[:, rc, :], rc=rc:
                                  e.matmul(o, a, r, start=(rc == 0), stop=(rc == 11)),
                                  waits=(pb.ww() + nready + wready) if rc == 0 else (), sig=(rc == 11))
                pb.wrote(lastpe)
                st = stg[si % 4]; si += 1
                ev = P.op("act", lambda e, o=st.t[:, :], a=pb.t[:, :]: e.mul(o, a, 192 ** -0.5), waits=pb.rw() + st.ww())
                pb.read(ev); st.wrote(ev)
                ev = P.dma("sync", S.dram["qBn"][h * 128:(h + 1) * 128, t0:t0 + 512], st.t[:, :], st.rw(), S.sem_stg[(si - 1) % 4])
                st.read(ev)
                pbs = []
                for which in range(2):
                    pb = nextps()
                    for rc in range(12):
                        a = wuq[:, rc, h * 192 + 128:h * 192 + 192] if which == 0 else wuqs[:, rc, h * 64:(h + 1) * 64]
                        lastpe = P.op("pe", lambda e, o=pb.t[0:64, :], a=a, r=cqn[:, rc, :], rc=rc:
                                      e.matmul(o, a, r, start=(rc == 0), stop=(rc == 11)),
                                      waits=(pb.ww() + nready + wready) if rc == 0 else (), sig=(rc == 11))
                    pb.wrote(lastpe)
                    pbs.append(pb)
                sc = 192 ** -0.5
                tg = T0 = t0
                e1 = P.op("dve", lambda e, a=pbs[0].t[0:64, :], c=S.cosT[0:64, tg:tg + 512]:
                          e.scalar_tensor_tensor(rt[0][0:64, :], a, sc, c, ALU.mult, ALU.mult), waits=pbs[0].rw())
                pbs[0].read(e1)
                e2 = P.op("dve", lambda e, a=pbs[1].t[0:64, :], c=S.sinT[0:64, tg:tg + 512]:
                          e.scalar_tensor_tensor(rt[1][0:64, :], a, sc, c, ALU.mult, ALU.mult), waits=pbs[1].rw())
                pbs[1].read(e2)
                st = stg[si % 4]; si += 1
                ev = P.op("dve", lambda e, o=st.t[0:64, :]: e.tensor_tensor(o, rt[0][0:64, :], rt[1][0:64, :], ALU.add),
                          waits=[e1, e2] + st.ww())
                st.wrote(ev)
                ev = P.dma("sync", S.dram["qBr"][h * 64:(h + 1) * 64, t0:t0 + 512], st.t[0:64, :], st.rw(), S.sem_stg[(si - 1) % 4])
                st.read(ev)
            for h in range(8):
                pb = nextps()
                for rc in range(4):
                    lastpe = P.op("pe", lambda e, o=pb.t[:, :], a=wukv[:, rc, h * 256:h * 256 + 128], r=ckn[:, rc, :], rc=rc:
                                  e.matmul(o, a, r, start=(rc == 0), stop=(rc == 3)),
                                  waits=(pb.ww() + nready + wready) if rc == 0 else (), sig=(rc == 3))
                pb.wrote(lastpe)
                st = stg[si % 4]; si += 1
                ev = P.op("act", lambda e, o=st.t[:, :], a=pb.t[:, :]: e.copy(o, a), waits=pb.rw() + st.ww())
                pb.read(ev); st.wrote(ev)
                ev = P.dma("sync", S.dram["kBn"][h * 128:(h + 1) * 128, t0:t0 + 512], st.t[:, :], st.rw(), S.sem_stg[(si - 1) % 4])
                st.read(ev)
            wv = wukv[:, :, :].rearrange("p k (h c) -> p k h c", h=8)
            for tt in range(4):
                for half in range(2):
                    pb = nextps()
                    for rc in range(4):
                        lastpe = P.op("pe", lambda e, o=pb.t[:, :].rearrange("p (h c) -> p h c", h=4),
                                      a=ckn[:, rc, tt * 128:(tt + 1) * 128], r=wv[:, rc, half * 4:(half + 1) * 4, 128:256], rc=rc:
                                      e.matmul(o, a, r, start=(rc == 0), stop=(rc == 3)),
                                      waits=(pb.ww() + nready + wready) if rc == 0 else (), sig=(rc == 3))
                    pb.wrote(lastpe)
                    st = stg[si % 4]; si += 1
                    ev = P.op("dve", lambda e, o=st.t[:, :], a=pb.t[:, :]: e.tensor_copy(o, a), waits=pb.rw() + st.ww())
                    pb.read(ev); st.wrote(ev)
                    r0 = t0 + tt * 128
                    ev = P.dma("sync", S.dram["vB"][r0:r0 + 128, half * 512:(half + 1) * 512], st.t[:, :], st.rw(), S.sem_stg[(si - 1) % 4])
                    st.read(ev)
            prev_norm_reads = [lastpe, sq_read]
        P.emit()


def emit_collectives(P, S, pairs, waits):
    first = True
    for a, o in pairs:
        P.add("pool", lambda e, a=a, o=o: e.collective_compute(
            "AllGather", ALU.bypass, replica_groups=PAIRS, ins=[a.opt()], outs=[o.opt()]),
            waits=waits if first else (), inc=S.sem_cc, amt=1)
        first = False


def exchange_a(P, S):
    d = S.dram
    e1 = P.dma("sync", d["kAh"][:, 0:384], d["kA"][:, 0:384], (), S.sem_ex)
    e1 = P.dma("sync", d["kAh"][:, 384:768], d["kA"][:, T - 384:T], (), S.sem_ex)
    e1 = P.dma("sync", d["vAh"][0:384, :], d["vA"][0:384, :], (), S.sem_ex)
    e1 = P.dma("sync", d["vAh"][384:768, :], d["vA"][T - 384:T, :], (), S.sem_ex)
    pairs = [(d["kAh"], d["kAh_g"]), (d["vAh"], d["vAh_g"]), (d["kr"], d["kr_g"])]
    for i in range(2):
        pairs.append((d["kC"][i * 512:(i + 1) * 512, :], d[f"kC_g{i}"]))
        pairs.append((d["vC"][i * 1024:(i + 1) * 1024, :], d[f"vC_g{i}"]))
    emit_collectives(P, S, pairs, [e1])


def exchange_b(P, S):
    d = S.dram
    pairs = []
    for i in range(2):
        pairs.append((d["kBn"][i * 512:(i + 1) * 512, :], d[f"kBn_g{i}"]))
        pairs.append((d["vB"][i * 1024:(i + 1) * 1024, :], d[f"vB_g{i}"]))
    emit_collectives(P, S, pairs, [])


def na_type(j):
    return 0 if j == 0 else 1 if j == 1 else 3 if j == 14 else 4 if j == 15 else 2


def phase_na(S, l):
    nc = S.nc
    P = Prog(nc, S)
    d = S.dram
    with ExitStack() as es:
        KTs = [sb(S, es, [128, 22 * 128], BF16, "naK") for _ in range(2)]
        Vs = [sb(S, es, [128, 22, 128], BF16, "naV") for _ in range(2)]
        QTs = [sb(S, es, [128, T], BF16, "naQ") for _ in range(2)]
        GTs = [sb(S, es, [128, T], BF16, "naG") for _ in range(2)]
        BT = sb(S, es, [128, 8, 35, 128], BF16, "naB")
        PT = [Buf(sb(S, es, [128, 7 * 128], BF16, "naP")) for _ in range(2)]
        rec = sb(S, es, [128, 512], F32, "narec")
        tmp = sb(S, es, [128, 512], F32, "natmp")
        stg = [Buf(sb(S, es, [128, 512], BF16, "nastg")) for _ in range(2)]
        ps = psum_bufs(S, es)
        psS = [ps[0], ps[1], ps[2], ps[3]]
        psO = [ps[4], ps[5]]
        psR = [ps[6], ps[7]]
        evbs = []
        for h in range(8):
            evbs.append(P.dma("pool", BT[:, h, :, :], S.nabias[l][h], (), S.sem_nab[h]))
        exchange_b(P, S)
        head_done = {}
        loaded = {}

        def load(h):
            KT, V, QT, GT = KTs[h % 2], Vs[h % 2], QTs[h % 2], GTs[h % 2]
            ww = head_done.get(h - 2, [])
            sem = S.sem_ln[h % 2]
            r = slice(h * 128, (h + 1) * 128)
            ev = P.dma("sync", KT[:, 0:384], d["kAh_g"][h * 128:(h + 1) * 128, 384:768], ww, sem)
            ev = P.dma("sync", KT[:, 384:384 + T], d["kA"][r, :], ww, sem)
            ev = P.dma("sync", KT[:, 384 + T:768 + T], d["kAh_g"][1024 + h * 128:1024 + (h + 1) * 128, 0:384], ww, sem)
            ev = P.dma("sync", V[:, 0:3, :], d["vAh_g"][384:768, r].rearrange("(i p) c -> p i c", p=128), ww, sem)
            vav = d["vA"][:, r].rearrange("(i p) c -> p i c", p=128)
            ev = P.dma("sync", V[:, 3:11, :], vav[:, 0:8, :], ww, sem)
            ev = P.dma("sync", V[:, 11:19, :], vav[:, 8:16, :], ww, sem)
            ev = P.dma("sync", V[:, 19:22, :], d["vAh_g"][768:768 + 384, r].rearrange("(i p) c -> p i c", p=128), ww, sem)
            ev = P.dma("sync", QT[:, :], d["qA"][r, :], ww, sem)
            ev = P.dma("sync", GT[:, :], d["gA"][r, :], ww, sem)
            loaded[h] = ev

        load(0)
        load(1)
        last_evac = None
        pti = 0
        sti = 0
        gi = 0
        for h in range(8):
            KT, V, QT, GT = KTs[h % 2], Vs[h % 2], QTs[h % 2], GTs[h % 2]
            r = slice(h * 128, (h + 1) * 128)
            ready = [loaded[h], evbs[h], S.const_ready]
            lp = None
            for qg in range(4):
                pO = psO[gi % 2]
                pR = psR[gi % 2]
                gi += 1
                for jj in range(4):
                    j = qg * 4 + jj
                    ty = na_type(j)
                    slots = list(range(7)) if ty != 2 else [1, 2, 3, 4, 5]
                    ns = len(slots)
                    pt = PT[pti % 2]
                    p1 = psS[(pti % 2) * 2]
                    p2 = psS[(pti % 2) * 2 + 1]
                    pti += 1
                    for idx, s in enumerate(slots):
                        pb = p1 if idx < 4 else p2
                        c0 = (idx % 4) * 128
                        P.op("pe", lambda e, o=pb.t[:, c0:c0 + 128], a=KT[:, (j + s) * 128:(j + s + 1) * 128],
                             q=QT[:, j * 128:(j + 1) * 128]: e.matmul(o, a, q, start=True, stop=False),
                             waits=(pb.ww() + ready) if idx in (0, 4) else (), sig=False)
                        lp = P.op("pe", lambda e, o=pb.t[:, c0:c0 + 128], b=BT[:, h, ty * 7 + s, :]:
                                  e.matmul(o, S.identb[:], b, start=False, stop=True), sig=(idx in (3, ns - 1)))
                        if idx == 3:
                            p1.wrote(lp)
                        if idx == ns - 1 and idx != 3:
                            p2.wrote(lp)
                    e1 = P.op("act", lambda e, o=pt.t[:, 0:512], a=p1.t[:, :]: e.activation(out=o, in_=a, func=AF.Exp),
                              waits=p1.rw() + pt.ww())
                    p1.read(e1)
                    n2 = (ns - 4) * 128
                    e2 = P.op("act", lambda e, o=pt.t[:, 512:512 + n2], a=p2.t[:, 0:n2]: e.activation(out=o, in_=a, func=AF.Exp),
                              waits=p2.rw())
                    p2.read(e2)
                    pt.wrote(e2)
                    for idx, s in enumerate(slots):
                        P.op("pe", lambda e, o=pO.t[:, jj * 128:(jj + 1) * 128], a=V[:, j + s, :], p=pt.t[:, idx * 128:(idx + 1) * 128], idx=idx, ns=ns:
                             e.matmul(o, a, p, start=(idx == 0), stop=(idx == ns - 1)),
                             waits=(pt.rw() + pO.ww() + pR.ww()) if idx == 0 else (), sig=False)
                        lp = P.op("pe", lambda e, o=pR.t[:, jj * 128:(jj + 1) * 128], p=pt.t[:, idx * 128:(idx + 1) * 128], idx=idx, ns=ns:
                                  e.matmul(o, S.onesb[:], p, start=(idx == 0), stop=(idx == ns - 1)), sig=(idx == ns - 1))
                    pt.read(lp)
                pO.wrote(lp)
                pR.wrote(lp)
                ea = P.op("dve", lambda e, a=pR.t[:, :]: e.reciprocal(rec[:], a), waits=pR.rw() + ([last_evac] if last_evac else []))
                pR.read(ea)
                eb = P.op("dve", lambda e, a=pO.t[:, :]: e.tensor_tensor(tmp[:], a, rec[:], ALU.mult), waits=pO.rw() + [ea])
                pO.read(eb)
                st = stg[sti % 2]; sti += 1
                ec = P.op("dve", lambda e, o=st.t[:, :], g=GT[:, qg * 512:(qg + 1) * 512]: e.tensor_tensor(o, tmp[:], g, ALU.mult),
                          waits=[eb] + st.ww())
                st.wrote(ec)
                last_evac = ec
                ev = P.dma("sync", d["yA"][r, qg * 512:(qg + 1) * 512], st.t[:, :], st.rw(), S.sem_stg[(sti - 1) % 2])
                st.read(ev)
            head_done[h] = [lp, last_evac]
            if h + 2 < 8:
                load(h + 2)
        P.emit()


def phase_dense(S, l, kind):
    nc = S.nc
    P = Prog(nc, S)
    d = S.dram
    mla = (kind == "mla")
    nm = 1 if mla else 2
    with ExitStack() as es:
        KTs = [sb(S, es, [128, SEQ], BF16, "dK") for _ in range(2)]
        Vs = [sb(S, es, [128, 32, 128], BF16, "dV") for _ in range(2)]
        GTs = [sb(S, es, [128, T], BF16, "dG") for _ in range(2)]
        if mla:
            KR = sb(S, es, [128, SEQ], BF16, "dKr")
            QTs = [sb(S, es, [128, T], BF16, "dQ") for _ in range(2)]
            QRs = [sb(S, es, [128, T], BF16, "dQr") for _ in range(2)]
        else:
            QAs = [sb(S, es, [128, T], BF16, "dQa") for _ in range(2)]
            QBs = [sb(S, es, [128, T], BF16, "dQb") for _ in range(2)]
        NPT = 12
        PT = [Buf(sb(S, es, [128, 512], BF16, "dP")) for _ in range(NPT)]
        rsum = [sb(S, es, [128, 512], F32, "drsum") for _ in range(nm)]
        oraw = [sb(S, es, [128, 512], F32, "doraw") for _ in range(nm)]
        rec = sb(S, es, [128, 512], F32, "drec")
        yy = sb(S, es, [128, 512], F32, "dyy")
        sqy = sb(S, es, [128, 512], F32, "dsq")
        rs = sb(S, es, [128, 512], F32, "drs")
        stg = [Buf(sb(S, es, [128, 512], BF16, "dstg")) for _ in range(2)]
        ps = psum_bufs(S, es)
        psO = ps[4:6]
        psR = ps[6:8]
        psS = (ps[0:4] + [ps[5], ps[7]]) if mla else ps[0:4]
        NS = len(psS)
        zev = None
        kr_ev = None
        if mla:
            zev = P.op("dve", lambda e: e.memset(KR[:, :], 0.0))
            for i in range(2):
                zev = P.op("dve", lambda e, q=QRs[i]: e.memset(q[:, :], 0.0))
            kr_ev = P.dma("sync", KR[0:64, 0:T], d["kr_g"][0:64, :], [zev], S.sem_ld3)
            kr_ev = P.dma("sync", KR[0:64, T:SEQ], d["kr_g"][64:128, :], [zev], S.sem_ld3)
        else:
            for i in range(2):
                P.op("dve", lambda e, q=QAs[i]: e.memset(q[:, :], 0.0))
                zev = P.op("dve", lambda e, q=QBs[i]: e.memset(q[:, :], 0.0))
        kgs = [d["kBn_g0"], d["kBn_g1"]] if mla else [d["kC_g0"], d["kC_g1"]]
        vgs = [d["vB_g0"], d["vB_g1"]] if mla else [d["vC_g0"], d["vC_g1"]]
        qsrc = d["qBn"] if mla else d["qC"]
        gsrc = d["gB"] if mla else d["gC"]
        ydst = d["yB"] if mla else d["yC"]
        head_done = {}
        loaded = {}

        def load(h):
            b = h % 2
            r = slice(h * 128, (h + 1) * 128)
            ww = head_done.get(h - 2, []) + [zev]
            sem = S.sem_ln[b]
            kg = kgs[h // 4]
            hr = (h % 4) * 128
            ev = P.dma("sync", KTs[b][:, 0:T], kg[hr:hr + 128, :], ww, sem)
            ev = P.dma("sync", KTs[b][:, T:SEQ], kg[512 + hr:512 + hr + 128, :], ww, sem)
            for rk_ in range(2):
                for th in range(2):
                    vi = rk_ * 2 + th
                    ev = P.dma("sync", Vs[b][:, vi * 8:(vi + 1) * 8, :],
                               vgs[th][rk_ * 1024:(rk_ + 1) * 1024, r].rearrange("(i p) c -> p i c", p=128), ww, sem)
            ev = P.dma("sync", GTs[b][:, :], gsrc[r, :], ww, sem)
            if mla:
                ev = P.dma("sync", QTs[b][:, :], qsrc[r, :], ww, sem)
                ev = P.dma("sync", QRs[b][0:64, :], d["qBr"][h * 64:(h + 1) * 64, :], ww, sem)
            else:
                ev = P.dma("sync", QAs[b][0:64, :], qsrc[h * 128:h * 128 + 64, :], ww, sem)
                ev = P.dma("sync", QBs[b][64:128, :], qsrc[h * 128 + 64:h * 128 + 128, :], ww, sem)
            loaded[h] = ev

        load(0)
        load(1)
        epi_done = [None]
        deferred = []
        si = 0
        pi = 0
        sti = 0
        for h in range(8):
            b = h % 2
            KT, V, GT = KTs[b], Vs[b], GTs[b]
            r = slice(h * 128, (h + 1) * 128)
            ready = [loaded[h], S.const_ready] + ([kr_ev] if mla else [])
            lp = None
            for qc in range(4):
                q0 = qc * 512
                pend = []
                units = [(kt, m) for kt in range(32) for m in range(nm)]

                def issue_s(u):
                    nonlocal si, pi
                    kt, m = u
                    pb = psS[si % NS]; si += 1
                    if mla:
                        P.op("pe", lambda e, o=pb.t[:, :], a=KT[:, kt * 128:(kt + 1) * 128], q=QTs[b][:, q0:q0 + 512]:
                             e.matmul(o, a, q, start=True, stop=False), waits=pb.ww() + ready, sig=False)
                        lps = P.op("pe", lambda e, o=pb.t[:, :], a=KR[:, kt * 128:(kt + 1) * 128], q=QRs[b][:, q0:q0 + 512]:
                                   e.matmul(o, a, q, start=False, stop=True))
                    else:
                        lps = P.op("pe", lambda e, o=pb.t[:, :], a=KT[:, kt * 128:(kt + 1) * 128],
                                   q=(QAs[b] if m == 0 else QBs[b])[:, q0:q0 + 512]: e.matmul(o, a, q, start=True, stop=True),
                                   waits=pb.ww() + ready)
                    pb.wrote(lps)
                    pt = PT[pi % NPT]; pi += 1
                    ee = P.op("act", lambda e, o=pt.t[:, :], a=pb.t[:, :]: e.activation(out=o, in_=a, func=AF.Exp),
                              waits=pb.rw() + pt.ww())
                    pb.read(ee)
                    pt.wrote(ee)
                    pend.append(pt)

                LA = NS - 1
                for u in units[:LA]:
                    issue_s(u)
                G = 4 * nm
                for gidx, g0 in enumerate(range(0, len(units), G)):
                    if gidx in ((2, 4) if mla else (1, 3, 5)) and deferred:
                        deferred.pop(0)()
                    grp = list(range(g0, g0 + G))
                    for ui in grp:
                        kt, m = units[ui]
                        pt = pend[ui]
                        pO = psO[m]
                        first = (kt == 0)
                        last = (kt == 31)
                        P.op("pe", lambda e, o=pO.t[:, :], a=V[:, kt, :], p=pt.t[:, :], first=first, last=last:
                             e.matmul(o, a, p, start=first, stop=last),
                             waits=pt.rw() + ((pO.ww() + psR[m].ww()) if first else []), sig=False)
                        if ui + LA < len(units):
                            issue_s(units[ui + LA])
                    for m in range(nm):
                        for ui in grp:
                            kt, mm = units[ui]
                            if mm != m:
                                continue
                            pt = pend[ui]
                            pR = psR[m]
                            j = kt % 4
                            lp = P.op("pe", lambda e, o=pR.t[32 * j:32 * j + 32, :], p=pt.t[:, :], kt=kt, j=j:
                                      e.matmul(o, S.onesb[:, 0:32], p, start=(kt < 4), stop=(kt >= 28), tile_position=(0, 32 * j)),
                                      sig=(j == 3))
                        for ui in grp:
                            if units[ui][1] == m:
                                pend[ui].read(lp)
                        if units[grp[-1]][0] == 31:
                            psO[m].wrote(lp)
                            psR[m].wrote(lp)
                pw = [epi_done[0]] if epi_done[0] else []
                cps = []
                for m in range(nm):
                    c1 = P.op("dve", lambda e, o=rsum[m][:], a=psR[m].t[:, :]: e.tensor_copy(o, a),
                              waits=psR[m].rw() + pw + ([lastpe_n[1]] if lastpe_n[1] else []))
                    psR[m].read(c1)
                    c2 = P.op("dve", lambda e, o=oraw[m][:], a=psO[m].t[:, :]: e.tensor_copy(o, a), waits=psO[m].rw() + pw)
                    psO[m].read(c2)
                    cps.append((c1, c2))
                state = {}

                def stage_a(m, cps=cps, state=state):
                    nonlocal si
                    outs = state.setdefault("outs", [])
                    c1, c2 = cps[m]
                    pn2 = psS[si % NS]; si += 1
                    l2 = P.op("pe", lambda e, o=pn2.t[:, :], rr=rsum[m][:]: e.matmul(o, S.sel4[:], rr, start=True, stop=True),
                              waits=pn2.ww() + [c1])
                    pn2.wrote(l2)
                    lastpe_n[1] = l2
                    c3 = P.op("dve", lambda e, a=pn2.t[:, :]: e.reciprocal(rec[:], a), waits=pn2.rw())
                    pn2.read(c3)
                    c4 = P.op("dve", lambda e, o=oraw[m][:]: e.tensor_tensor(o, o, rec[:], ALU.mult), waits=[c3, c2])
                    outs.append(c4)
                    if (not mla) and m == 1:
                        e3 = P.op("dve", lambda e: e.scalar_tensor_tensor(yy[:], oraw[1][:], S.neglam[l][:, 0:1], oraw[0][:], ALU.mult, ALU.add),
                                  waits=[outs[1], S.lam_ready])
                        e4 = P.op("pool", lambda e: e.tensor_tensor(sqy[:], yy[:], yy[:], ALU.mult),
                                  waits=[e3] + ([lastpe_n[0]] if lastpe_n[0] else []))
                        state["e4"] = e4

                def stage_b(state=state, GT=GT, q0=q0, r=r, h=h, qc=qc, lp=lp):
                    nonlocal si, sti
                    st = stg[sti % 2]; sti += 1
                    if mla:
                        ec = P.op("dve", lambda e, o=st.t[:, :], g=GT[:, q0:q0 + 512]: e.tensor_tensor(o, oraw[0][:], g, ALU.mult),
                                  waits=[state["outs"][0]] + st.ww())
                    else:
                        pn = psS[si % NS]; si += 1
                        lpn = P.op("pe", lambda e, o=pn.t[:, :]: e.matmul(o, S.onesf[:], sqy[:], start=True, stop=True),
                                   waits=pn.ww() + [state["e4"]])
                        pn.wrote(lpn)
                        lastpe_n[0] = lpn
                        e5 = rstd_from_ps(P, S, pn, 1.0 / 128, RMS_EPS, rs[:], rs[:])
                        e6 = P.op("dve", lambda e: e.tensor_tensor(yy[:], yy[:], rs[:], ALU.mult), waits=[e5])
                        ec = P.op("dve", lambda e, o=st.t[:, :], g=GT[:, q0:q0 + 512]:
                                  e.scalar_tensor_tensor(o, yy[:], S.subc[l][:, 0:1], g, ALU.mult, ALU.mult),
                                  waits=[e6] + st.ww())
                    st.wrote(ec)
                    epi_done[0] = ec
                    ev2 = P.dma("sync", ydst[r, q0:q0 + 512], st.t[:, :], st.rw(), S.sem_stg[(sti - 1) % 2])
                    st.read(ev2)
                    if qc == 3:
                        head_done[h] = [lp, ec]
                        if h + 2 < 8:
                            load(h + 2)

                for m in range(nm):
                    deferred.append(lambda m=m, f=stage_a: f(m))
                deferred.append(stage_b)
        while deferred:
            deferred.pop(0)()
        P.emit()


lastpe_n = [None, None]


def phase_outproj(S, l, tb, xsrc):
    nc = S.nc
    d = S.dram
    tb0 = tb * 1024
    with ExitStack() as es0:
        mT = sb(S, es0, [128, 32, 1024], BF16, "mT")
        P = Prog(nc, S)
        with ExitStack() as es:
            yT = sb(S, es, [128, 3, 8, 1024], BF16, "yT")
            wo = [Buf(sb(S, es, [128, 3, 8, 512], BF16, "wo")) for _ in range(2)]
            gt = [Buf(sb(S, es, [128, 3, 512], BF16, "gt")) for _ in range(2)]
            tt_ = [[sb(S, es, [128, 512], F32, "t5") for _ in range(3)] for _ in range(2)]
            ps = psum_bufs(S, es)
            psi = 0
            yev = None
            for j, nmy in enumerate(("yA", "yB", "yC")):
                yev = P.dma("sync", yT[:, j, :, :], d[nmy][:, tb0:tb0 + 1024].rearrange("(wc p) t -> p wc t", p=128), (), S.sem_ld3)
            wsrc = [S.w_o[j][l] for j in range(3)]

            def load_wo(dg):
                b = wo[dg % 2]
                ww = b.ww()
                ev = None
                for j in range(3):
                    ev = P.dma("pool", b.t[:, j, :, :], wsrc[j][:, dg * 512:(dg + 1) * 512].rearrange("(wc p) n -> p wc n", p=128),
                               ww, S.sem_wb[dg % 2])
                b.wrote(ev)

            gview = d["gm"].rearrange("(j dc p) t -> dc p j t", j=3, p=128)
            load_wo(0)
            load_wo(1)
            gi = 0
            prev_tt = [None, None]
            mT_ev = []
            for dg in range(8):
                b = wo[dg % 2]
                lp = None
                for ds in range(4):
                    dc = dg * 4 + ds
                    for tc in range(2):
                        g = gt[gi % 2]
                        tset = tt_[gi % 2]
                        pv = prev_tt[gi % 2]
                        gi += 1
                        ev = P.dma("sync", g.t[:, :, :], gview[dc][:, :, tb0 + tc * 512:tb0 + (tc + 1) * 512], g.ww(), S.sem_ld2[(gi - 1) % 2])
                        g.wrote(ev)
                        pbs = []
                        for j in range(3):
                            pb = ps[psi % 8]; psi += 1
                            for wc in range(8):
                                lp = P.op("pe", lambda e, o=pb.t[:, :], a=b.t[:, j, wc, ds * 128:(ds + 1) * 128],
                                          r=yT[:, j, wc, tc * 512:(tc + 1) * 512], wc=wc:
                                          e.matmul(o, a, r, start=(wc == 0), stop=(wc == 7)),
                                          waits=(pb.ww() + b.rw() + [yev]) if wc == 0 else (), sig=(wc == 7))
                            pb.wrote(lp)
                            pbs.append(pb)
                        evs = []
                        for j in range(3):
                            e1 = P.op("dve", lambda e, o=tset[j][:], a=pbs[j].t[:, :], gg=g.t[:, j, :]: e.tensor_tensor(o, a, gg, ALU.mult),
                                      waits=pbs[j].rw() + g.rw() + ([pv] if pv else []))
                            pbs[j].read(e1)
                            evs.append(e1)
                        g.read(evs[-1])
                        e2 = P.op("pool", lambda e, a=tset[0][:], c=tset[1][:]: e.tensor_tensor(a, a, c, ALU.add), waits=evs)
                        e3 = P.op("pool", lambda e, o=mT[:, dc, tc * 512:(tc + 1) * 512], a=tset[0][:], c=tset[2][:]:
                                  e.tensor_tensor(o, a, c, ALU.add), waits=[e2])
                        prev_tt[(gi - 1) % 2] = e3
                        mT_ev = [e3]
                b.read(lp)
                if dg + 2 < 8:
                    load_wo(dg + 2)
            P.emit()
        P = Prog(nc, S)
        with ExitStack() as es:
            wob = [Buf(sb(S, es, [128, 32, 512], BF16, "wout")) for _ in range(2)]
            stf = [Buf(sb(S, es, [128, 512], F32, "stf")) for _ in range(4)]
            xcs = [Buf(sb(S, es, [128, 512], F32, "xc")) for _ in range(4)]
            ps = psum_bufs(S, es)
            psi = 0
            sti = 0
            wsrc = S.w_out[l]

            def load_w(eg):
                b = wob[eg % 2]
                ww = b.ww()
                src = wsrc[:, eg * 512:(eg + 1) * 512].rearrange("(dc p) n -> p dc n", p=128)
                ev = None
                for i in range(4):
                    ev = P.dma("pool", b.t[:, 8 * i:8 * i + 8, :], src[:, 8 * i:8 * i + 8, :], ww, S.sem_wb[eg % 2])
                b.wrote(ev)

            load_w(0)
            load_w(1)
            zrs = P.op("dve", lambda e: e.memset(S.rowsum[:, tb * 8:(tb + 1) * 8, :], 0.0))
            for eg in range(8):
                b = wob[eg % 2]
                lp = None
                for tt in range(8):
                    pb = ps[psi % 8]; psi += 1
                    for dc in range(32):
                        lp = P.op("pe", lambda e, o=pb.t[:, :], a=mT[:, dc, tt * 128:(tt + 1) * 128], r=b.t[:, dc, :], dc=dc:
                                  e.matmul(o, a, r, start=(dc == 0), stop=(dc == 31)),
                                  waits=(pb.ww() + b.rw()) if dc == 0 else (), sig=(dc == 31))
                    pb.wrote(lp)
                    st = stf[sti % 4]
                    xc = xcs[sti % 4]
                    sti += 1
                    r0 = tb0 + tt * 128
                    evx = P.dma("act", xc.t[:, :], xsrc[r0:r0 + 128, eg * 512:(eg + 1) * 512], xc.ww(), S.sem_ln[(sti - 1) % 4])
                    xc.wrote(evx)
                    ev = P.op("dve", lambda e, o=st.t[:, :], a=pb.t[:, :], x=xc.t[:, :], acc=S.rowsum[:, tb * 8 + tt, eg:eg + 1]:
                              e.scalar_tensor_tensor(o, x, ALPHA, a, ALU.mult, ALU.add, accum_out=acc),
                              waits=pb.rw() + st.ww() + xc.rw() + [zrs])
                    pb.read(ev); st.wrote(ev); xc.read(ev)
                    ev = P.dma("sync", d["yout"][r0:r0 + 128, eg * 512:(eg + 1) * 512], st.t[:, :], st.rw(), S.sem_stg[(sti - 1) % 4])
                    st.read(ev)
                b.read(lp)
                if eg + 2 < 8:
                    load_w(eg + 2)
            P.emit()


def phase_ln(S, l, xsrc, xdst):
    nc = S.nc
    P = Prog(nc, S)
    d = S.dram
    NT = T // 128
    with ExitStack() as es:
        lng = sb(S, es, [128, D], F32, "lng")
        lnb = sb(S, es, [128, D], F32, "lnb")
        yt = [Buf(sb(S, es, [128, D], F32, "lny")) for _ in range(4)]
        xt = [Buf(sb(S, es, [128, D], F32, "lnx")) for _ in range(4)]
        sts = [sb(S, es, [128, 8], F32, "lnst") for _ in range(4)]
        cev = P.dma("sync", lng[:], S.lngD[l], (), S.sem_c)
        cev = P.dma("sync", lnb[:], S.lnbD[l], (), S.sem_c)
        evR, evQ, evT, evN = {}, {}, {}, {}

        def load(i):
            yb = yt[i % 4]
            r0 = i * 128
            ev = P.dma("sync", yb.t[:], d["yout"][r0:r0 + 128, :], yb.ww(), S.sem_ln[i % 4])
            yb.wrote(ev)

        def st_R(i):
            yb, st1 = yt[i % 4], sts[i % 4]
            e0 = P.op("dve", lambda e, st1=st1: e.memset(st1[:, :], 0.0), waits=[evN[i - 4]] if (i - 4) in evN else [])
            e1 = P.op("dve", lambda e, rsrc=S.rowsum[:, i, :], st1=st1: e.reduce_sum(st1[:, 0:1], rsrc, axis=mybir.AxisListType.X),
                      waits=[e0])
            evR[i] = P.op("dve", lambda e, st1=st1: e.tensor_scalar(st1[:, 1:2], st1[:, 0:1], -1.0 / D, None, ALU.mult), waits=[e1])

        def st_Q(i):
            yb, xb, st1 = yt[i % 4], xt[i % 4], sts[i % 4]
            evQ[i] = P.op("act", lambda e, y=yb.t[:], x=xb.t[:], st1=st1:
                          e.activation(out=x, in_=y, func=AF.Square, bias=st1[:, 1:2], scale=1.0, accum_out=st1[:, 2:3]),
                          waits=[evR[i]] + xb.ww() + yb.rw())

        def st_T(i):
            st1 = sts[i % 4]
            e7 = P.op("dve", lambda e, st1=st1: e.tensor_scalar(st1[:, 3:4], st1[:, 2:3], 1.0 / D, LN_EPS, ALU.mult, ALU.add), waits=[evQ[i]])
            e8 = P.op("act", lambda e, st1=st1: e.sqrt(st1[:, 4:5], st1[:, 3:4]), waits=[e7])
            e9 = P.op("dve", lambda e, st1=st1: e.reciprocal(st1[:, 5:6], st1[:, 4:5]), waits=[e8])
            evT[i] = P.op("dve", lambda e, st1=st1: e.tensor_tensor(st1[:, 6:7], st1[:, 1:2], st1[:, 5:6], ALU.mult), waits=[e9])

        def st_N(i):
            yb, st1 = yt[i % 4], sts[i % 4]
            evN[i] = P.op("act", lambda e, y=yb.t[:], st1=st1:
                          e.activation(out=y, in_=y, func=AF.Identity, bias=st1[:, 6:7], scale=st1[:, 5:6]), waits=[evT[i]])

        evM = {}

        def st_M(i):
            yb, xb = yt[i % 4], xt[i % 4]
            e10 = P.op("dve", lambda e, y=yb.t[:], x=xb.t[:]: e.tensor_tensor(x, y, lng[:], ALU.mult), waits=[evN[i], cev])
            yb.read(e10)
            evM[i] = e10

        def st_H(i):
            yb, xb = yt[i % 4], xt[i % 4]
            r0 = i * 128
            e10 = evM[i]
            e11 = P.op("dve", lambda e, x=xb.t[:]: e.tensor_tensor(x, x, lnb[:], ALU.add), waits=[e10])
            xb.wrote(e11)
            ev = P.dma("pool", xdst[r0:r0 + 128, :], xb.t[:], [e11], S.sem_stg[i % 4])
            xb.read(ev)

        for i in range(4):
            load(i)
        st_R(0)
        st_Q(0)
        st_T(0)
        st_R(1)
        for i in range(NT):
            st_N(i)
            st_M(i)
            if i + 1 < NT:
                st_Q(i + 1)
                st_T(i + 1)
            st_H(i)
            if i + 2 < NT:
                st_R(i + 2)
            if i + 4 < NT:
                load(i + 4)
        P.emit()


def build(stop_after=None, debug_out=(), nlayers=L):
    nc = bass.Bass("TRN2", target_bir_lowering=False)
    S = State()
    S.nc = nc
    S.uid = 0
    S.allsems = []
    S.seen = {e: {} for e in ENGS}
    lastpe_n[0] = None
    lastpe_n[1] = None

    def din(name, shape, dt=F32):
        return nc.dram_tensor(name, shape, dt, kind="ExternalInput").ap()

    S.x = din("x", [T, D])
    w_in_all = din("w_in", [L, D, INW])
    S.w_in = [w_in_all[l] for l in range(L)]
    w_uq = din("w_uq", [L, 1536, 1536]); S.w_uq = [w_uq[l] for l in range(L)]
    w_ukv = din("w_ukv", [L, 512, 2048]); S.w_ukv = [w_ukv[l] for l in range(L)]
    S.w_o = []
    for nm in ("w_o_a", "w_o_b", "w_o_c"):
        t = din(nm, [L, 1024, D])
        S.w_o.append([t[l] for l in range(L)])
    w_out = din("w_out", [L, D, D]); S.w_out = [w_out[l] for l in range(L)]
    S.identD = din("ident", [128, 128])
    S.pswapD = din("pswap", [128, 128])
    S.sel4D = din("sel4", [128, 128])
    S.cosD = din("cosT", [128, T])
    S.sinD = din("sinT", [128, T])
    S.bmD = din("bm", [L, 128, 96])
    S.gqD = din("gq", [L, 128, 12])
    S.gkvD = din("gkv", [L, 128, 4])
    S.sublnD = din("subln", [L, 128, 1])
    S.lamD = din("lamrep", [L, 128, 4, 64])
    lng = din("lng", [L, 128, D]); S.lngD = [lng[l] for l in range(L)]
    lnb = din("lnb", [L, 128, D]); S.lnbD = [lnb[l] for l in range(L)]
    nab = din("nabias", [L, 8, 128, 35, 128])
    S.nabias = [[nab[l][h] for h in range(8)] for l in range(L)]
    S.out = nc.dram_tensor("out", [T, D], F32, kind="ExternalOutput").ap()

    S.dram = {}

    def scr(name, shape, dt=BF16):
        if name in debug_out:
            S.dram[name] = nc.dram_tensor(name, shape, dt, kind="ExternalOutput").ap()
        else:
            S.dram[name] = nc.dram_tensor(name, shape, dt).ap()

    for nm in ("qA", "kA", "gA", "gB", "qC", "kC", "gC", "qBn", "kBn", "yA", "yB", "yC"):
        scr(nm, [1024, T])
    for nm in ("vA", "vC", "vB"):
        scr(nm, [T, 1024])
    scr("cq", [1536, T]); scr("ckv", [512, T]); scr("kr", [64, T]); scr("gm", [12288, T])
    scr("qBr", [512, T])
    scr("kAh", [1024, 768]); scr("kAh_g", [2048, 768])
    scr("vAh", [768, 1024]); scr("vAh_g", [1536, 1024])
    scr("kr_g", [128, T])
    for i in range(2):
        scr(f"kBn_g{i}", [1024, T]); scr(f"kC_g{i}", [1024, T])
        scr(f"vB_g{i}", [2048, 1024]); scr(f"vC_g{i}", [2048, 1024])
    scr("yout", [T, D], F32)
    scr("x1", [T, D], F32)

    with ExitStack() as es:
        S.esem = {e: newsem(S, es, f"e_{e}") for e in ("act", "dve", "pool", "pe")}
        S.sem_xs = [newsem(S, es, f"xs{i}") for i in range(4)]
        S.sem_nab = [newsem(S, es, f"nab{i}") for i in range(8)]
        S.sem_wb = [newsem(S, es, f"wb{i}") for i in range(2)]
        S.sem_stg = [newsem(S, es, f"stg{i}") for i in range(4)]
        S.sem_ld2 = [newsem(S, es, f"ld2{i}") for i in range(2)]
        S.sem_c = newsem(S, es, "const")
        S.sem_out = newsem(S, es, "outs")
        S.sem_w2 = newsem(S, es, "w2")
        S.sem_ex = newsem(S, es, "ex")
        S.sem_cc = newsem(S, es, "cc")
        S.sem_ld3 = newsem(S, es, "ld3")
        S.sem_ld4 = newsem(S, es, "ld4")
        S.sem_ln = [newsem(S, es, f"ln{i}") for i in range(6)]
        S.ident = sb(S, es, [128, 128], F32, "ident")
        S.identb = sb(S, es, [128, 128], BF16, "identb")
        S.pswap = sb(S, es, [128, 128], F32, "pswap")
        S.sel4 = sb(S, es, [128, 128], F32, "sel4")
        S.onesb = sb(S, es, [128, 128], BF16, "onesb")
        S.onesf = sb(S, es, [128, 128], F32, "onesf")
        S.cosT = sb(S, es, [128, T], F32, "cosT")
        S.sinT = sb(S, es, [128, T], F32, "sinT")
        S.bm = [sb(S, es, [128, 96], F32, "bm") for _ in range(L)]
        S.gq = [sb(S, es, [128, 12], F32, "gq") for _ in range(L)]
        S.gkv = [sb(S, es, [128, 4], F32, "gkv") for _ in range(L)]
        S.subc = [sb(S, es, [128, 1], F32, "subc") for _ in range(L)]
        S.neglam = [sb(S, es, [128, 1], F32, "neglam") for _ in range(L)]
        S.rowsum = sb(S, es, [128, T // 128, 8], F32, "rowsum")
        lamt = sb(S, es, [128, 4, 64], F32, "lamt")
        lamw = sb(S, es, [128, 8], F32, "lamw")
        P = Prog(nc, S)
        P.dma("sync", S.ident[:], S.identD[:, :], (), S.sem_c)
        P.dma("sync", S.pswap[:], S.pswapD[:, :], (), S.sem_c)
        P.dma("sync", S.sel4[:], S.sel4D[:, :], (), S.sem_c)
        P.dma("sync", S.cosT[:], S.cosD[:, :], (), S.sem_c)
        P.dma("sync", S.sinT[:], S.sinD[:, :], (), S.sem_c)
        cev = None
        for l in range(L):
            P.dma("sync", S.bm[l][:], S.bmD[l], (), S.sem_c)
            P.dma("sync", S.gq[l][:], S.gqD[l], (), S.sem_c)
            P.dma("sync", S.gkv[l][:], S.gkvD[l], (), S.sem_c)
            cev = P.dma("sync", S.subc[l][:], S.sublnD[l], (), S.sem_c)
        e0 = P.op("dve", lambda e: e.tensor_copy(S.identb[:], S.ident[:]), waits=[cev])
        P.op("dve", lambda e: e.memset(S.onesb[:], 1.0))
        e1 = P.op("dve", lambda e: e.memset(S.onesf[:], 1.0))
        S.const_ready = e1
        ev = e1
        for l in range(L):
            lam_init = 0.8 - 0.6 * math.exp(-0.3 * l)
            lev = P.dma("sync", lamt[:], S.lamD[l], [ev], S.sem_c)
            a = P.op("dve", lambda e: e.tensor_tensor(lamt[:, 0, :], lamt[:, 0, :], lamt[:, 1, :], ALU.mult), waits=[lev])
            a = P.op("dve", lambda e: e.tensor_tensor(lamt[:, 2, :], lamt[:, 2, :], lamt[:, 3, :], ALU.mult), waits=[a])
            a = P.op("dve", lambda e: e.reduce_sum(lamw[:, 0:1], lamt[:, 0, :], axis=mybir.AxisListType.X), waits=[a])
            a = P.op("dve", lambda e: e.reduce_sum(lamw[:, 1:2], lamt[:, 2, :], axis=mybir.AxisListType.X), waits=[a])
            b = P.op("act", lambda e: e.activation(out=lamw[:, 2:4], in_=lamw[:, 0:2], func=AF.Exp), waits=[a])
            a = P.op("dve", lambda e: e.tensor_tensor(lamw[:, 4:5], lamw[:, 3:4], lamw[:, 2:3], ALU.subtract), waits=[b])
            a = P.op("dve", lambda e, l=l, li=lam_init: e.tensor_scalar(S.neglam[l][:], lamw[:, 4:5], -li, None, ALU.add), waits=[a])
            a = P.op("dve", lambda e, l=l, li=lam_init: e.tensor_scalar(S.subc[l][:], S.subc[l][:], 1.0 - li, None, ALU.mult), waits=[a])
            ev = a
        S.lam_ready = ev
        P.emit()

        for l in range(nlayers):
            xsrc = S.x if l == 0 else S.dram["x1"]
            xdst = S.dram["x1"] if l < L - 1 else S.out
            for tb in range(2):
                phase_inproj(S, l, tb, xsrc)
            if stop_after == "inproj":
                break
            phase_mla_proj(S, l)
            if stop_after == "mla_proj":
                break
            phase_na(S, l)
            if stop_after == "na":
                break
            phase_dense(S, l, "mla")
            if stop_after == "mla":
                break
            phase_dense(S, l, "diff")
            if stop_after == "diff":
                break
            for tb in range(2):
                phase_outproj(S, l, tb, xsrc)
            if stop_after == "outproj":
                break
            phase_ln(S, l, xsrc, xdst)

        if stop_after is not None or nlayers < L:
            P = Prog(nc, S)
            with ExitStack() as es2:
                o = sb(S, es2, [128, 512], F32, "dummy")
                ev = P.op("dve", lambda e: e.memset(o[:], 0.0))
                ev = P.dma("sync", S.out[0:128, 0:512], o[:], [ev], S.sem_out)
                P.emit()
    return nc


def rope_tables(hf):
    half = 32
    inv = (10000.0 ** (-np.arange(half, dtype=np.float32) * 2.0 / 64)).astype(np.float32)
    pos = (np.arange(T) + hf * T).astype(np.float32)
    ang = pos[:, None] * inv[None, :]
    c = np.cos(ang).T.astype(np.float32)
    s = np.sin(ang).T.astype(np.float32)
    cosT = np.concatenate([c, c, c, c], axis=0)
    sinT = np.concatenate([-s, s, -s, s], axis=0)
    return np.ascontiguousarray(cosT), np.ascontiguousarray(sinT)


def na_bias_table(rpb, hf):
    out = np.full((5, 7, 8, 128, 128), NEG, np.float32)
    reps = [0, 1, 5, 14, 15]
    kl = np.arange(128)
    krl, wk = kl // 64, kl % 64
    ql = np.arange(128)
    qrl, wq = ql // 64, ql % 64
    for t, j in enumerate(reps):
        gj = hf * 16 + j
        for s in range(7):
            gk = gj + s - 3
            krow = 2 * gk + krl
            qrow = 2 * gj + qrl
            start = np.clip(qrow - 4, 0, 56)
            cstart = np.clip(wq - 8, 0, 48)
            valid = ((krow[:, None] >= 0) & (krow[:, None] < 64)
                     & (krow[:, None] >= start[None, :]) & (krow[:, None] < start[None, :] + 8)
                     & (wk[:, None] >= cstart[None, :]) & (wk[:, None] < cstart[None, :] + 16))
            dr = np.clip(krow[:, None] - qrow[None, :] + 7, 0, 14)
            dc = np.clip(wk[:, None] - wq[None, :] + 15, 0, 30)
            vals = rpb[:, dr, dc]
            out[t, s] = np.where(valid[None], vals, np.float32(NEG))
    return np.ascontiguousarray(out.reshape(35, 8, 128, 128).transpose(1, 2, 0, 3))


def make_in_maps(inputs):
    f = lambda k: np.ascontiguousarray(np.asarray(inputs[k], dtype=np.float32))
    x = f("x")
    shared = {
        "w_in": f("w_in"), "w_uq": f("w_uq"), "w_ukv": f("w_ukv"),
        "w_o_a": f("w_o_a"), "w_o_b": f("w_o_b"), "w_o_c": f("w_o_c"), "w_out": f("w_out"),
        "ident": np.eye(128, dtype=np.float32),
        "pswap": np.ascontiguousarray(np.eye(128, dtype=np.float32)[np.arange(128) ^ 32]),
        "sel4": np.ascontiguousarray(np.broadcast_to((np.arange(128) % 32 == 0).astype(np.float32)[:, None], (128, 128))),
        "bm": np.ascontiguousarray(f("b_merge").reshape(L, 96, 128).transpose(0, 2, 1)),
        "gq": np.ascontiguousarray(f("q_norm").reshape(L, 12, 128).transpose(0, 2, 1)),
        "gkv": np.ascontiguousarray(f("kv_norm").reshape(L, 4, 128).transpose(0, 2, 1)),
        "subln": np.ascontiguousarray(f("diff_subln").reshape(L, 128, 1)),
        "lamrep": np.ascontiguousarray(np.broadcast_to(
            np.stack([f("lam_q1"), f("lam_k1"), f("lam_q2"), f("lam_k2")], axis=1)[:, None], (L, 128, 4, 64))),
        "lng": np.ascontiguousarray(np.broadcast_to(f("ln_g")[:, None, :], (L, 128, D))),
        "lnb": np.ascontiguousarray(np.broadcast_to(f("ln_b")[:, None, :], (L, 128, D))),
    }
    rpb = f("na_rpb")
    nab = [np.stack([na_bias_table(rpb[l], hf) for l in range(L)]) for hf in range(2)]
    tabs = [rope_tables(hf) for hf in range(2)]
    maps = []
    for c in range(8):
        b, hf = c // 2, c % 2
        m = dict(shared)
        m["x"] = np.ascontiguousarray(x[b, hf * T:(hf + 1) * T, :])
        m["cosT"], m["sinT"] = tabs[hf]
        m["nabias"] = nab[hf]
        maps.append(m)
    return maps


def kernel(**inputs):
    nc = build()
    res = run_bass_kernel_spmd(nc, make_in_maps(inputs), core_ids=list(range(8)))
    full = np.zeros((4, SEQ, D), np.float32)
    for c in range(8):
        full[c // 2, (c % 2) * T:(c % 2 + 1) * T, :] = res.results[c]["out"]
    return full
```

```python
import math
from contextlib import ExitStack
import numpy as np
import concourse.bass as bass
import concourse.mybir as mybir
from concourse.bass_utils import run_bass_kernel_spmd

F32 = mybir.dt.float32
BF16 = mybir.dt.bfloat16
AF = mybir.ActivationFunctionType
ALU = mybir.AluOpType

D = 4096
T = 2048
SEQ = 4096
L = 2
INW = 23616
NEG = -30000.0
LN_EPS = 1e-5
RMS_EPS = 1e-6
ALPHA = (2.0 * L) ** 0.25
PAIRS = [[0, 1], [2, 3], [4, 5], [6, 7]]

SEGS = [
    ("qA", 0, 1024, "fm", "copy", 128 ** -0.5),
    ("kA", 1024, 1024, "fm", "copy", 1.0),
    ("vA", 2048, 1024, "tm", None, 1.0),
    ("gA", 3072, 1024, "fm", "silu", 1.0),
    ("cq", 4096, 1536, "fm", "copy", 1.0),
    ("ckv", 5632, 512, "fm", "copy", 1.0),
    ("kr", 6144, 64, "rope", None, 1.0),
    ("gB", 6208, 1024, "fm", "silu", 1.0),
    ("qC", 7232, 1024, "rope", None, 0.125),
    ("kC", 8256, 1024, "rope", None, 1.0),
    ("vC", 9280, 1024, "tm", None, 1.0),
    ("gC", 10304, 1024, "fm", "silu", 1.0),
    ("gm", 11328, 12288, "fm", "sigmoid", 1.0),
]


class Sem:
    __slots__ = ("h", "v")

    def __init__(self, h):
        self.h = h
        self.v = 0


class Buf:
    __slots__ = ("t", "wr", "rd")

    def __init__(self, t):
        self.t = t
        self.wr = None
        self.rd = []

    def ww(self):
        if self.rd:
            return list(self.rd)
        return [self.wr] if self.wr is not None else []

    def wrote(self, ev):
        self.wr = ev
        self.rd = []

    def rw(self):
        return [self.wr] if self.wr is not None else []

    def read(self, ev):
        self.rd.append(ev)


class State:
    pass


ENGS = ("sync", "act", "dve", "pool", "pe")
ENGMAP = {"sync": "sync", "act": "scalar", "dve": "vector", "pool": "gpsimd", "pe": "tensor"}


class Prog:
    def __init__(self, nc, S):
        self.nc = nc
        self.S = S
        self.q = {e: [] for e in ENGS}

    def add(self, eng, fn, waits=(), inc=None, amt=1):
        ws = []
        seen = self.S.seen[eng]
        for w in waits:
            if w is None:
                continue
            s, v = w
            if seen.get(s, 0) >= v:
                continue
            seen[s] = v
            ws.append((s, v))
        ev = None
        if inc is not None:
            inc.v += amt
            ev = (inc, inc.v)
        self.q[eng].append((fn, ws, inc, amt))
        return ev

    def op(self, eng, fn, waits=(), sig=True):
        return self.add(eng, fn, waits, self.S.esem[eng] if sig else None, 1)

    def dma(self, eng, out, in_, waits, sem):
        return self.add(eng, lambda e: e.dma_start(out=out, in_=in_), waits, sem, 16)

    def join(self):
        allsems = [s for s in self.S.allsems if s.v > 0]
        for e in ENGS:
            self.add(e, None, [(s, s.v) for s in allsems])

    def emit(self):
        self.join()
        with self.nc.Block() as block:
            for e in ENGS:
                items = self.q[e]

                def body(eng, items=items):
                    for fn, ws, inc, amt in items:
                        for s, v in ws:
                            eng.wait_ge(s.h, v)
                        if fn is None:
                            continue
                        ins = fn(eng)
                        if inc is not None:
                            ins.then_inc(inc.h, amt)

                getattr(block, ENGMAP[e])(body)


def newsem(S, es, name):
    s = Sem(es.enter_context(S.nc.semaphore(name)))
    S.allsems.append(s)
    return s


def sb(S, es, shape, dt, name):
    S.uid += 1
    return es.enter_context(S.nc.sbuf_tensor(f"{name}_{S.uid}", shape, dt))


def psum_bufs(S, es, n=8):
    out = []
    for i in range(n):
        S.uid += 1
        out.append(Buf(es.enter_context(S.nc.psum_tensor(f"ps_{S.uid}", [128, 512], F32))))
    return out


def inproj_groups():
    groups = []
    for name, col0, width, kind, func, scale in SEGS:
        for c in range(0, width, 512):
            w = min(512, width - c)
            if kind == "tm":
                jobs = [dict(kind="tm", dest=name, dcol0=c, n=w)]
            elif kind == "rope":
                jobs = [dict(kind="rope", wcol=j, n=min(128, w - j), scale=scale, dest=name, row0=c + j)
                        for j in range(0, w, 128)]
            else:
                jobs = [dict(kind="fm", wcol=j, n=128, func=func, scale=scale, dest=name,
                             row0=c + j) for j in range(0, w, 128)]
            groups.append(dict(col0=col0 + c, w=w, jobs=jobs))
    return groups


def phase_inproj(S, l, tb, xsrc):
    nc = S.nc
    P = Prog(nc, S)
    w_in = S.w_in[l]
    with ExitStack() as es:
        xT = sb(S, es, [128, 32, 1024], BF16, "xT")
        xs = [Buf(sb(S, es, [128, 2048], F32, "xs")) for _ in range(4)]
        wb = [Buf(sb(S, es, [128, 32, 512], BF16, "wb")) for _ in range(2)]
        stg = [Buf(sb(S, es, [128, 1024], BF16, "stg")) for _ in range(4)]
        rt = [sb(S, es, [128, 512], F32, "rt") for _ in range(2)]
        rtA = [Buf(sb(S, es, [128, 512], F32, "rtA")) for _ in range(3)]
        ps = psum_bufs(S, es)
        psi = [0]
        pending = []
        rai = [0]
        rope_dve = [None]

        def nextps():
            b = ps[psi[0] % 8]
            psi[0] += 1
            return b

        def flush():
            while pending:
                job, tc, ta, st, si_ = pending.pop(0)
                n = job["n"]
                sc = job["scale"]
                t0c = tb * 1024 + tc * 512
                pb2 = nextps()
                lp2 = P.op("pe", lambda e, o=pb2.t[0:n, :], r=ta.t[0:n, :], n=n: e.matmul(o, S.pswap[0:n, 0:n], r, start=True, stop=True),
                           waits=pb2.ww() + ta.rw())
                pb2.wrote(lp2)
                e1 = P.op("dve", lambda e, o=rt[0][0:n, :], a=ta.t[0:n, :], c=S.cosT[0:n, t0c:t0c + 512], sc=sc:
                          e.scalar_tensor_tensor(o, a, sc, c, ALU.mult, ALU.mult),
                          waits=ta.rw() + ([rope_dve[0]] if rope_dve[0] else []))
                e2 = P.op("dve", lambda e, o=rt[1][0:n, :], a=pb2.t[0:n, :], c=S.sinT[0:n, t0c:t0c + 512], sc=sc:
                          e.scalar_tensor_tensor(o, a, sc, c, ALU.mult, ALU.mult), waits=pb2.rw())
                pb2.read(e2)
                ta.read(lp2)
                ta.read(e1)
                e3 = P.op("dve", lambda e, o=st.t[0:n, tc * 512:(tc + 1) * 512], a=rt[0][0:n, :], c=rt[1][0:n, :]:
                          e.tensor_tensor(o, a, c, ALU.add), waits=[e1, e2] + (st.ww() if tc == 0 else []))
                rope_dve[0] = e3
                if tc == 1:
                    st.wrote(e3)
                    dest = S.dram[job["dest"]]
                    ev = P.dma("sync", dest[job["row0"]:job["row0"] + n, tb * 1024:(tb + 1) * 1024],
                               st.t[0:n, :], st.rw(), S.sem_stg[si_ % 4])
                    st.read(ev)

        xT_events = []
        k = 0
        for tt in range(8):
            r0 = tb * 1024 + tt * 128
            for hf in range(2):
                xb = xs[k % 4]
                k += 1
                ev = P.dma("sync" if k % 2 else "act", xb.t[:], xsrc[r0:r0 + 128, hf * 2048:(hf + 1) * 2048], xb.ww(),
                           S.sem_xs[(k - 1) % 4])
                xb.wrote(ev)
                for g in range(4):
                    pb = nextps()
                    for i in range(4):
                        lastev = P.op("pe", (lambda e, o=pb.t[:, i * 128:(i + 1) * 128],
                                             a=xb.t[:, (g * 4 + i) * 128:(g * 4 + i + 1) * 128]:
                                             e.transpose(o, a, S.ident[:])),
                                      waits=(pb.ww() + xb.rw()) if i == 0 else (), sig=(i == 3))
                    pb.wrote(lastev)
                    kc0 = hf * 16 + g * 4
                    o = xT[:, kc0:kc0 + 4, tt * 128:(tt + 1) * 128]
                    a = pb.t[:, :].rearrange("p (a b) -> p a b", a=4)
                    eng = "act" if (g % 2 == 0) else "dve"
                    if eng == "act":
                        ev2 = P.op("act", lambda e, o=o, a=a: e.copy(o, a), waits=pb.rw())
                    else:
                        ev2 = P.op("dve", lambda e, o=o, a=a: e.tensor_copy(o, a), waits=pb.rw())
                    pb.read(ev2)
                    xT_events.append(ev2)
                xb.read(lastev)
        xT_ready = []
        for en in ("act", "dve"):
            evs = [e for e in xT_events if e[0] is S.esem[en]]
            xT_ready.append(max(evs, key=lambda t: t[1]))

        groups = inproj_groups()

        def load_group(gi):
            g = groups[gi]
            b = wb[gi % 2]
            src = w_in[:, g["col0"]:g["col0"] + g["w"]].rearrange("(kc p) n -> p kc n", p=128)
            ww = b.ww()
            ev = None
            for i in range(4):
                ev = P.dma("pool", b.t[:, 8 * i:8 * i + 8, 0:g["w"]], src[:, 8 * i:8 * i + 8, :], ww,
                           S.sem_wb[gi % 2])
            b.wrote(ev)

        load_group(0)
        load_group(1)
        si = 0
        for gi, g in enumerate(groups):
            b = wb[gi % 2]
            lastpe = None
            for job in g["jobs"]:
                dest = S.dram[job["dest"]]
                if job["kind"] == "tm":
                    n = job["n"]
                    for tt in range(8):
                        pb = nextps()
                        for kc in range(32):
                            lastpe = P.op("pe", (lambda e, o=pb.t[:, 0:n], a=xT[:, kc, tt * 128:(tt + 1) * 128],
                                                 r=b.t[:, kc, 0:n], kc=kc:
                                                 e.matmul(o, a, r, start=(kc == 0), stop=(kc == 31))),
                                          waits=(pb.ww() + b.rw() + xT_ready) if kc == 0 else (),
                                          sig=(kc == 31))
                        pb.wrote(lastpe)
                        flush()
                        st = stg[si % 4]
                        si += 1
                        ev = P.op("dve", lambda e, o=st.t[:, 0:n], a=pb.t[:, 0:n]: e.tensor_copy(o, a),
                                  waits=pb.rw() + st.ww())
                        pb.read(ev)
                        st.wrote(ev)
                        r0 = tb * 1024 + tt * 128
                        ev = P.dma("sync", dest[r0:r0 + 128, job["dcol0"]:job["dcol0"] + n], st.t[:, 0:n],
                                   st.rw(), S.sem_stg[(si - 1) % 4])
                        st.read(ev)
                elif job["kind"] == "fm":
                    n = job["n"]
                    st = stg[si % 4]
                    si += 1
                    evs = []
                    for tc in range(2):
                        pb = nextps()
                        for kc in range(32):
                            lastpe = P.op("pe", (lambda e, o=pb.t[0:n, :], a=b.t[:, kc, job["wcol"]:job["wcol"] + n],
                                                 r=xT[:, kc, tc * 512:(tc + 1) * 512], kc=kc:
                                                 e.matmul(o, a, r, start=(kc == 0), stop=(kc == 31))),
                                          waits=(pb.ww() + b.rw() + xT_ready) if kc == 0 else (),
                                          sig=(kc == 31))
                        pb.wrote(lastpe)
                        flush()
                        o = st.t[0:n, tc * 512:(tc + 1) * 512]
                        a = pb.t[0:n, :]
                        f = job["func"]
                        if f == "copy":
                            if job["scale"] == 1.0:
                                fn = lambda e, o=o, a=a: e.copy(o, a)
                            else:
                                fn = lambda e, o=o, a=a, s=job["scale"]: e.mul(o, a, s)
                        elif f == "silu":
                            fn = lambda e, o=o, a=a: e.activation(out=o, in_=a, func=AF.Silu)
                        else:
                            bidx = job["row0"] // 128
                            fn = lambda e, o=o, a=a, bi=bidx: e.activation(
                                out=o, in_=a, func=AF.Sigmoid, bias=S.bm[l][:, bi:bi + 1], scale=1.0)
                        ev = P.op("act", fn, waits=pb.rw() + (st.ww() if tc == 0 else []))
                        pb.read(ev)
                        evs.append(ev)
                    st.wrote(evs[-1])
                    ev = P.dma("sync", dest[job["row0"]:job["row0"] + n, tb * 1024:(tb + 1) * 1024],
                               st.t[0:n, :], st.rw(), S.sem_stg[(si - 1) % 4])
                    st.read(ev)
                else:
                    n = job["n"]
                    st = stg[si % 4]
                    si += 1
                    for tc in range(2):
                        pb = nextps()
                        for kc in range(32):
                            lastpe = P.op("pe", (lambda e, o=pb.t[0:n, :], a=b.t[:, kc, job["wcol"]:job["wcol"] + n],
                                                 r=xT[:, kc, tc * 512:(tc + 1) * 512], kc=kc:
                                                 e.matmul(o, a, r, start=(kc == 0), stop=(kc == 31))),
                                          waits=(pb.ww() + b.rw() + xT_ready) if kc == 0 else (),
                                          sig=(kc == 31))
                        pb.wrote(lastpe)
                        flush()
                        ta = rtA[rai[0] % 3]
                        rai[0] += 1
                        ev = P.op("act", lambda e, o=ta.t[0:n, :], a=pb.t[0:n, :]: e.copy(o, a), waits=pb.rw() + ta.ww())
                        pb.read(ev)
                        ta.wrote(ev)
                        pending.append((job, tc, ta, st, si - 1))
            b.read(lastpe)
            if gi + 2 < len(groups):
                load_group(gi + 2)
        flush()
        P.emit()


def rstd_from_ps(P, S, psb, n_inv, eps, tmp, out, extra_waits=()):
    e1 = P.op("dve", lambda e: e.tensor_scalar(tmp, psb.t[:, :], n_inv, eps, ALU.mult, ALU.add),
              waits=psb.rw() + list(extra_waits))
    psb.read(e1)
    e2 = P.op("act", lambda e: e.sqrt(tmp, tmp), waits=[e1])
    e3 = P.op("dve", lambda e: e.reciprocal(out, tmp), waits=[e2])
    return e3


def phase_mla_proj(S, l):
    nc = S.nc
    P = Prog(nc, S)
    with ExitStack() as es:
        wuq = sb(S, es, [128, 12, 1536], BF16, "wuq")
        wuqs = sb(S, es, [128, 12, 512], BF16, "wuqs")
        wukv = sb(S, es, [128, 4, 2048], BF16, "wukv")
        cqc = [Buf(sb(S, es, [128, 12, 512], BF16, "cqc")) for _ in range(2)]
        ckc = [Buf(sb(S, es, [128, 4, 512], BF16, "ckc")) for _ in range(2)]
        sq = sb(S, es, [128, 16, 512], BF16, "sq")
        cqn = sb(S, es, [128, 12, 512], BF16, "cqn")
        ckn = sb(S, es, [128, 4, 512], BF16, "ckn")
        tmpq = sb(S, es, [128, 512], F32, "tmpq")
        tmpk = sb(S, es, [128, 512], F32, "tmpk")
        rq = sb(S, es, [128, 512], F32, "rq")
        rk = sb(S, es, [128, 512], F32, "rk")
        stg = [Buf(sb(S, es, [128, 512], BF16, "stg2")) for _ in range(4)]
        rt = [sb(S, es, [128, 512], F32, "rt2") for _ in range(2)]
        ps = psum_bufs(S, es)
        psi = [0]

        def nextps():
            b = ps[psi[0] % 8]
            psi[0] += 1
            return b

        wev = None
        srcq = S.w_uq[l].rearrange("(rc p) n -> p rc n", p=128)
        for i in range(3):
            wev = P.dma("pool", wuq[:, 4 * i:4 * i + 4, :], srcq[:, 4 * i:4 * i + 4, :], (), S.sem_w2)
        srck = S.w_ukv[l].rearrange("(rc p) n -> p rc n", p=128)
        wev = P.dma("pool", wukv[:, :, :], srck, (), S.sem_w2)
        sview = wuq[:, :, :].rearrange("p k (h c) -> p k h c", h=8)
        dview = wuqs[:, :, :].rearrange("p k (h c) -> p k h c", h=8)
        P.op("pool", lambda e: e.tensor_copy(dview[:, :, :, 0:32], sview[:, :, :, 160:192]), waits=[wev])
        wsw = P.op("pool", lambda e: e.tensor_copy(dview[:, :, :, 32:64], sview[:, :, :, 128:160]))
        wready = [wev, wsw]
        exchange_a(P, S)

        si = 0
        prev_norm_reads = []
        for tc in range(4):
            t0 = tc * 512
            cb = cqc[tc % 2]
            kb = ckc[tc % 2]
            ev = P.dma("sync", cb.t[:], S.dram["cq"][:, t0:t0 + 512].rearrange("(rc p) t -> p rc t", p=128),
                       cb.ww(), S.sem_ld2[tc % 2])
            ev = P.dma("sync", kb.t[:], S.dram["ckv"][:, t0:t0 + 512].rearrange("(rc p) t -> p rc t", p=128),
                       kb.ww(), S.sem_ld2[tc % 2])
            cb.wrote(ev)
            kb.wrote(ev)
            e_sq1 = P.op("act", lambda e, a=cb.t[:]: e.square(sq[:, 0:12, :], a), waits=cb.rw() + prev_norm_reads)
            e_sq2 = P.op("act", lambda e, a=kb.t[:]: e.square(sq[:, 12:16, :], a), waits=kb.rw())
            pa = nextps()
            for i in range(12):
                lp = P.op("pe", lambda e, o=pa.t[:, :], r=sq[:, i, :], i=i: e.matmul(o, S.onesb[:], r, start=(i == 0), stop=(i == 11)),
                          waits=(pa.ww() + [e_sq1, S.const_ready]) if i == 0 else (), sig=(i == 11))
            pa.wrote(lp)
            pk = nextps()
            for i in range(4):
                lp = P.op("pe", lambda e, o=pk.t[:, :], r=sq[:, 12 + i, :], i=i: e.matmul(o, S.onesb[:], r, start=(i == 0), stop=(i == 3)),
                          waits=(pk.ww() + [e_sq2]) if i == 0 else (), sig=(i == 3))
            pk.wrote(lp)
            sq_read = lp
            e_rq = rstd_from_ps(P, S, pa, 1.0 / 1536, RMS_EPS, tmpq[:], rq[:], extra_waits=prev_norm_reads)
            e_rk = rstd_from_ps(P, S, pk, 1.0 / 512, RMS_EPS, tmpk[:], rk[:])
            evn = []
            for i in range(12):
                eng = "dve"
                evn.append(P.op(eng, lambda e, o=cqn[:, i, :], a=cb.t[:, i, :], g=S.gq[l][:, i:i + 1]:
                                e.scalar_tensor_tensor(o, a, g, rq[:], ALU.mult, ALU.mult),
                                waits=[e_rq] + cb.rw() + prev_norm_reads))
            for i in range(4):
                eng = "dve"
                evn.append(P.op(eng, lambda e, o=ckn[:, i, :], a=kb.t[:, i, :], g=S.gkv[l][:, i:i + 1]:
                                e.scalar_tensor_tensor(o, a, g, rk[:], ALU.mult, ALU.mult),
                                waits=[e_rk] + kb.rw() + prev_norm_reads))
            nready = [evn[-1], evn[-2], evn[11], evn[10]]
            cb.read(evn[11]); cb.read(evn[10]); kb.read(evn[-1]); kb.read(evn[-2])
            lastpe = None
            for h in range(8):
                pb = nextps()
                for rc in range(12):
                    lastpe = P.op("pe", lambda e, o=pb.t[:, :], a=wuq[:, rc, h * 192:h * 192 + 128], r=cqn[:, rc, :], rc=rc:
                                  e.matmul(o, a, r, start=(rc == 0), stop=(rc == 11)),
                                  waits=(pb.ww() + nready + wready) if rc == 0 else (), sig=(rc == 11))
                pb.wrote(lastpe)
                st = stg[si % 4]; si += 1
                ev = P.op("act", lambda e, o=st.t[:, :], a=pb.t[:, :]: e.mul(o, a, 192 ** -0.5), waits=pb.rw() + st.ww())
                pb.read(ev); st.wrote(ev)
                ev = P.dma("sync", S.dram["qBn"][h * 128:(h + 1) * 128, t0:t0 + 512], st.t[:, :], st.rw(), S.sem_stg[(si - 1) % 4])
                st.read(ev)
                pbs = []
                for which in range(2):
                    pb = nextps()
                    for rc in range(12):
                        a = wuq[:, rc, h * 192 + 128:h * 192 + 192] if which == 0 else wuqs[:, rc, h * 64:(h + 1) * 64]
                        lastpe = P.op("pe", lambda e, o=pb.t[0:64, :], a=a, r=cqn[:, rc, :], rc=rc:
                                      e.matmul(o, a, r, start=(rc == 0), stop=(rc == 11)),
                                      waits=(pb.ww() + nready + wready) if rc == 0 else (), sig=(rc == 11))
                    pb.wrote(lastpe)
                    pbs.append(pb)
                sc = 192 ** -0.5
                tg = T0 = t0
                e1 = P.op("dve", lambda e, a=pbs[0].t[0:64, :], c=S.cosT[0:64, tg:tg + 512]:
                          e.scalar_tensor_tensor(rt[0][0:64, :], a, sc, c, ALU.mult, ALU.mult), waits=pbs[0].rw())
                pbs[0].read(e1)
                e2 = P.op("dve", lambda e, a=pbs[1].t[0:64, :], c=S.sinT[0:64, tg:tg + 512]:
                          e.scalar_tensor_tensor(rt[1][0:64, :], a, sc, c, ALU.mult, ALU.mult), waits=pbs[1].rw())
                pbs[1].read(e2)
                st = stg[si % 4]; si += 1
                ev = P.op("dve", lambda e, o=st.t[0:64, :]: e.tensor_tensor(o, rt[0][0:64, :], rt[1][0:64, :], ALU.add),
                          waits=[e1, e2] + st.ww())
                st.wrote(ev)
                ev = P.dma("sync", S.dram["qBr"][h * 64:(h + 1) * 64, t0:t0 + 512], st.t[0:64, :], st.rw(), S.sem_stg[(si - 1) % 4])
                st.read(ev)
            for h in range(8):
                pb = nextps()
                for rc in range(4):
                    lastpe = P.op("pe", lambda e, o=pb.t[:, :], a=wukv[:, rc, h * 256:h * 256 + 128], r=ckn[:, rc, :], rc=rc:
                                  e.matmul(o, a, r, start=(rc == 0), stop=(rc == 3)),
                                  waits=(pb.ww() + nready + wready) if rc == 0 else (), sig=(rc == 3))
                pb.wrote(lastpe)
                st = stg[si % 4]; si += 1
                ev = P.op("act", lambda e, o=st.t[:, :], a=pb.t[:, :]: e.copy(o, a), waits=pb.rw() + st.ww())
                pb.read(ev); st.wrote(ev)
                ev = P.dma("sync", S.dram["kBn"][h * 128:(h + 1) * 128, t0:t0 + 512], st.t[:, :], st.rw(), S.sem_stg[(si - 1) % 4])
                st.read(ev)
            wv = wukv[:, :, :].rearrange("p k (h c) -> p k h c", h=8)
            for tt in range(4):
                for half in range(2):
                    pb = nextps()
                    for rc in range(4):
                        lastpe = P.op("pe", lambda e, o=pb.t[:, :].rearrange("p (h c) -> p h c", h=4),
                                      a=ckn[:, rc, tt * 128:(tt + 1) * 128], r=wv[:, rc, half * 4:(half + 1) * 4, 128:256], rc=rc:
                                      e.matmul(o, a, r, start=(rc == 0), stop=(rc == 3)),
                                      waits=(pb.ww() + nready + wready) if rc == 0 else (), sig=(rc == 3))
                    pb.wrote(lastpe)
                    st = stg[si % 4]; si += 1
                    ev = P.op("dve", lambda e, o=st.t[:, :], a=pb.t[:, :]: e.tensor_copy(o, a), waits=pb.rw() + st.ww())
                    pb.read(ev); st.wrote(ev)
                    r0 = t0 + tt * 128
                    ev = P.dma("sync", S.dram["vB"][r0:r0 + 128, half * 512:(half + 1) * 512], st.t[:, :], st.rw(), S.sem_stg[(si - 1) % 4])
                    st.read(ev)
            prev_norm_reads = [lastpe, sq_read]
        P.emit()


def emit_collectives(P, S, pairs, waits):
    first = True
    for a, o in pairs:
        P.add("pool", lambda e, a=a, o=o: e.collective_compute(
            "AllGather", ALU.bypass, replica_groups=PAIRS, ins=[a.opt()], outs=[o.opt()]),
            waits=waits if first else (), inc=S.sem_cc, amt=1)
        first = False


def exchange_a(P, S):
    d = S.dram
    e1 = P.dma("sync", d["kAh"][:, 0:384], d["kA"][:, 0:384], (), S.sem_ex)
    e1 = P.dma("sync", d["kAh"][:, 384:768], d["kA"][:, T - 384:T], (), S.sem_ex)
    e1 = P.dma("sync", d["vAh"][0:384, :], d["vA"][0:384, :], (), S.sem_ex)
    e1 = P.dma("sync", d["vAh"][384:768, :], d["vA"][T - 384:T, :], (), S.sem_ex)
    pairs = [(d["kAh"], d["kAh_g"]), (d["vAh"], d["vAh_g"]), (d["kr"], d["kr_g"])]
    for i in range(2):
        pairs.append((d["kC"][i * 512:(i + 1) * 512, :], d[f"kC_g{i}"]))
        pairs.append((d["vC"][i * 1024:(i + 1) * 1024, :], d[f"vC_g{i}"]))
    emit_collectives(P, S, pairs, [e1])


def exchange_b(P, S):
    d = S.dram
    pairs = []
    for i in range(2):
        pairs.append((d["kBn"][i * 512:(i + 1) * 512, :], d[f"kBn_g{i}"]))
        pairs.append((d["vB"][i * 1024:(i + 1) * 1024, :], d[f"vB_g{i}"]))
    emit_collectives(P, S, pairs, [])


def na_type(j):
    return 0 if j == 0 else 1 if j == 1 else 3 if j == 14 else 4 if j == 15 else 2


def phase_na(S, l):
    nc = S.nc
    P = Prog(nc, S)
    d = S.dram
    with ExitStack() as es:
        KTs = [sb(S, es, [128, 22 * 128], BF16, "naK") for _ in range(2)]
        Vs = [sb(S, es, [128, 22, 128], BF16, "naV") for _ in range(2)]
        QTs = [sb(S, es, [128, T], BF16, "naQ") for _ in range(2)]
        GTs = [sb(S, es, [128, T], BF16, "naG") for _ in range(2)]
        BT = sb(S, es, [128, 8, 35, 128], BF16, "naB")
        PT = [Buf(sb(S, es, [128, 7 * 128], BF16, "naP")) for _ in range(2)]
        rec = sb(S, es, [128, 512], F32, "narec")
        tmp = sb(S, es, [128, 512], F32, "natmp")
        stg = [Buf(sb(S, es, [128, 512], BF16, "nastg")) for _ in range(2)]
        ps = psum_bufs(S, es)
        psS = [ps[0], ps[1], ps[2], ps[3]]
        psO = [ps[4], ps[5]]
        psR = [ps[6], ps[7]]
        evbs = []
        for h in range(8):
            evbs.append(P.dma("pool", BT[:, h, :, :], S.nabias[l][h], (), S.sem_nab[h]))
        exchange_b(P, S)
        head_done = {}
        loaded = {}

        def load(h):
            KT, V, QT, GT = KTs[h % 2], Vs[h % 2], QTs[h % 2], GTs[h % 2]
            ww = head_done.get(h - 2, [])
            sem = S.sem_ln[h % 2]
            r = slice(h * 128, (h + 1) * 128)
            ev = P.dma("sync", KT[:, 0:384], d["kAh_g"][h * 128:(h + 1) * 128, 384:768], ww, sem)
            ev = P.dma("sync", KT[:, 384:384 + T], d["kA"][r, :], ww, sem)
            ev = P.dma("sync", KT[:, 384 + T:768 + T], d["kAh_g"][1024 + h * 128:1024 + (h + 1) * 128, 0:384], ww, sem)
            ev = P.dma("sync", V[:, 0:3, :], d["vAh_g"][384:768, r].rearrange("(i p) c -> p i c", p=128), ww, sem)
            vav = d["vA"][:, r].rearrange("(i p) c -> p i c", p=128)
            ev = P.dma("sync", V[:, 3:11, :], vav[:, 0:8, :], ww, sem)
            ev = P.dma("sync", V[:, 11:19, :], vav[:, 8:16, :], ww, sem)
            ev = P.dma("sync", V[:, 19:22, :], d["vAh_g"][768:768 + 384, r].rearrange("(i p) c -> p i c", p=128), ww, sem)
            ev = P.dma("sync", QT[:, :], d["qA"][r, :], ww, sem)
            ev = P.dma("sync", GT[:, :], d["gA"][r, :], ww, sem)
            loaded[h] = ev

        load(0)
        load(1)
        last_evac = None
        pti = 0
        sti = 0
        gi = 0
        for h in range(8):
            KT, V, QT, GT = KTs[h % 2], Vs[h % 2], QTs[h % 2], GTs[h % 2]
            r = slice(h * 128, (h + 1) * 128)
            ready = [loaded[h], evbs[h], S.const_ready]
            lp = None
            for qg in range(4):
                pO = psO[gi % 2]
                pR = psR[gi % 2]
                gi += 1
                for jj in range(4):
                    j = qg * 4 + jj
                    ty = na_type(j)
                    slots = {0: [1, 2, 3, 4, 5, 6], 1: [1, 2, 3, 4, 5], 2: [1, 2, 3, 4, 5],
                             3: [1, 2, 3, 4, 5], 4: [0, 1, 2, 3, 4, 5]}[ty]
                    ns = len(slots)
                    pt = PT[pti % 2]
                    p1 = psS[(pti % 2) * 2]
                    p2 = psS[(pti % 2) * 2 + 1]
                    pti += 1
                    for idx, s in enumerate(slots):
                        pb = p1 if idx < 4 else p2
                        c0 = (idx % 4) * 128
                        P.op("pe", lambda e, o=pb.t[:, c0:c0 + 128], a=KT[:, (j + s) * 128:(j + s + 1) * 128],
                             q=QT[:, j * 128:(j + 1) * 128]: e.matmul(o, a, q, start=True, stop=False),
                             waits=(pb.ww() + ready) if idx in (0, 4) else (), sig=False)
                        lp = P.op("pe", lambda e, o=pb.t[:, c0:c0 + 128], b=BT[:, h, ty * 7 + s, :]:
                                  e.matmul(o, S.identb[:], b, start=False, stop=True), sig=(idx in (3, ns - 1)))
                        if idx == 3:
                            p1.wrote(lp)
                        if idx == ns - 1 and idx != 3:
                            p2.wrote(lp)
                    e1 = P.op("act", lambda e, o=pt.t[:, 0:512], a=p1.t[:, :]: e.activation(out=o, in_=a, func=AF.Exp),
                              waits=p1.rw() + pt.ww())
                    p1.read(e1)
                    n2 = (ns - 4) * 128
                    e2 = P.op("act", lambda e, o=pt.t[:, 512:512 + n2], a=p2.t[:, 0:n2]: e.activation(out=o, in_=a, func=AF.Exp),
                              waits=p2.rw())
                    p2.read(e2)
                    pt.wrote(e2)
                    for idx, s in enumerate(slots):
                        P.op("pe", lambda e, o=pO.t[:, jj * 128:(jj + 1) * 128], a=V[:, j + s, :], p=pt.t[:, idx * 128:(idx + 1) * 128], idx=idx, ns=ns:
                             e.matmul(o, a, p, start=(idx == 0), stop=(idx == ns - 1)),
                             waits=(pt.rw() + pO.ww() + pR.ww()) if idx == 0 else (), sig=False)
                        lp = P.op("pe", lambda e, o=pR.t[:, jj * 128:(jj + 1) * 128], p=pt.t[:, idx * 128:(idx + 1) * 128], idx=idx, ns=ns:
                                  e.matmul(o, S.onesb[:], p, start=(idx == 0), stop=(idx == ns - 1)), sig=(idx == ns - 1))
                    pt.read(lp)
                pO.wrote(lp)
                pR.wrote(lp)
                ea = P.op("dve", lambda e, a=pR.t[:, :]: e.reciprocal(rec[:], a), waits=pR.rw() + ([last_evac] if last_evac else []))
                pR.read(ea)
                eb = P.op("dve", lambda e, a=pO.t[:, :]: e.tensor_tensor(tmp[:], a, rec[:], ALU.mult), waits=pO.rw() + [ea])
                pO.read(eb)
                st = stg[sti % 2]; sti += 1
                ec = P.op("dve", lambda e, o=st.t[:, :], g=GT[:, qg * 512:(qg + 1) * 512]: e.tensor_tensor(o, tmp[:], g, ALU.mult),
                          waits=[eb] + st.ww())
                st.wrote(ec)
                last_evac = ec
                ev = P.dma("sync", d["yA"][r, qg * 512:(qg + 1) * 512], st.t[:, :], st.rw(), S.sem_stg[(sti - 1) % 2])
                st.read(ev)
            head_done[h] = [lp, last_evac]
            if h + 2 < 8:
                load(h + 2)
        P.emit()


def phase_dense(S, l, kind):
    nc = S.nc
    P = Prog(nc, S)
    d = S.dram
    mla = (kind == "mla")
    nm = 1 if mla else 2
    with ExitStack() as es:
        KTs = [sb(S, es, [128, SEQ], BF16, "dK") for _ in range(2)]
        Vs = [sb(S, es, [128, 32, 128], BF16, "dV") for _ in range(2)]
        GTs = [sb(S, es, [128, T], BF16, "dG") for _ in range(2)]
        if mla:
            KR = sb(S, es, [128, SEQ], BF16, "dKr")
            QTs = [sb(S, es, [128, T], BF16, "dQ") for _ in range(2)]
            QRs = [sb(S, es, [128, T], BF16, "dQr") for _ in range(2)]
        else:
            QAs = [sb(S, es, [128, T], BF16, "dQa") for _ in range(2)]
            QBs = [sb(S, es, [128, T], BF16, "dQb") for _ in range(2)]
        NPT = 12
        PT = [Buf(sb(S, es, [128, 512], BF16, "dP")) for _ in range(NPT)]
        rsum = [sb(S, es, [128, 512], F32, "drsum") for _ in range(nm)]
        oraw = [sb(S, es, [128, 512], F32, "doraw") for _ in range(nm)]
        rec = sb(S, es, [128, 512], F32, "drec")
        yy = sb(S, es, [128, 512], F32, "dyy")
        sqy = sb(S, es, [128, 512], F32, "dsq")
        rs = sb(S, es, [128, 512], F32, "drs")
        stg = [Buf(sb(S, es, [128, 512], BF16, "dstg")) for _ in range(2)]
        ps = psum_bufs(S, es)
        psO = ps[4:6]
        psR = ps[6:8]
        psS = (ps[0:4] + [ps[5], ps[7]]) if mla else ps[0:4]
        NS = len(psS)
        zev = None
        kr_ev = None
        if mla:
            zev = P.op("dve", lambda e: e.memset(KR[:, :], 0.0))
            for i in range(2):
                zev = P.op("dve", lambda e, q=QRs[i]: e.memset(q[:, :], 0.0))
            kr_ev = P.dma("sync", KR[0:64, 0:T], d["kr_g"][0:64, :], [zev], S.sem_ld3)
            kr_ev = P.dma("sync", KR[0:64, T:SEQ], d["kr_g"][64:128, :], [zev], S.sem_ld3)
        else:
            for i in range(2):
                P.op("dve", lambda e, q=QAs[i]: e.memset(q[:, :], 0.0))
                zev = P.op("dve", lambda e, q=QBs[i]: e.memset(q[:, :], 0.0))
        kgs = [d["kBn_g0"], d["kBn_g1"]] if mla else [d["kC_g0"], d["kC_g1"]]
        vgs = [d["vB_g0"], d["vB_g1"]] if mla else [d["vC_g0"], d["vC_g1"]]
        qsrc = d["qBn"] if mla else d["qC"]
        gsrc = d["gB"] if mla else d["gC"]
        ydst = d["yB"] if mla else d["yC"]
        head_done = {}
        loaded = {}

        def load(h):
            b = h % 2
            r = slice(h * 128, (h + 1) * 128)
            ww = head_done.get(h - 2, []) + [zev]
            sem = S.sem_ln[b]
            kg = kgs[h // 4]
            hr = (h % 4) * 128
            ev = P.dma("sync", KTs[b][:, 0:T], kg[hr:hr + 128, :], ww, sem)
            ev = P.dma("sync", KTs[b][:, T:SEQ], kg[512 + hr:512 + hr + 128, :], ww, sem)
            for rk_ in range(2):
                for th in range(2):
                    vi = rk_ * 2 + th
                    ev = P.dma("sync", Vs[b][:, vi * 8:(vi + 1) * 8, :],
                               vgs[th][rk_ * 1024:(rk_ + 1) * 1024, r].rearrange("(i p) c -> p i c", p=128), ww, sem)
            ev = P.dma("sync", GTs[b][:, :], gsrc[r, :], ww, sem)
            if mla:
                ev = P.dma("sync", QTs[b][:, :], qsrc[r, :], ww, sem)
                ev = P.dma("sync", QRs[b][0:64, :], d["qBr"][h * 64:(h + 1) * 64, :], ww, sem)
            else:
                ev = P.dma("sync", QAs[b][0:64, :], qsrc[h * 128:h * 128 + 64, :], ww, sem)
                ev = P.dma("sync", QBs[b][64:128, :], qsrc[h * 128 + 64:h * 128 + 128, :], ww, sem)
            loaded[h] = ev

        load(0)
        load(1)
        epi_done = [None]
        deferred = []
        si = 0
        pi = 0
        sti = 0
        for h in range(8):
            b = h % 2
            KT, V, GT = KTs[b], Vs[b], GTs[b]
            r = slice(h * 128, (h + 1) * 128)
            ready = [loaded[h], S.const_ready] + ([kr_ev] if mla else [])
            lp = None
            for qc in range(4):
                q0 = qc * 512
                pend = []
                units = [(kt, m) for kt in range(32) for m in range(nm)]

                def issue_s(u):
                    nonlocal si, pi
                    kt, m = u
                    pb = psS[si % NS]; si += 1
                    if mla:
                        P.op("pe", lambda e, o=pb.t[:, :], a=KT[:, kt * 128:(kt + 1) * 128], q=QTs[b][:, q0:q0 + 512]:
                             e.matmul(o, a, q, start=True, stop=False), waits=pb.ww() + ready, sig=False)
                        lps = P.op("pe", lambda e, o=pb.t[:, :], a=KR[:, kt * 128:(kt + 1) * 128], q=QRs[b][:, q0:q0 + 512]:
                                   e.matmul(o, a, q, start=False, stop=True))
                    else:
                        lps = P.op("pe", lambda e, o=pb.t[:, :], a=KT[:, kt * 128:(kt + 1) * 128],
                                   q=(QAs[b] if m == 0 else QBs[b])[:, q0:q0 + 512]: e.matmul(o, a, q, start=True, stop=True),
                                   waits=pb.ww() + ready)
                    pb.wrote(lps)
                    pt = PT[pi % NPT]; pi += 1
                    ee = P.op("act", lambda e, o=pt.t[:, :], a=pb.t[:, :]: e.activation(out=o, in_=a, func=AF.Exp),
                              waits=pb.rw() + pt.ww())
                    pb.read(ee)
                    pt.wrote(ee)
                    pend.append(pt)

                LA = NS - 1
                for u in units[:LA]:
                    issue_s(u)
                G = 4 * nm
                for gidx, g0 in enumerate(range(0, len(units), G)):
                    if gidx in ((2, 4) if mla else (1, 3, 5)) and deferred:
                        deferred.pop(0)()
                    grp = list(range(g0, g0 + G))
                    for ui in grp:
                        kt, m = units[ui]
                        pt = pend[ui]
                        pO = psO[m]
                        first = (kt == 0)
                        last = (kt == 31)
                        P.op("pe", lambda e, o=pO.t[:, :], a=V[:, kt, :], p=pt.t[:, :], first=first, last=last:
                             e.matmul(o, a, p, start=first, stop=last),
                             waits=pt.rw() + ((pO.ww() + psR[m].ww()) if first else []), sig=False)
                        if ui + LA < len(units):
                            issue_s(units[ui + LA])
                    for m in range(nm):
                        for ui in grp:
                            kt, mm = units[ui]
                            if mm != m:
                                continue
                            pt = pend[ui]
                            pR = psR[m]
                            j = kt % 4
                            lp = P.op("pe", lambda e, o=pR.t[32 * j:32 * j + 32, :], p=pt.t[:, :], kt=kt, j=j:
                                      e.matmul(o, S.onesb[:, 0:32], p, start=(kt < 4), stop=(kt >= 28), tile_position=(0, 32 * j)),
                                      sig=(j == 3))
                        for ui in grp:
                            if units[ui][1] == m:
                                pend[ui].read(lp)
                        if units[grp[-1]][0] == 31:
                            psO[m].wrote(lp)
                            psR[m].wrote(lp)
                pw = [epi_done[0]] if epi_done[0] else []
                cps = []
                for m in range(nm):
                    c1 = P.op("dve", lambda e, o=rsum[m][:], a=psR[m].t[:, :]: e.tensor_copy(o, a),
                              waits=psR[m].rw() + pw + ([lastpe_n[1]] if lastpe_n[1] else []))
                    psR[m].read(c1)
                    c2 = P.op("dve", lambda e, o=oraw[m][:], a=psO[m].t[:, :]: e.tensor_copy(o, a), waits=psO[m].rw() + pw)
                    psO[m].read(c2)
                    cps.append((c1, c2))
                state = {}

                def stage_a(m, cps=cps, state=state):
                    nonlocal si
                    outs = state.setdefault("outs", [])
                    c1, c2 = cps[m]
                    pn2 = psS[si % NS]; si += 1
                    l2 = P.op("pe", lambda e, o=pn2.t[:, :], rr=rsum[m][:]: e.matmul(o, S.sel4[:], rr, start=True, stop=True),
                              waits=pn2.ww() + [c1])
                    pn2.wrote(l2)
                    lastpe_n[1] = l2
                    c3 = P.op("dve", lambda e, a=pn2.t[:, :]: e.reciprocal(rec[:], a), waits=pn2.rw())
                    pn2.read(c3)
                    c4 = P.op("dve", lambda e, o=oraw[m][:]: e.tensor_tensor(o, o, rec[:], ALU.mult), waits=[c3, c2])
                    outs.append(c4)
                    if (not mla) and m == 1:
                        e3 = P.op("dve", lambda e: e.scalar_tensor_tensor(yy[:], oraw[1][:], S.neglam[l][:, 0:1], oraw[0][:], ALU.mult, ALU.add),
                                  waits=[outs[1], S.lam_ready])
                        e4 = P.op("pool", lambda e: e.tensor_tensor(sqy[:], yy[:], yy[:], ALU.mult),
                                  waits=[e3] + ([lastpe_n[0]] if lastpe_n[0] else []))
                        state["e4"] = e4

                def stage_b(state=state, GT=GT, q0=q0, r=r, h=h, qc=qc, lp=lp):
                    nonlocal si, sti
                    st = stg[sti % 2]; sti += 1
                    if mla:
                        ec = P.op("dve", lambda e, o=st.t[:, :], g=GT[:, q0:q0 + 512]: e.tensor_tensor(o, oraw[0][:], g, ALU.mult),
                                  waits=[state["outs"][0]] + st.ww())
                    else:
                        pn = psS[si % NS]; si += 1
                        lpn = P.op("pe", lambda e, o=pn.t[:, :]: e.matmul(o, S.onesf[:], sqy[:], start=True, stop=True),
                                   waits=pn.ww() + [state["e4"]])
                        pn.wrote(lpn)
                        lastpe_n[0] = lpn
                        e5 = rstd_from_ps(P, S, pn, 1.0 / 128, RMS_EPS, rs[:], rs[:])
                        e6 = P.op("dve", lambda e: e.tensor_tensor(yy[:], yy[:], rs[:], ALU.mult), waits=[e5])
                        ec = P.op("dve", lambda e, o=st.t[:, :], g=GT[:, q0:q0 + 512]:
                                  e.scalar_tensor_tensor(o, yy[:], S.subc[l][:, 0:1], g, ALU.mult, ALU.mult),
                                  waits=[e6] + st.ww())
                    st.wrote(ec)
                    epi_done[0] = ec
                    ev2 = P.dma("sync", ydst[r, q0:q0 + 512], st.t[:, :], st.rw(), S.sem_stg[(sti - 1) % 2])
                    st.read(ev2)
                    if qc == 3:
                        head_done[h] = [lp, ec]
                        if h + 2 < 8:
                            load(h + 2)

                for m in range(nm):
                    deferred.append(lambda m=m, f=stage_a: f(m))
                deferred.append(stage_b)
        while deferred:
            deferred.pop(0)()
        P.emit()


lastpe_n = [None, None]


def phase_outproj(S, l, tb, xsrc):
    nc = S.nc
    d = S.dram
    tb0 = tb * 1024
    with ExitStack() as es0:
        mT = sb(S, es0, [128, 32, 1024], BF16, "mT")
        P = Prog(nc, S)
        with ExitStack() as es:
            yT = sb(S, es, [128, 3, 8, 1024], BF16, "yT")
            wo = [Buf(sb(S, es, [128, 3, 8, 512], BF16, "wo")) for _ in range(2)]
            gt = [Buf(sb(S, es, [128, 3, 512], BF16, "gt")) for _ in range(2)]
            tt_ = [[sb(S, es, [128, 512], F32, "t5") for _ in range(3)] for _ in range(2)]
            ps = psum_bufs(S, es)
            psi = 0
            yev = None
            for j, nmy in enumerate(("yA", "yB", "yC")):
                yev = P.dma("sync", yT[:, j, :, :], d[nmy][:, tb0:tb0 + 1024].rearrange("(wc p) t -> p wc t", p=128), (), S.sem_ld3)
            wsrc = [S.w_o[j][l] for j in range(3)]

            def load_wo(dg):
                b = wo[dg % 2]
                ww = b.ww()
                ev = None
                for j in range(3):
                    ev = P.dma("pool", b.t[:, j, :, :], wsrc[j][:, dg * 512:(dg + 1) * 512].rearrange("(wc p) n -> p wc n", p=128),
                               ww, S.sem_wb[dg % 2])
                b.wrote(ev)

            gview = d["gm"].rearrange("(j dc p) t -> dc p j t", j=3, p=128)
            load_wo(0)
            load_wo(1)
            gi = 0
            prev_tt = [None, None]
            mT_ev = []
            for dg in range(8):
                b = wo[dg % 2]
                lp = None
                for ds in range(4):
                    dc = dg * 4 + ds
                    for tc in range(2):
                        g = gt[gi % 2]
                        tset = tt_[gi % 2]
                        pv = prev_tt[gi % 2]
                        gi += 1
                        ev = P.dma("sync", g.t[:, :, :], gview[dc][:, :, tb0 + tc * 512:tb0 + (tc + 1) * 512], g.ww(), S.sem_ld2[(gi - 1) % 2])
                        g.wrote(ev)
                        pbs = []
                        for j in range(3):
                            pb = ps[psi % 8]; psi += 1
                            for wc in range(8):
                                lp = P.op("pe", lambda e, o=pb.t[:, :], a=b.t[:, j, wc, ds * 128:(ds + 1) * 128],
                                          r=yT[:, j, wc, tc * 512:(tc + 1) * 512], wc=wc:
                                          e.matmul(o, a, r, start=(wc == 0), stop=(wc == 7)),
                                          waits=(pb.ww() + b.rw() + [yev]) if wc == 0 else (), sig=(wc == 7))
                            pb.wrote(lp)
                            pbs.append(pb)
                        evs = []
                        for j in range(3):
                            e1 = P.op("dve", lambda e, o=tset[j][:], a=pbs[j].t[:, :], gg=g.t[:, j, :]: e.tensor_tensor(o, a, gg, ALU.mult),
                                      waits=pbs[j].rw() + g.rw() + ([pv] if pv else []))
                            pbs[j].read(e1)
                            evs.append(e1)
                        g.read(evs[-1])
                        e2 = P.op("pool", lambda e, a=tset[0][:], c=tset[1][:]: e.tensor_tensor(a, a, c, ALU.add), waits=evs)
                        e3 = P.op("pool", lambda e, o=mT[:, dc, tc * 512:(tc + 1) * 512], a=tset[0][:], c=tset[2][:]:
                                  e.tensor_tensor(o, a, c, ALU.add), waits=[e2])
                        prev_tt[(gi - 1) % 2] = e3
                        mT_ev = [e3]
                b.read(lp)
                if dg + 2 < 8:
                    load_wo(dg + 2)
            P.emit()
        P = Prog(nc, S)
        with ExitStack() as es:
            wob = [Buf(sb(S, es, [128, 32, 512], BF16, "wout")) for _ in range(2)]
            stf = [Buf(sb(S, es, [128, 512], F32, "stf")) for _ in range(4)]
            xcs = [Buf(sb(S, es, [128, 512], F32, "xc")) for _ in range(4)]
            ps = psum_bufs(S, es)
            psi = 0
            sti = 0
            wsrc = S.w_out[l]

            def load_w(eg):
                b = wob[eg % 2]
                ww = b.ww()
                src = wsrc[:, eg * 512:(eg + 1) * 512].rearrange("(dc p) n -> p dc n", p=128)
                ev = None
                for i in range(4):
                    ev = P.dma("pool", b.t[:, 8 * i:8 * i + 8, :], src[:, 8 * i:8 * i + 8, :], ww, S.sem_wb[eg % 2])
                b.wrote(ev)

            load_w(0)
            load_w(1)
            zrs = P.op("dve", lambda e: e.memset(S.rowsum[:, tb * 8:(tb + 1) * 8, :], 0.0))
            for eg in range(8):
                b = wob[eg % 2]
                lp = None
                for tt in range(8):
                    pb = ps[psi % 8]; psi += 1
                    for dc in range(32):
                        lp = P.op("pe", lambda e, o=pb.t[:, :], a=mT[:, dc, tt * 128:(tt + 1) * 128], r=b.t[:, dc, :], dc=dc:
                                  e.matmul(o, a, r, start=(dc == 0), stop=(dc == 31)),
                                  waits=(pb.ww() + b.rw()) if dc == 0 else (), sig=(dc == 31))
                    pb.wrote(lp)
                    st = stf[sti % 4]
                    xc = xcs[sti % 4]
                    sti += 1
                    r0 = tb0 + tt * 128
                    evx = P.dma("act", xc.t[:, :], xsrc[r0:r0 + 128, eg * 512:(eg + 1) * 512], xc.ww(), S.sem_ln[(sti - 1) % 4])
                    xc.wrote(evx)
                    ev = P.op("dve", lambda e, o=st.t[:, :], a=pb.t[:, :], x=xc.t[:, :], acc=S.rowsum[:, tb * 8 + tt, eg:eg + 1]:
                              e.scalar_tensor_tensor(o, x, ALPHA, a, ALU.mult, ALU.add, accum_out=acc),
                              waits=pb.rw() + st.ww() + xc.rw() + [zrs])
                    pb.read(ev); st.wrote(ev); xc.read(ev)
                    ev = P.dma("sync", d["yout"][r0:r0 + 128, eg * 512:(eg + 1) * 512], st.t[:, :], st.rw(), S.sem_stg[(sti - 1) % 4])
                    st.read(ev)
                b.read(lp)
                if eg + 2 < 8:
                    load_w(eg + 2)
            P.emit()


def phase_ln(S, l, xsrc, xdst):
    nc = S.nc
    P = Prog(nc, S)
    d = S.dram
    NT = T // 128
    with ExitStack() as es:
        lng = sb(S, es, [128, D], F32, "lng")
        lnb = sb(S, es, [128, D], F32, "lnb")
        yt = [Buf(sb(S, es, [128, D], F32, "lny")) for _ in range(4)]
        xt = [Buf(sb(S, es, [128, D], F32, "lnx")) for _ in range(4)]
        sts = [sb(S, es, [128, 8], F32, "lnst") for _ in range(4)]
        cev = P.dma("sync", lng[:], S.lngD[l], (), S.sem_c)
        cev = P.dma("sync", lnb[:], S.lnbD[l], (), S.sem_c)
        evR, evQ, evT, evN = {}, {}, {}, {}

        def load(i):
            yb = yt[i % 4]
            r0 = i * 128
            ev = P.dma("sync", yb.t[:], d["yout"][r0:r0 + 128, :], yb.ww(), S.sem_ln[i % 4])
            yb.wrote(ev)

        def st_R(i):
            yb, st1 = yt[i % 4], sts[i % 4]
            e0 = P.op("dve", lambda e, st1=st1: e.memset(st1[:, :], 0.0), waits=[evN[i - 4]] if (i - 4) in evN else [])
            e1 = P.op("dve", lambda e, rsrc=S.rowsum[:, i, :], st1=st1: e.reduce_sum(st1[:, 0:1], rsrc, axis=mybir.AxisListType.X),
                      waits=[e0])
            evR[i] = P.op("dve", lambda e, st1=st1: e.tensor_scalar(st1[:, 1:2], st1[:, 0:1], -1.0 / D, None, ALU.mult), waits=[e1])

        def st_Q(i):
            yb, xb, st1 = yt[i % 4], xt[i % 4], sts[i % 4]
            evQ[i] = P.op("act", lambda e, y=yb.t[:], x=xb.t[:], st1=st1:
                          e.activation(out=x, in_=y, func=AF.Square, bias=st1[:, 1:2], scale=1.0, accum_out=st1[:, 2:3]),
                          waits=[evR[i]] + xb.ww() + yb.rw())

        def st_T(i):
            st1 = sts[i % 4]
            e7 = P.op("dve", lambda e, st1=st1: e.tensor_scalar(st1[:, 3:4], st1[:, 2:3], 1.0 / D, LN_EPS, ALU.mult, ALU.add), waits=[evQ[i]])
            e8 = P.op("act", lambda e, st1=st1: e.sqrt(st1[:, 4:5], st1[:, 3:4]), waits=[e7])
            e9 = P.op("dve", lambda e, st1=st1: e.reciprocal(st1[:, 5:6], st1[:, 4:5]), waits=[e8])
            evT[i] = P.op("dve", lambda e, st1=st1: e.tensor_tensor(st1[:, 6:7], st1[:, 1:2], st1[:, 5:6], ALU.mult), waits=[e9])

        def st_N(i):
            yb, st1 = yt[i % 4], sts[i % 4]
            evN[i] = P.op("act", lambda e, y=yb.t[:], st1=st1:
                          e.activation(out=y, in_=y, func=AF.Identity, bias=st1[:, 6:7], scale=st1[:, 5:6]), waits=[evT[i]])

        evM = {}

        def st_M(i):
            yb, xb = yt[i % 4], xt[i % 4]
            e10 = P.op("dve", lambda e, y=yb.t[:], x=xb.t[:]: e.tensor_tensor(x, y, lng[:], ALU.mult), waits=[evN[i], cev])
            yb.read(e10)
            evM[i] = e10

        def st_H(i):
            yb, xb = yt[i % 4], xt[i % 4]
            r0 = i * 128
            e10 = evM[i]
            e11 = P.op("dve", lambda e, x=xb.t[:]: e.tensor_tensor(x, x, lnb[:], ALU.add), waits=[e10])
            xb.wrote(e11)
            ev = P.dma("pool", xdst[r0:r0 + 128, :], xb.t[:], [e11], S.sem_stg[i % 4])
            xb.read(ev)

        for i in range(4):
            load(i)
        st_R(0)
        st_Q(0)
        st_T(0)
        st_R(1)
        for i in range(NT):
            st_N(i)
            st_M(i)
            if i + 1 < NT:
                st_Q(i + 1)
                st_T(i + 1)
            st_H(i)
            if i + 2 < NT:
                st_R(i + 2)
            if i + 4 < NT:
                load(i + 4)
        P.emit()


def build(stop_after=None, debug_out=(), nlayers=L):
    nc = bass.Bass("TRN2", target_bir_lowering=False)
    S = State()
    S.nc = nc
    S.uid = 0
    S.allsems = []
    S.seen = {e: {} for e in ENGS}
    lastpe_n[0] = None
    lastpe_n[1] = None

    def din(name, shape, dt=F32):
        return nc.dram_tensor(name, shape, dt, kind="ExternalInput").ap()

    S.x = din("x", [T, D])
    w_in_all = din("w_in", [L, D, INW])
    S.w_in = [w_in_all[l] for l in range(L)]
    w_uq = din("w_uq", [L, 1536, 1536]); S.w_uq = [w_uq[l] for l in range(L)]
    w_ukv = din("w_ukv", [L, 512, 2048]); S.w_ukv = [w_ukv[l] for l in range(L)]
    S.w_o = []
    for nm in ("w_o_a", "w_o_b", "w_o_c"):
        t = din(nm, [L, 1024, D])
        S.w_o.append([t[l] for l in range(L)])
    w_out = din("w_out", [L, D, D]); S.w_out = [w_out[l] for l in range(L)]
    S.identD = din("ident", [128, 128])
    S.pswapD = din("pswap", [128, 128])
    S.sel4D = din("sel4", [128, 128])
    S.cosD = din("cosT", [128, T])
    S.sinD = din("sinT", [128, T])
    S.bmD = din("bm", [L, 128, 96])
    S.gqD = din("gq", [L, 128, 12])
    S.gkvD = din("gkv", [L, 128, 4])
    S.sublnD = din("subln", [L, 128, 1])
    S.lamD = din("lamrep", [L, 128, 4, 64])
    lng = din("lng", [L, 128, D]); S.lngD = [lng[l] for l in range(L)]
    lnb = din("lnb", [L, 128, D]); S.lnbD = [lnb[l] for l in range(L)]
    nab = din("nabias", [L, 8, 128, 35, 128])
    S.nabias = [[nab[l][h] for h in range(8)] for l in range(L)]
    S.out = nc.dram_tensor("out", [T, D], F32, kind="ExternalOutput").ap()

    S.dram = {}

    def scr(name, shape, dt=BF16):
        if name in debug_out:
            S.dram[name] = nc.dram_tensor(name, shape, dt, kind="ExternalOutput").ap()
        else:
            S.dram[name] = nc.dram_tensor(name, shape, dt).ap()

    for nm in ("qA", "kA", "gA", "gB", "qC", "kC", "gC", "qBn", "kBn", "yA", "yB", "yC"):
        scr(nm, [1024, T])
    for nm in ("vA", "vC", "vB"):
        scr(nm, [T, 1024])
    scr("cq", [1536, T]); scr("ckv", [512, T]); scr("kr", [64, T]); scr("gm", [12288, T])
    scr("qBr", [512, T])
    scr("kAh", [1024, 768]); scr("kAh_g", [2048, 768])
    scr("vAh", [768, 1024]); scr("vAh_g", [1536, 1024])
    scr("kr_g", [128, T])
    for i in range(2):
        scr(f"kBn_g{i}", [1024, T]); scr(f"kC_g{i}", [1024, T])
        scr(f"vB_g{i}", [2048, 1024]); scr(f"vC_g{i}", [2048, 1024])
    scr("yout", [T, D], F32)
    scr("x1", [T, D], F32)

    with ExitStack() as es:
        S.esem = {e: newsem(S, es, f"e_{e}") for e in ("act", "dve", "pool", "pe")}
        S.sem_xs = [newsem(S, es, f"xs{i}") for i in range(4)]
        S.sem_nab = [newsem(S, es, f"nab{i}") for i in range(8)]
        S.sem_wb = [newsem(S, es, f"wb{i}") for i in range(2)]
        S.sem_stg = [newsem(S, es, f"stg{i}") for i in range(4)]
        S.sem_ld2 = [newsem(S, es, f"ld2{i}") for i in range(2)]
        S.sem_c = newsem(S, es, "const")
        S.sem_out = newsem(S, es, "outs")
        S.sem_w2 = newsem(S, es, "w2")
        S.sem_ex = newsem(S, es, "ex")
        S.sem_cc = newsem(S, es, "cc")
        S.sem_ld3 = newsem(S, es, "ld3")
        S.sem_ld4 = newsem(S, es, "ld4")
        S.sem_ln = [newsem(S, es, f"ln{i}") for i in range(6)]
        S.ident = sb(S, es, [128, 128], F32, "ident")
        S.identb = sb(S, es, [128, 128], BF16, "identb")
        S.pswap = sb(S, es, [128, 128], F32, "pswap")
        S.sel4 = sb(S, es, [128, 128], F32, "sel4")
        S.onesb = sb(S, es, [128, 128], BF16, "onesb")
        S.onesf = sb(S, es, [128, 128], F32, "onesf")
        S.cosT = sb(S, es, [128, T], F32, "cosT")
        S.sinT = sb(S, es, [128, T], F32, "sinT")
        S.bm = [sb(S, es, [128, 96], F32, "bm") for _ in range(L)]
        S.gq = [sb(S, es, [128, 12], F32, "gq") for _ in range(L)]
        S.gkv = [sb(S, es, [128, 4], F32, "gkv") for _ in range(L)]
        S.subc = [sb(S, es, [128, 1], F32, "subc") for _ in range(L)]
        S.neglam = [sb(S, es, [128, 1], F32, "neglam") for _ in range(L)]
        S.rowsum = sb(S, es, [128, T // 128, 8], F32, "rowsum")
        lamt = sb(S, es, [128, 4, 64], F32, "lamt")
        lamw = sb(S, es, [128, 8], F32, "lamw")
        P = Prog(nc, S)
        P.dma("sync", S.ident[:], S.identD[:, :], (), S.sem_c)
        P.dma("sync", S.pswap[:], S.pswapD[:, :], (), S.sem_c)
        P.dma("sync", S.sel4[:], S.sel4D[:, :], (), S.sem_c)
        P.dma("sync", S.cosT[:], S.cosD[:, :], (), S.sem_c)
        P.dma("sync", S.sinT[:], S.sinD[:, :], (), S.sem_c)
        cev = None
        for l in range(L):
            P.dma("sync", S.bm[l][:], S.bmD[l], (), S.sem_c)
            P.dma("sync", S.gq[l][:], S.gqD[l], (), S.sem_c)
            P.dma("sync", S.gkv[l][:], S.gkvD[l], (), S.sem_c)
            cev = P.dma("sync", S.subc[l][:], S.sublnD[l], (), S.sem_c)
        e0 = P.op("dve", lambda e: e.tensor_copy(S.identb[:], S.ident[:]), waits=[cev])
        P.op("dve", lambda e: e.memset(S.onesb[:], 1.0))
        e1 = P.op("dve", lambda e: e.memset(S.onesf[:], 1.0))
        S.const_ready = e1
        ev = e1
        for l in range(L):
            lam_init = 0.8 - 0.6 * math.exp(-0.3 * l)
            lev = P.dma("sync", lamt[:], S.lamD[l], [ev], S.sem_c)
            a = P.op("dve", lambda e: e.tensor_tensor(lamt[:, 0, :], lamt[:, 0, :], lamt[:, 1, :], ALU.mult), waits=[lev])
            a = P.op("dve", lambda e: e.tensor_tensor(lamt[:, 2, :], lamt[:, 2, :], lamt[:, 3, :], ALU.mult), waits=[a])
            a = P.op("dve", lambda e: e.reduce_sum(lamw[:, 0:1], lamt[:, 0, :], axis=mybir.AxisListType.X), waits=[a])
            a = P.op("dve", lambda e: e.reduce_sum(lamw[:, 1:2], lamt[:, 2, :], axis=mybir.AxisListType.X), waits=[a])
            b = P.op("act", lambda e: e.activation(out=lamw[:, 2:4], in_=lamw[:, 0:2], func=AF.Exp), waits=[a])
            a = P.op("dve", lambda e: e.tensor_tensor(lamw[:, 4:5], lamw[:, 3:4], lamw[:, 2:3], ALU.subtract), waits=[b])
            a = P.op("dve", lambda e, l=l, li=lam_init: e.tensor_scalar(S.neglam[l][:], lamw[:, 4:5], -li, None, ALU.add), waits=[a])
            a = P.op("dve", lambda e, l=l, li=lam_init: e.tensor_scalar(S.subc[l][:], S.subc[l][:], 1.0 - li, None, ALU.mult), waits=[a])
            ev = a
        S.lam_ready = ev
        P.emit()

        for l in range(nlayers):
            xsrc = S.x if l == 0 else S.dram["x1"]
            xdst = S.dram["x1"] if l < L - 1 else S.out
            for tb in range(2):
                phase_inproj(S, l, tb, xsrc)
            if stop_after == "inproj":
                break
            phase_mla_proj(S, l)
            if stop_after == "mla_proj":
                break
            phase_na(S, l)
            if stop_after == "na":
                break
            phase_dense(S, l, "mla")
            if stop_after == "mla":
                break
            phase_dense(S, l, "diff")
            if stop_after == "diff":
                break
            for tb in range(2):
                phase_outproj(S, l, tb, xsrc)
            if stop_after == "outproj":
                break
            phase_ln(S, l, xsrc, xdst)

        if stop_after is not None or nlayers < L:
            P = Prog(nc, S)
            with ExitStack() as es2:
                o = sb(S, es2, [128, 512], F32, "dummy")
                ev = P.op("dve", lambda e: e.memset(o[:], 0.0))
                ev = P.dma("sync", S.out[0:128, 0:512], o[:], [ev], S.sem_out)
                P.emit()
    return nc


def rope_tables(hf):
    half = 32
    inv = (10000.0 ** (-np.arange(half, dtype=np.float32) * 2.0 / 64)).astype(np.float32)
    pos = (np.arange(T) + hf * T).astype(np.float32)
    ang = pos[:, None] * inv[None, :]
    c = np.cos(ang).T.astype(np.float32)
    s = np.sin(ang).T.astype(np.float32)
    cosT = np.concatenate([c, c, c, c], axis=0)
    sinT = np.concatenate([-s, s, -s, s], axis=0)
    return np.ascontiguousarray(cosT), np.ascontiguousarray(sinT)


def na_bias_table(rpb, hf):
    out = np.full((5, 7, 8, 128, 128), NEG, np.float32)
    reps = [0, 1, 5, 14, 15]
    kl = np.arange(128)
    krl, wk = kl // 64, kl % 64
    ql = np.arange(128)
    qrl, wq = ql // 64, ql % 64
    for t, j in enumerate(reps):
        gj = hf * 16 + j
        for s in range(7):
            gk = gj + s - 3
            krow = 2 * gk + krl
            qrow = 2 * gj + qrl
            start = np.clip(qrow - 4, 0, 56)
            cstart = np.clip(wq - 8, 0, 48)
            valid = ((krow[:, None] >= 0) & (krow[:, None] < 64)
                     & (krow[:, None] >= start[None, :]) & (krow[:, None] < start[None, :] + 8)
                     & (wk[:, None] >= cstart[None, :]) & (wk[:, None] < cstart[None, :] + 16))
            dr = np.clip(krow[:, None] - qrow[None, :] + 7, 0, 14)
            dc = np.clip(wk[:, None] - wq[None, :] + 15, 0, 30)
            vals = rpb[:, dr, dc]
            out[t, s] = np.where(valid[None], vals, np.float32(NEG))
    return np.ascontiguousarray(out.reshape(35, 8, 128, 128).transpose(1, 2, 0, 3))


def make_in_maps(inputs):
    f = lambda k: np.ascontiguousarray(np.asarray(inputs[k], dtype=np.float32))
    x = f("x")
    shared = {
        "w_in": f("w_in"), "w_uq": f("w_uq"), "w_ukv": f("w_ukv"),
        "w_o_a": f("w_o_a"), "w_o_b": f("w_o_b"), "w_o_c": f("w_o_c"), "w_out": f("w_out"),
        "ident": np.eye(128, dtype=np.float32),
        "pswap": np.ascontiguousarray(np.eye(128, dtype=np.float32)[np.arange(128) ^ 32]),
        "sel4": np.ascontiguousarray(np.broadcast_to((np.arange(128) % 32 == 0).astype(np.float32)[:, None], (128, 128))),
        "bm": np.ascontiguousarray(f("b_merge").reshape(L, 96, 128).transpose(0, 2, 1)),
        "gq": np.ascontiguousarray(f("q_norm").reshape(L, 12, 128).transpose(0, 2, 1)),
        "gkv": np.ascontiguousarray(f("kv_norm").reshape(L, 4, 128).transpose(0, 2, 1)),
        "subln": np.ascontiguousarray(f("diff_subln").reshape(L, 128, 1)),
        "lamrep": np.ascontiguousarray(np.broadcast_to(
            np.stack([f("lam_q1"), f("lam_k1"), f("lam_q2"), f("lam_k2")], axis=1)[:, None], (L, 128, 4, 64))),
        "lng": np.ascontiguousarray(np.broadcast_to(f("ln_g")[:, None, :], (L, 128, D))),
        "lnb": np.ascontiguousarray(np.broadcast_to(f("ln_b")[:, None, :], (L, 128, D))),
    }
    rpb = f("na_rpb")
    nab = [np.stack([na_bias_table(rpb[l], hf) for l in range(L)]) for hf in range(2)]
    tabs = [rope_tables(hf) for hf in range(2)]
    maps = []
    for c in range(8):
        b, hf = c // 2, c % 2
        m = dict(shared)
        m["x"] = np.ascontiguousarray(x[b, hf * T:(hf + 1) * T, :])
        m["cosT"], m["sinT"] = tabs[hf]
        m["nabias"] = nab[hf]
        maps.append(m)
    return maps


def kernel(**inputs):
    nc = build()
    res = run_bass_kernel_spmd(nc, make_in_maps(inputs), core_ids=list(range(8)))
    full = np.zeros((4, SEQ, D), np.float32)
    for c in range(8):
        full[c // 2, (c % 2) * T:(c % 2 + 1) * T, :] = res.results[c]["out"]
    return full
```
